# Optimizing a Trainium2 kernel written in Bass

```python
import math
import jax, jax.numpy as jnp
from jax import lax
import numpy as np

D_MODEL = 1024
BATCH = 8
SEQ = 2048
DEPTH = 1
DEC_BATCH = 128
DEC_SEQ = 8
PAST_LEN = 16384
PAGE_SIZE = 128

RET_HEADS = 4
RET_DK = D_MODEL // RET_HEADS
RET_DV = D_MODEL // RET_HEADS
RET_WIDTH = RET_HEADS * RET_DV
RET_CHUNK = 128
ROPE_BASE = 10000.0
CONV_DIM = D_MODEL
CONV_WIDTH = 3
MEM_LEN = 256
MEM_HEADS = 4
MEM_HEAD_DIM = D_MODEL // MEM_HEADS
MEM_WIDTH = MEM_HEADS * MEM_HEAD_DIM
N_BRANCH = 3
EPS = 1e-6
IN_SIZES = (RET_HEADS * RET_DK, RET_HEADS * RET_DK, RET_WIDTH, RET_WIDTH,
            CONV_DIM, CONV_DIM, CONV_DIM, CONV_DIM,
            MEM_WIDTH, MEM_WIDTH, N_BRANCH * D_MODEL)
IN_TOTAL = sum(IN_SIZES)

kernel_name = 'hybrid_retention_shortconv_memxattn_step'


def rms_norm(x, g):
    xf = x.astype(jnp.float32)
    y = xf * lax.rsqrt(jnp.mean(xf * xf, axis=-1, keepdims=True) + EPS)
    return (y * g.astype(jnp.float32)).astype(x.dtype)


def rotary(x, pos):
    half = x.shape[-1] // 2
    inv = ROPE_BASE ** (-jnp.arange(half, dtype=jnp.float32) / half)
    ang = pos.astype(jnp.float32)[:, None] * inv[None, :]
    cos = jnp.cos(ang)[None, :, None, :]
    sin = jnp.sin(ang)[None, :, None, :]
    xf = x.astype(jnp.float32)
    x1, x2 = xf[..., :half], xf[..., half:]
    return jnp.concatenate([x1 * cos - x2 * sin, x1 * sin + x2 * cos], axis=-1)


def retention(q, k, v, state0):
    B, L, H, _ = q.shape
    C = RET_CHUNK if L % RET_CHUNK == 0 else L
    n = L // C
    lg = jnp.log1p(-jnp.exp2(-5.0 - jnp.arange(H, dtype=jnp.float32)))
    idx = jnp.arange(C, dtype=jnp.float32)
    diff = idx[:, None] - idx[None, :]
    inner = jnp.where(diff[None] >= 0, jnp.exp(lg[:, None, None] * jnp.maximum(diff, 0.0)[None]), 0.0)
    q_dec = jnp.exp(lg[None, :] * (idx[:, None] + 1.0))
    k_dec = jnp.exp(lg[None, :] * (C - 1.0 - idx[:, None]))
    chunk_dec = jnp.exp(lg * C)

    def to_chunks(t):
        return t.reshape(B, n, C, H, t.shape[-1]).transpose(1, 0, 2, 3, 4)

    def step(S, blk):
        qc, kc, vc = blk
        s = jnp.einsum('bihd,bjhd->bhij', qc, kc) * inner[None]
        o = (jnp.einsum('bhij,bjhe->bihe', s, vc)
             + jnp.einsum('bihd,bhde->bihe', qc, S) * q_dec[None, :, :, None])
        S = (S * chunk_dec[None, :, None, None]
             + jnp.einsum('bjhd,bjhe->bhde', kc * k_dec[None, :, :, None], vc))
        return S, o

    S, o = lax.scan(step, state0, (to_chunks(q), to_chunks(k), to_chunks(v)))
    o = o.transpose(1, 0, 2, 3, 4).reshape(B, L, H, v.shape[-1])
    return o, S


def memory_kv(mem, mem_norm_g, w_mem_kv):
    B = mem.shape[0]
    kv = rms_norm(mem, mem_norm_g) @ w_mem_kv
    k, v = jnp.split(kv, 2, axis=-1)
    return (k.reshape(B, MEM_LEN, MEM_HEADS, MEM_HEAD_DIM),
            v.reshape(B, MEM_LEN, MEM_HEADS, MEM_HEAD_DIM))


def mixer_layer(x, pos, ret_state, conv_state, mem_k, mem_v,
                norm_g, w_in, b_gate, ret_norm_g, conv_w, conv_b,
                w_ret_o, w_conv_o, w_mem_o, w_out):
    B, L, _ = x.shape
    h = rms_norm(x, norm_g)
    z = h @ w_in
    (rq, rk, rv, rg, cu, cb, cc, cg, mq, mg, gl) = jnp.split(
        z, [int(o) for o in np.cumsum(IN_SIZES)[:-1]], axis=-1)

    q = rotary(rq.reshape(B, L, RET_HEADS, RET_DK), pos)
    k = rotary(rk.reshape(B, L, RET_HEADS, RET_DK), pos) * (RET_DK ** -0.5)
    v = rv.reshape(B, L, RET_HEADS, RET_DV).astype(jnp.float32)
    o, new_ret = retention(q, k, v, ret_state.astype(jnp.float32))
    o = o * lax.rsqrt(jnp.mean(o * o, axis=-1, keepdims=True) + EPS) * ret_norm_g.astype(jnp.float32)
    o = o.reshape(B, L, RET_WIDTH).astype(x.dtype) * jax.nn.silu(rg)
    y_ret = o @ w_ret_o

    pre = cc * cu
    xp = jnp.concatenate([conv_state.astype(pre.dtype), pre], axis=1)
    conv = conv_b + sum(conv_w[t] * xp[:, t:t + L] for t in range(CONV_WIDTH))
    new_conv = xp[:, L:]
    y_conv = (cb * conv * jax.nn.silu(cg)) @ w_conv_o

    qm = mq.reshape(B, L, MEM_HEADS, MEM_HEAD_DIM).astype(jnp.float32)
    s = jnp.einsum('blhd,bmhd->bhlm', qm, mem_k.astype(jnp.float32)) * (MEM_HEAD_DIM ** -0.5)
    p = jax.nn.softmax(s, axis=-1)
    om = jnp.einsum('bhlm,bmhd->blhd', p, mem_v.astype(jnp.float32))
    om = om.reshape(B, L, MEM_WIDTH).astype(x.dtype) * jax.nn.silu(mg)
    y_mem = om @ w_mem_o

    g_ret, g_conv, g_mem = jnp.split(jax.nn.sigmoid(gl + b_gate), N_BRANCH, axis=-1)
    merged = g_ret * y_ret + g_conv * y_conv + g_mem * y_mem
    y = x + merged @ w_out
    return y, new_ret.astype(x.dtype), new_conv


def setup_inputs(seed: int = 0) -> dict:
    key = jax.random.key(seed)
    ks = jax.random.split(key, 24)
    f32 = jnp.float32
    nrm = lambda k, s, sc: jax.random.normal(k, s, f32) * sc
    return {
        'x_prompt': nrm(ks[0], (BATCH, SEQ, D_MODEL), 1.0),
        'x_sample': nrm(ks[1], (DEC_BATCH, DEC_SEQ, D_MODEL), 1.0),
        'state_ret': nrm(ks[2], (DEPTH, DEC_BATCH, RET_HEADS, RET_DK, RET_DV), 0.5),
        'state_conv': nrm(ks[3], (DEPTH, DEC_BATCH, CONV_WIDTH - 1, CONV_DIM), 1.0),
        'cache_mem_k': nrm(ks[4], (DEPTH, DEC_BATCH, MEM_LEN, MEM_HEADS, MEM_HEAD_DIM), 1.0),
        'cache_mem_v': nrm(ks[5], (DEPTH, DEC_BATCH, MEM_LEN, MEM_HEADS, MEM_HEAD_DIM), 1.0),
        'mem_prompt': nrm(ks[6], (BATCH, MEM_LEN, D_MODEL), 1.0),
        'norm_g': 1.0 + nrm(ks[7], (DEPTH, D_MODEL), 0.02),
        'w_in': nrm(ks[8], (DEPTH, D_MODEL, IN_TOTAL), D_MODEL ** -0.5),
        'b_gate': nrm(ks[9], (DEPTH, N_BRANCH * D_MODEL), 0.02),
        'ret_norm_g': 1.0 + nrm(ks[10], (DEPTH, RET_HEADS, RET_DV), 0.02),
        'conv_w': nrm(ks[11], (DEPTH, CONV_WIDTH, CONV_DIM), CONV_WIDTH ** -0.5),
        'conv_b': nrm(ks[12], (DEPTH, CONV_DIM), 0.02),
        'w_ret_o': nrm(ks[13], (DEPTH, RET_WIDTH, D_MODEL), RET_WIDTH ** -0.5),
        'w_conv_o': nrm(ks[14], (DEPTH, CONV_DIM, D_MODEL), CONV_DIM ** -0.5),
        'w_mem_o': nrm(ks[15], (DEPTH, MEM_WIDTH, D_MODEL), MEM_WIDTH ** -0.5),
        'w_out': nrm(ks[16], (DEPTH, D_MODEL, D_MODEL), D_MODEL ** -0.5),
        'mem_norm_g': 1.0 + nrm(ks[17], (DEPTH, D_MODEL), 0.02),
        'w_mem_kv': nrm(ks[18], (DEPTH, D_MODEL, 2 * MEM_WIDTH), D_MODEL ** -0.5),
        'final_norm_g': 1.0 + nrm(ks[19], (D_MODEL,), 0.02),
    }


def reference(x_prompt, x_sample, state_ret, state_conv, cache_mem_k, cache_mem_v, mem_prompt,
              norm_g, w_in, b_gate, ret_norm_g, conv_w, conv_b, w_ret_o, w_conv_o, w_mem_o,
              w_out, mem_norm_g, w_mem_kv, final_norm_g):
    Bp, Lp, _ = x_prompt.shape
    Ls = x_sample.shape[1]
    pos_p = jnp.arange(Lp, dtype=jnp.int32)
    pos_s = PAST_LEN + jnp.arange(Ls, dtype=jnp.int32)
    hp, hs = x_prompt, x_sample
    ret_p, ret_s, conv_p, conv_s, mk_list, mv_list = [], [], [], [], [], []
    for l in range(DEPTH):
        mk_p, mv_p = memory_kv(mem_prompt, mem_norm_g[l], w_mem_kv[l])
        zr = jnp.zeros((Bp, RET_HEADS, RET_DK, RET_DV), x_prompt.dtype)
        zc = jnp.zeros((Bp, CONV_WIDTH - 1, CONV_DIM), x_prompt.dtype)
        hp, rp, cp = mixer_layer(hp, pos_p, zr, zc, mk_p, mv_p,
                                 norm_g[l], w_in[l], b_gate[l], ret_norm_g[l], conv_w[l], conv_b[l],
                                 w_ret_o[l], w_conv_o[l], w_mem_o[l], w_out[l])
        hs, rs, cs = mixer_layer(hs, pos_s, state_ret[l], state_conv[l], cache_mem_k[l], cache_mem_v[l],
                                 norm_g[l], w_in[l], b_gate[l], ret_norm_g[l], conv_w[l], conv_b[l],
                                 w_ret_o[l], w_conv_o[l], w_mem_o[l], w_out[l])
        ret_p.append(rp); ret_s.append(rs); conv_p.append(cp); conv_s.append(cs)
        mk_list.append(mk_p); mv_list.append(mv_p)
    y_prompt = rms_norm(hp, final_norm_g)
    y_sample = rms_norm(hs, final_norm_g)
    new_ret_prompt = jnp.stack(ret_p)
    new_ret_sample = jnp.stack(ret_s)
    new_conv_prompt = jnp.stack(conv_p)
    new_conv_sample = jnp.stack(conv_s)
    new_mem_k_prompt = jnp.stack(mk_list)
    new_mem_v_prompt = jnp.stack(mv_list)
    return (y_prompt, y_sample, new_ret_prompt, new_ret_sample, new_conv_prompt, new_conv_sample,
            new_mem_k_prompt, new_mem_v_prompt)
```

```python
import contextlib
import numpy as np
import concourse.bass as bass
import concourse.mybir as mybir
from concourse.bass_utils import run_bass_kernel_spmd

F32 = mybir.dt.float32
BF16 = mybir.dt.bfloat16
AF = mybir.ActivationFunctionType
ALU = mybir.AluOpType

NCORES = 8
D = 1024
LP = 2048
NB = 16
LS = 8
NS = NB * LS
NT = LP + NS
NTILE = NT // 128
PAST = 16384
EPS = 1e-6
GROUPS = [(0, 512), (512, 512), (1024, 512), (1536, 512), (2048, 128)]

COMPUTE = ('pe', 'act', 'dve', 'pool')
ENGS = ('pe', 'act', 'dve', 'pool', 'sp')


class Buf:
    __slots__ = ('name', 'last_w', 'reads', 'excl')

    def __init__(self, name='', excl=False):
        self.name = name
        self.last_w = None
        self.reads = []
        self.excl = excl


class Op:
    __slots__ = ('eng', 'idx', 'fn', 'waits', 'signal', 'sem', 'val', 'is_dma', 'signo', 'tag')

    def __init__(self, eng, idx, fn, is_dma):
        self.eng = eng
        self.idx = idx
        self.fn = fn
        self.waits = []
        self.signal = False
        self.sem = None
        self.val = 0
        self.is_dma = is_dma
        self.signo = 0


def _flat(x):
    out = []
    for i in x:
        if isinstance(i, (list, tuple)):
            out.extend(_flat(i))
        else:
            out.append(i)
    return out


class Plan:
    def __init__(self):
        self.streams = {e: [] for e in ENGS}
        self.waited = {e: {} for e in ENGS}
        self.waited_dma = {e: {} for e in ENGS}
        self.dma_counts = {}
        self.out_sems = set()
        self.disabled = False
        self.tag = 'init'

    def _dep(self, ev, d, kind):
        if d is ev:
            return
        if d.is_dma:
            w = self.waited_dma[ev.eng]
            if w.get(d.sem, 0) >= d.val:
                return
            need = self.dma_counts[d.sem]
            w[d.sem] = need
            ev.waits.append((d.sem, need))
            return
        if d.eng == ev.eng and not ev.is_dma and d.eng == 'pe':
            return
        w = self.waited[ev.eng]
        if w.get(d.eng, -1) >= d.idx:
            return
        w[d.eng] = d.idx
        d.signal = True
        ev.waits.append(d)

    def op(self, eng, fn, reads=(), writes=(), dma_sem=None, is_out=False):
        reads = _flat(reads)
        writes = _flat(writes)
        st = self.streams[eng]
        ev = Op(eng, len(st), fn, dma_sem is not None)
        ev.tag = self.tag
        if self.disabled:
            return ev
        best = {}

        def cand(d, kind):
            key = ('d', d.sem) if d.is_dma else ('e', d.eng, kind == 'war')
            cur = best.get(key)
            if cur is None or (d.val > cur[0].val if d.is_dma else d.idx > cur[0].idx):
                best[key] = (d, kind)

        for b in reads:
            if b.last_w is not None:
                cand(b.last_w, 'raw')
            if b.excl:
                for r in b.reads:
                    cand(r, 'war')
        for b in writes:
            if b.last_w is not None:
                cand(b.last_w, 'waw')
            for r in b.reads:
                cand(r, 'war')
        for (d, kind) in best.values():
            self._dep(ev, d, kind)
        for b in reads:
            if not ev.is_dma:
                b.reads = [r for r in b.reads if r.is_dma or r.eng != ev.eng]
            b.reads.append(ev)
        for b in writes:
            b.last_w = ev
            b.reads = []
        if dma_sem is not None:
            v = self.dma_counts.get(dma_sem, 0) + 16
            self.dma_counts[dma_sem] = v
            ev.sem = dma_sem
            ev.val = v
            if is_out:
                self.out_sems.add(dma_sem)
        st.append(ev)
        return ev

    def emit(self, nc):
        engobj_names = {'pe': 'tensor', 'act': 'scalar', 'dve': 'vector', 'pool': 'gpsimd', 'sp': 'sync'}
        for e in COMPUTE:
            n = 0
            for o in self.streams[e]:
                if o.signal and not o.is_dma:
                    n += 1
                    o.signo = n
        with contextlib.ExitStack() as es:
            esem = {e: es.enter_context(nc.semaphore('c_' + e)) for e in COMPUTE}
            dsem = {k: es.enter_context(nc.semaphore('d_%s' % (k,))) for k in self.dma_counts}
            block = es.enter_context(nc.Block())

            def run(ename):
                def body(eng):
                    for o in self.streams[ename]:
                        for d in o.waits:
                            if isinstance(d, tuple):
                                eng.wait_ge(dsem[d[0]], d[1])
                            else:
                                eng.wait_ge(esem[d.eng], d.signo)
                        ins = o.fn(eng)
                        if o.is_dma:
                            ins.then_inc(dsem[o.sem], 16)
                        elif o.signal:
                            ins.then_inc(esem[o.eng], 1)
                    if ename == 'sp':
                        for k in sorted(self.out_sems):
                            eng.wait_ge(dsem[k], self.dma_counts[k])
                return body

            for e in ENGS:
                getattr(block, engobj_names[e])(run(e))


def _consts():
    half = 128
    inv = (np.float32(10000.0) ** (-(np.arange(half, dtype=np.float32)) / np.float32(half))).astype(np.float32)
    pos = np.concatenate([np.arange(LP, dtype=np.float32),
                          np.tile(PAST + np.arange(LS, dtype=np.float32), NB)]).astype(np.float32)
    ang = (pos[None, :] * inv[:, None]).astype(np.float32)
    cos = np.cos(ang.astype(np.float64)).astype(np.float32)
    sin = np.sin(ang.astype(np.float64)).astype(np.float32)
    lg = np.log1p(-np.exp2(-5.0 - np.arange(4, dtype=np.float32))).astype(np.float32)

    def dec(C):
        idx = np.arange(C, dtype=np.float32)
        diff = idx[:, None] - idx[None, :]
        inner = np.where(diff[None] >= 0, np.exp(lg[:, None, None] * np.maximum(diff, 0.0)[None]), 0.0)
        inner = inner.astype(np.float32)
        qd = np.exp(lg[None, :] * (idx[:, None] + 1.0)).astype(np.float32)
        kd = np.exp(lg[None, :] * (C - 1.0 - idx[:, None])).astype(np.float32)
        cd = np.exp(lg * C).astype(np.float32)
        return inner, qd, kd, cd

    lg64 = lg.astype(np.float64)
    r = np.arange(512, dtype=np.float64)
    j = np.arange(128, dtype=np.float64)
    diff = r[None, :] - j[:, None]
    maskX = np.where(diff[None] >= 0, np.exp(lg64[:, None, None] * np.maximum(diff, 0.0)[None]), 0.0) / 16.0
    qdecX = np.exp(lg64[:, None] * (r[None, :] + 1.0))
    kdecX = np.zeros((128, 16))
    for h in range(4):
        for c in range(4):
            kdecX[:, h * 4 + c] = np.exp(lg64[h] * (511.0 - (c * 128 + j))) / 16.0
    cdX = np.exp(lg64 * 512.0)
    t = np.arange(128)
    bb = t // 8; tt = (t % 8).astype(np.float64)
    same = (bb[:, None] == bb[None, :])
    d8 = tt[None, :] - tt[:, None]
    maskSf = np.where(same[None] & (d8[None] >= 0), np.exp(lg64[:, None, None] * np.maximum(d8, 0.0)[None]), 0.0) / 16.0
    rmd = np.zeros((128, 64))
    for h in range(4):
        for b in range(NB):
            sel = bb == b
            rmd[sel, h * NB + b] = np.exp(lg64[h] * (7.0 - tt[sel])) / 16.0
    qdS = np.exp(lg64[:, None] * (np.arange(LS, dtype=np.float64)[None, :] + 1.0))
    cdS = np.exp(lg64 * LS)
    f32 = lambda a: np.ascontiguousarray(a, dtype=np.float32)
    return dict(cos=cos, sin=sin, maskX=f32(maskX), qdecX=f32(qdecX), kdecX=f32(kdecX), maskSf=f32(maskSf), rmd=f32(rmd),
                qdecS=f32(qdS), ident=np.eye(128, dtype=np.float32)), [float(x) for x in cdX], [float(x) for x in cdS]


def _wblocks():
    blocks = []
    for c in range(8):
        blocks.append([("w_in", 4096 + q * 1024 + c * 128, 128, q * 128) for q in range(4)])
    for j in range(4):
        blocks.append([("w_conv_o", j * 256, 256, 0), ("w_in", 10240 + 1024 + j * 256, 256, 256)])
    for j in range(4):
        blocks.append([("w_mem_kv", j * 512, 512, 0)])
    for h in range(4):
        blocks.append([("w_in", 8192 + h * 256, 256, 0), ("w_in", 9216 + h * 256, 256, 256)])
    for j in range(4):
        blocks.append([("w_mem_o", j * 256, 256, 0), ("w_in", 10240 + 2048 + j * 256, 256, 256)])
    for h in range(4):
        blocks.append([("w_in", h * 256, 256, 0), ("w_in", 1024 + h * 256, 256, 256)])
        blocks.append([("w_in", 2048 + h * 256, 256, 0), ("w_in", 3072 + h * 256, 256, 256)])
    for j in range(4):
        blocks.append([("w_ret_o", j * 256, 256, 0), ("w_in", 10240 + j * 256, 256, 256)])
    for j in range(2):
        blocks.append([("w_out", j * 512, 512, 0)])
    return blocks


def _pack_weights(ws):
    blocks = _wblocks()
    wb = np.empty((len(blocks), 128, 8, 512), dtype=np.float32)
    for i, blk in enumerate(blocks):
        for (name, c0, n, off) in blk:
            wb[i, :, :, off:off + n] = ws[name][:, c0:c0 + n].reshape(8, 128, n).transpose(1, 0, 2)
    return wb.reshape(len(blocks), 128, 4096)


def build_nc(stop=99, dbg=0):
    consts, CDX, CDS = _consts()
    nc = bass.Bass("TRN2", target_bir_lowering=False)

    def din(name, shape):
        return nc.dram_tensor(name, list(shape), F32, kind="ExternalInput").ap()

    def dout(name, shape):
        return nc.dram_tensor(name, list(shape), F32, kind="ExternalOutput").ap()

    xp = din("xp", [LP, D]); xs = din("xs", [NS, D])
    sret = din("sret", [NB, 4, 256, 256]); sconv = din("sconv", [NB * 2, D])
    cmk = din("cmk", [NB, 256, D]); cmv = din("cmv", [NB, 256, D]); memp = din("memp", [256, D])
    norm_g = din("norm_g", [D]); b_gate = din("b_gate", [3072])
    wb = din("wb", [len(_wblocks()), 128, 4096])
    ret_norm_g = din("ret_norm_g", [1024]); conv_w = din("conv_w", [3, D]); conv_b = din("conv_b", [D])
    mem_norm_g = din("mem_norm_g", [D])
    final_norm_g = din("final_norm_g", [D])
    c_cos = din("c_cos", [128, NT]); c_sin = din("c_sin", [128, NT])
    c_maskX = din("c_maskX", [4, 128, 512]); c_qdecX = din("c_qdecX", [4, 512]); c_kdecX = din("c_kdecX", [128, 16])
    c_maskSf = din("c_maskSf", [4, 128, 128]); c_rmd = din("c_rmd", [128, 64]); c_qdecS = din("c_qdecS", [4, 8])
    c_ident = din("c_ident", [128, 128])

    yp = dout("yp", [LP, D]); ys = dout("ys", [NS, D])
    nrp = dout("nrp", [4, 256, 256]); nrs = dout("nrs", [NB, 4, 256, 256])
    ncp = dout("ncp", [2, D]); ncs = dout("ncs", [NB * 2, D])
    nmk = dout("nmk", [256, D]); nmv = dout("nmv", [256, D])

    P = Plan()
    with contextlib.ExitStack() as es:
        def sb(name, shape, dt=F32):
            return es.enter_context(nc.sbuf_tensor(name, list(shape), dt))

        def psb(name, shape, dt=F32):
            return es.enter_context(nc.psum_tensor(name, list(shape), dt))

        hT = sb("hT", [128, 8, NT], BF16)
        ob = sb("ob", [128, 8, NT], BF16)
        mgd = sb("mgd", [128, 8, NT], BF16)
        B_hTa = Buf('hTa'); B_hTb = Buf('hTb'); B_hT = [B_hTa, B_hTb]
        B_ob = [[Buf('ob') for _ in GROUPS] for _ in range(8)]
        B_mg = [[Buf('mg') for _ in GROUPS] for _ in range(8)]
        B_mgt = [Buf('mgt') for _ in range(NTILE)]
        NSLOT = 4
        wsl = [sb("w%d" % i, [128, 8, 512], BF16) for i in range(NSLOT)]
        B_w = [Buf('w%d' % i) for i in range(NSLOT)]
        gaux = sb("gaux", [128, D])
        maskT = sb("maskT", [128, 4, 128]); qdec = sb("qdec", [128, 4, 128])
        qdecS = sb("qdecS", [128, 4, 8])
        idf = sb("idf", [128, 128]); idb = sb("idb", [128, 128], BF16); ones = sb("ones", [128, 128], BF16)
        mhalf = sb("mhalf", [128, 8])
        epsb = sb("epsb", [128, 2])
        epsb4 = sb("epsb4", [128, 2])
        svec = sb("svec", [128, 64]); hbg = sb("hbg", [128, 24])
        bg = svec[:, 0:24]; cbias = svec[:, 24:32]; gretT = svec[:, 32:40]
        cw = svec[:, 40:64].rearrange("p (t c) -> p t c", t=3)
        B_c = Buf('consts')
        B_gaux = Buf('gaux')
        psf = [psb("psf%d" % i, [128, 512], F32) for i in range(6)]
        psh = [psb("psh%d" % i, [128, 1024], BF16) for i in range(2)]
        B_psf = [Buf('psf%d' % i, True) for i in range(6)]
        B_psh = [Buf('psh%d' % i, True) for i in range(2)]
        free_f = list(range(6))
        free_h = list(range(2))

        def falloc():
            return free_f.pop(0)

        def ffree(i):
            free_f.append(i)

        def halloc():
            return free_h.pop(0)

        def hfree(i):
            free_h.append(i)

        class Arena:
            def __init__(self, t, n, rs):
                self.t = t
                self.rs = rs
                self.bufs = [Buf('ar') for _ in range((n + rs - 1) // rs)]

            def view(self, off, size, parts=128):
                return self.t[0:parts, off:off + size], self.bufs[off // self.rs:(off + size - 1) // self.rs + 1]

        AF_N = 4096
        AH_N = 8192
        arF = Arena(sb("arF", [128, AF_N]), AF_N, 256)
        arH = Arena(sb("arH", [128, AH_N], BF16), AH_N, 512)
        xbuf = [sb("xbuf%d" % i, [128, D]) for i in range(2)]
        B_x = [[Buf('x0a'), Buf('x0b')], [Buf('x1a'), Buf('x1b')]]
        hb = [sb("hb%d" % i, [128, D], BF16) for i in range(2)]
        B_hb = [Buf('hb0'), Buf('hb1')]
        scr = None
        ssb = sb("ssb", [128, 64])
        stat_i = [0]

        def stat():
            i = stat_i[0] % 8
            stat_i[0] += 1
            return i

        B_st = [Buf('st%d' % i) for i in range(8)]
        stg = [sb("stg%d" % i, [128, 512]) for i in range(2)]
        B_stg = [Buf('stg0'), Buf('stg1')]
        scr = stg[0][:, 0:128].bitcast(BF16)
        B_scr = B_stg[0]
        NTF = 4
        tf = [sb("tf%d" % i, [128, 512])[:] for i in range(NTF)]
        B_tf = [Buf('tf%d' % i) for i in range(NTF)]
        tf += [xbuf[1][:, 0:512], xbuf[1][:, 512:1024], xbuf[0][:, 0:512], xbuf[0][:, 512:1024],
               hb[0][:].bitcast(F32), hb[1][:].bitcast(F32)]
        B_tf += [B_x[1][0], B_x[1][1], B_x[0][0], B_x[0][1], B_hb[0], B_hb[1]]
        tf_i = [0]
        tf_n = [NTF]

        def tfa():
            i = tf_i[0] % tf_n[0]
            tf_i[0] += 1
            return i

        NTH = 4
        th_ = [sb("th%d" % i, [128, 1024], BF16) for i in range(NTH)]
        B_th = [Buf('th%d' % i) for i in range(NTH)]
        th_i = [0]

        def tha():
            i = th_i[0] % NTH
            th_i[0] += 1
            return i

        def sl(g):
            return slice(GROUPS[g][0], GROUPS[g][0] + GROUPS[g][1])

        def ld(dst, src, sem, extra=()):
            P.op('sp', lambda e: e.dma_start(out=dst, in_=src), writes=[B_c] + list(extra), dma_sem=sem)

        ld(qdecS[:].rearrange("p h i -> p (h i)"), c_qdecS.rearrange("h i -> (h i)").partition_broadcast(128), 'c0')
        ld(idf[:], c_ident, 'c0')
        svt = tf[0][0:64, 0:128]
        ld(svt[0:24, :], b_gate.rearrange("(j p) -> j p", p=128), 'c0', [B_tf[0]])
        ld(svt[24:32, :], conv_b.rearrange("(c p) -> c p", p=128), 'c0', [B_tf[0]])
        ld(svt[32:40, :], ret_norm_g.rearrange("(c p) -> c p", p=128), 'c0', [B_tf[0]])
        ld(svt[40:64, :], conv_w.rearrange("t (c p) -> (t c) p", p=128), 'c0', [B_tf[0]])
        if dbg != 1:
            P.op('pe', lambda e: e.matmul(psf[0][:, 0:64], lhsT=svt, rhs=idf[0:64, 0:64], start=True, stop=True), reads=[B_c, B_tf[0]], writes=[B_psf[0]])
            P.op('dve', lambda e: e.tensor_copy(out=svec[:], in_=psf[0][:, 0:64]), reads=[B_psf[0]], writes=[B_c])
        P.op('dve', lambda e: e.tensor_copy(out=idb[:], in_=idf[:]), reads=[B_c], writes=[B_c])
        P.op('pool', lambda e: e.memset(ones[:], 1.0), writes=[B_c])
        P.op('pool', lambda e: e.memset(mhalf[:], -0.5), writes=[B_c])
        P.op('pool', lambda e: e.memset(epsb[:], EPS), writes=[B_c])
        P.op('pool', lambda e: e.memset(epsb4[:], 4.0 * EPS), writes=[B_c])
        P.op('dve', lambda e: e.tensor_scalar(out=hbg[:], in0=bg, scalar1=0.5, scalar2=None, op0=ALU.mult),
             reads=[B_c], writes=[B_c])
        RC = [B_c]

        blocks = _wblocks()
        wstate = {'issued': 0, 'next': 0}

        def wissue(upto, after=()):
            while wstate['issued'] < min(upto, len(blocks)):
                i = wstate['issued']
                s = i % NSLOT
                P.op('pool', lambda e, i=i, s=s: e.dma_start(out=wsl[s][:].rearrange("p k n -> p (k n)"), in_=wb[i]),
                     reads=list(after), writes=[B_w[s]], dma_sem='w%d' % s)
                wstate['issued'] += 1

        def wget():
            i = wstate['next']
            wstate['next'] += 1
            assert i < wstate['done'] + NSLOT
            wissue(i + 1)
            s = i % NSLOT
            return wsl[s], B_w[s]

        def wdone():
            wstate['done'] += 1
            wissue(wstate['done'] + NSLOT)

        wstate['done'] = 0
        wissue(1)

        def mm(out, lhsT, rhs, start, stop, reads, wbuf):
            P.op('pe', lambda e: e.matmul(out, lhsT=lhsT, rhs=rhs, start=start, stop=stop), reads=reads, writes=[wbuf])

        def tr(out, in_, ident, reads, wbuf):
            P.op('pe', lambda e: e.transpose(out=out, in_=in_, identity=ident), reads=reads + RC, writes=[wbuf])

        def rstd_of(sumsq_ap, out_ap, n, bufs, pre=1.0):
            s = 1.0 / (pre * pre)
            P.op('act', lambda e: e.activation(out=out_ap, in_=sumsq_ap, func=AF.Sqrt, bias=epsb[:, 0:1], scale=s / n),
                 reads=bufs + RC, writes=bufs)
            P.op('dve', lambda e: e.reciprocal(out=out_ap, in_=out_ap), reads=bufs, writes=bufs)

        def norm_tile(src_ap, g_tile, gbuf, xi, nparts=128):
            st = stat()
            sa = ssb[:, st * 4:st * 4 + 1]
            sb_ = ssb[:, st * 4 + 1:st * 4 + 2]
            P.op('act', lambda e: e.activation(out=hb[xi][:], in_=xbuf[xi][:], func=AF.Square, accum_out=sa),
                 reads=[B_x[xi]], writes=[B_hb[xi], B_st[st]])
            rstd_of(sa, sb_, D, [B_st[st]])
            P.op('dve', lambda e: e.scalar_tensor_tensor(out=hb[xi][:], in0=xbuf[xi][:], scalar=sb_, in1=g_tile[:],
                                                         op0=ALU.mult, op1=ALU.mult),
                 reads=[B_x[xi], B_st[st], gbuf], writes=[B_hb[xi]])

        def transpose8(xi, dst3, dbufs, dbufs2=None):
            hi = halloc()
            for k in range(8):
                tr(psh[hi][:, k * 128:(k + 1) * 128], hb[xi][:, k * 128:(k + 1) * 128], idb[:], [B_hb[xi]], B_psh[hi])
            src3 = psh[hi][:].rearrange("p (k t) -> p k t", k=8)
            P.op('act', lambda e: e.copy(out=dst3[:, 0:4, :], in_=src3[:, 0:4, :]), reads=[B_psh[hi]], writes=dbufs)
            P.op('dve', lambda e: e.tensor_copy(out=dst3[:, 4:8, :], in_=src3[:, 4:8, :]), reads=[B_psh[hi]],
                 writes=dbufs if dbufs2 is None else dbufs2)
            hfree(hi)

        if stop < 2:
            P.disabled = True
        def xtile_src(t):
            return xp[t * 128:(t + 1) * 128, :] if t < 16 else xs

        P.op('sp', lambda e: e.dma_start(out=gaux[:], in_=norm_g.partition_broadcast(128)), writes=[B_gaux], dma_sem='gaux')
        P.tag = 'p2'
        for i in range(4):
            _v, _b = arF.view(i * 1024, 1024); xbuf.append(_v); B_x.append(_b)
            _v, _b = arH.view(i * 1024, 1024); hb.append(_v); B_hb.append(_b)
        NXR = 6
        SKEW = 3
        for t in range(NTILE + SKEW):
            if t < NTILE:
                xi = t % NXR
                P.op('sp', lambda e, t=t, xi=xi: e.dma_start(out=xbuf[xi][:], in_=xtile_src(t)),
                     writes=[B_x[xi]], dma_sem='x%d' % xi)
                norm_tile(None, gaux, B_gaux, xi)
            if t >= SKEW:
                tt = t - SKEW
                transpose8(tt % NXR, hT[:, :, tt * 128:(tt + 1) * 128], [B_hTa], [B_hTb])
        wissue(NSLOT, after=B_hT)

        def project_branch(bidx, first):
            P.tag = 'proj%d' % bidx
            for j in range(4):
                Wo, BWo = wget()
                Wg, BWg = Wo, BWo
                for c2 in range(2):
                    c = c2
                    fc = j * 2 + c2
                    for g in range(5):
                        n = GROUPS[g][1]
                        fy = falloc()
                        for k in range(8):
                            mm(psf[fy][:, 0:n], Wo[:, k, c * 128:(c + 1) * 128], ob[:, k, sl(g)], k == 0, k == 7,
                               [B_ob[k][g], BWo], B_psf[fy])
                        fg = falloc()
                        for k in range(8):
                            mm(psf[fg][:, 0:n], Wg[:, k, 256 + c * 128:256 + (c + 1) * 128], hT[:, k, sl(g)], k == 0, k == 7,
                               [B_hT, BWg], B_psf[fg])
                        ti = tfa()
                        bcol = bidx * 8 + fc
                        P.op('act', lambda e, fg=fg, ti=ti, n=n, bcol=bcol: e.activation(
                            out=tf[ti][:, 0:n], in_=psf[fg][:, 0:n], func=AF.Tanh, bias=hbg[:, bcol:bcol + 1], scale=0.5),
                            reads=[B_psf[fg]] + RC, writes=[B_tf[ti]])
                        ffree(fg)
                        if first:
                            P.op('dve', lambda e, fy=fy, ti=ti, n=n, fc=fc, g=g: e.scalar_tensor_tensor(
                                out=mgd[:, fc, sl(g)], in0=tf[ti][:, 0:n], scalar=1.0, in1=psf[fy][:, 0:n],
                                op0=ALU.add, op1=ALU.mult),
                                reads=[B_tf[ti], B_psf[fy]], writes=[B_mg[fc][g]])
                        else:
                            t2 = tfa()
                            P.op('dve', lambda e, fy=fy, ti=ti, t2=t2, n=n: e.scalar_tensor_tensor(
                                out=tf[t2][:, 0:n], in0=tf[ti][:, 0:n], scalar=1.0, in1=psf[fy][:, 0:n],
                                op0=ALU.add, op1=ALU.mult),
                                reads=[B_tf[ti], B_psf[fy]], writes=[B_tf[t2]])
                            P.op('pool', lambda e, t2=t2, n=n, fc=fc, g=g: e.tensor_tensor(
                                out=mgd[:, fc, sl(g)], in0=tf[t2][:, 0:n], in1=mgd[:, fc, sl(g)], op=ALU.add),
                                reads=[B_tf[t2], B_mg[fc][g]], writes=[B_mg[fc][g]])
                        ffree(fy)
                wdone()

        if stop < 3:
            P.disabled = True
        tf_n[0] = NTF + 6
        _v, BL_scT = arF.view(0, 256); scT = _v.rearrange("p (c r) -> p c r", c=8)
        sctm, BL_sctm = arF.view(256, 1024, parts=32)
        pre = []; B_pre = []
        for i in range(2):
            _v, _b = arF.view(1280 + i * 768, 514); pre.append(_v); B_pre.append(_b)
        _v, BL_preS = arF.view(2816, 160); preS = _v.rearrange("p (b r) -> p b r", r=10)
        nct = []; B_nct = []
        for _off in (3072, 256):
            _v, _b = arF.view(_off, 1024, parts=16); nct.append(_v); B_nct.append(_b)
        P.op('sp', lambda e: e.dma_start(out=sctm[:], in_=sconv), writes=BL_sctm, dma_sem='sctm')
        fi = falloc()
        for c in range(8):
            P.op('pe', lambda e, c=c, fi=fi: e.matmul(psf[fi][:, c * 32:(c + 1) * 32], lhsT=sctm[:, c * 128:(c + 1) * 128],
                                                     rhs=idf[0:32, 0:32], start=True, stop=True),
                 reads=BL_sctm + RC, writes=[B_psf[fi]])
        P.op('dve', lambda e, fi=fi: e.tensor_copy(out=scT[:].rearrange("p c r -> p (c r)"), in_=psf[fi][:, 0:256]),
             reads=[B_psf[fi]], writes=BL_scT)
        ffree(fi)

        for c in range(8):
            W, BW = wget()
            P.tag = 'conv'
            for g in range(5):
                n = GROUPS[g][1]
                fcu = falloc(); fcc = falloc(); fcb = falloc(); fcg = falloc()
                for (fx, off) in ((fcu, 0), (fcc, 256), (fcb, 128), (fcg, 384)):
                    for k in range(8):
                        mm(psf[fx][:, 0:n], W[:, k, off:off + 128], hT[:, k, sl(g)], k == 0, k == 7, [B_hT, BW], B_psf[fx])
                t_cu = tfa()
                P.op('act', lambda e, fcu=fcu, t_cu=t_cu, n=n: e.copy(out=tf[t_cu][:, 0:n], in_=psf[fcu][:, 0:n]),
                     reads=[B_psf[fcu]], writes=[B_tf[t_cu]])
                ffree(fcu)
                if g < 4:
                    pi = g % 2
                    pb, Bp = pre[pi], B_pre[pi]
                    if g == 0:
                        P.op('pool', lambda e, pb=pb: e.memset(pb[:, 0:2], 0.0), writes=Bp)
                    else:
                        po = pre[1 - pi]
                        P.op('pool', lambda e, pb=pb, po=po: e.tensor_copy(out=pb[:, 0:2], in_=po[:, 512:514]),
                             reads=B_pre[1 - pi], writes=Bp)
                    cur = pb[:, 2:514]; m1 = pb[:, 1:513]; m2 = pb[:, 0:512]

                    def v3(a):
                        return a
                else:
                    pb, Bp = preS, BL_preS
                    P.op('pool', lambda e, c=c: e.tensor_copy(out=preS[:, :, 0:2],
                                                              in_=scT[:, c, :].rearrange("p (b r) -> p b r", r=2)),
                         reads=BL_scT, writes=Bp)
                    cur = pb[:, :, 2:10]; m1 = pb[:, :, 1:9]; m2 = pb[:, :, 0:8]

                    def v3(a):
                        return a.rearrange("p (b t) -> p b t", t=8)
                P.op('dve', lambda e, fcc=fcc, t_cu=t_cu, n=n, cur=cur, v3=v3: e.tensor_tensor(
                    out=cur, in0=v3(psf[fcc][:, 0:n]), in1=v3(tf[t_cu][:, 0:n]), op=ALU.mult),
                    reads=[B_psf[fcc], B_tf[t_cu]], writes=Bp)
                ffree(fcc)
                t_cv = tfa()
                P.op('act', lambda e, t_cv=t_cv, cur=cur, n=n, c=c, v3=v3: e.activation(
                    out=v3(tf[t_cv][:, 0:n]), in_=cur, func=AF.Identity, bias=cbias[:, c:c + 1], scale=cw[:, 2, c:c + 1]),
                    reads=Bp + RC, writes=[B_tf[t_cv]])
                P.op('dve', lambda e, t_cv=t_cv, m1=m1, n=n, c=c, v3=v3: e.scalar_tensor_tensor(
                    out=v3(tf[t_cv][:, 0:n]), in0=m1, scalar=cw[:, 1, c:c + 1], in1=v3(tf[t_cv][:, 0:n]),
                    op0=ALU.mult, op1=ALU.add), reads=Bp + [B_tf[t_cv]] + RC, writes=[B_tf[t_cv]])
                P.op('dve', lambda e, t_cv=t_cv, m2=m2, n=n, c=c, v3=v3: e.scalar_tensor_tensor(
                    out=v3(tf[t_cv][:, 0:n]), in0=m2, scalar=cw[:, 0, c:c + 1], in1=v3(tf[t_cv][:, 0:n]),
                    op0=ALU.mult, op1=ALU.add), reads=Bp + [B_tf[t_cv]] + RC, writes=[B_tf[t_cv]])
                t_th = tfa()
                P.op('act', lambda e, fcg=fcg, t_th=t_th, n=n: e.activation(
                    out=tf[t_th][:, 0:n], in_=psf[fcg][:, 0:n], func=AF.Silu),
                    reads=[B_psf[fcg]], writes=[B_tf[t_th]])
                ffree(fcg)
                P.op('dve', lambda e, fcb=fcb, t_cv=t_cv, n=n: e.tensor_tensor(
                    out=tf[t_cv][:, 0:n], in0=psf[fcb][:, 0:n], in1=tf[t_cv][:, 0:n], op=ALU.mult),
                    reads=[B_psf[fcb], B_tf[t_cv]], writes=[B_tf[t_cv]])
                ffree(fcb)
                P.op('pool', lambda e, t_cv=t_cv, t_th=t_th, n=n, c=c, g=g: e.tensor_tensor(
                    out=ob[:, c, sl(g)], in0=tf[t_cv][:, 0:n], in1=tf[t_th][:, 0:n], op=ALU.mult),
                    reads=[B_tf[t_cv], B_tf[t_th]], writes=[B_ob[c][g]])
            for (ntok, src2d, rdb, dst, Bd) in (
                    (2, pre[1][:, 512:514], B_pre[1], None, None),
                    (16, preS[:, :, 8], BL_preS, nct[0], B_nct[0]),
                    (16, preS[:, :, 9], BL_preS, nct[1], B_nct[1])):
                f1 = falloc()
                P.op('pe', lambda e, f1=f1, ntok=ntok, src2d=src2d: e.matmul(
                    psf[f1][0:ntok, 0:128], lhsT=src2d, rhs=idf[:, :], start=True, stop=True),
                    reads=[rdb] + RC, writes=[B_psf[f1]])
                if dst is None:
                    dsl = stg[c // 4][0:2, (c % 4) * 128:(c % 4 + 1) * 128]
                    Bd = [B_stg[c // 4]]
                else:
                    dsl = dst[0:ntok, c * 128:(c + 1) * 128]
                P.op('act', lambda e, f1=f1, ntok=ntok, dsl=dsl: e.copy(out=dsl, in_=psf[f1][0:ntok, 0:128]),
                     reads=[B_psf[f1]], writes=Bd)
                ffree(f1)
            wdone()
        for hf in range(2):
            P.op('sp', lambda e, hf=hf: e.dma_start(out=ncp[:, hf * 512:(hf + 1) * 512], in_=stg[hf][0:2, :]), reads=[B_stg[hf]],
                 dma_sem='o_stg%d' % hf, is_out=True)
        for t2_ in range(2):
            P.op('sp', lambda e, t2_=t2_: e.dma_start(out=ncs.rearrange("(b t) d -> t b d", t=2)[t2_], in_=nct[t2_]), reads=B_nct[t2_],
                 dma_sem='o_nc%d' % t2_, is_out=True)
        if stop < 4:
            P.disabled = True
        project_branch(1, True)

        if stop < 1:
            P.disabled = True
        P.tag = 'p1'
        _v, BL_memT = arH.view(0, 2048); memT = _v.rearrange("p (k m) -> p k m", k=8)
        _v, BL_KT = arH.view(2048, 2048); KT = _v.rearrange("p (k m) -> p k m", k=8)
        _v, BL_Vp = arH.view(4096, 2048); Vp = _v.rearrange("p (t d) -> p t d", t=2)
        P.op('sp', lambda e: e.dma_start(out=gaux[:], in_=mem_norm_g.partition_broadcast(128)), writes=[B_gaux], dma_sem='gaux')
        for t in range(2):
            P.op('sp', lambda e, t=t: e.dma_start(out=xbuf[t][:], in_=memp[t * 128:(t + 1) * 128, :]),
                 writes=[B_x[t]], dma_sem='x%d' % t)
            norm_tile(None, gaux, B_gaux, t)
            transpose8(t, memT[:, :, t * 128:(t + 1) * 128], BL_memT)
        for j in range(4):
            W, BW = wget()
            isK = j < 2
            for t in range(2):
                fi = falloc()
                for k in range(8):
                    mm(psf[fi][:], memT[:, k, t * 128:(t + 1) * 128], W[:, k, :], k == 0, k == 7, BL_memT + [BW], B_psf[fi])
                si = (j * 2 + t) % 2
                P.op('act', lambda e, fi=fi, si=si: e.copy(out=stg[si][:, 0:512], in_=psf[fi][:]),
                     reads=[B_psf[fi]], writes=[B_stg[si]])
                if not isK:
                    P.op('dve', lambda e, fi=fi, t=t, j=j: e.tensor_copy(out=Vp[:, t, (j - 2) * 512:(j - 1) * 512], in_=psf[fi][:]),
                         reads=[B_psf[fi]], writes=BL_Vp)
                ffree(fi)
                dst = (nmk if isK else nmv)[t * 128:(t + 1) * 128, (j % 2) * 512:(j % 2) * 512 + 512]
                P.op('sp', lambda e, dst=dst, si=si: e.dma_start(out=dst, in_=stg[si][:, 0:512]),
                     reads=[B_stg[si]], dma_sem='o_stg%d' % si, is_out=True)
            if isK:
                fi = falloc()
                for c in range(4):
                    if c == 2:
                        pass
                    half = c % 2
                    if c == 2:
                        P.op('dve', lambda e, fi=fi, j=j: e.tensor_copy(
                            out=KT[:, j * 4:j * 4 + 2, :], in_=psf[fi][:].rearrange("p (c m) -> p c m", c=2)),
                            reads=[B_psf[fi]], writes=BL_KT)
                        ffree(fi)
                        fi = falloc()
                    for k in range(8):
                        mm(psf[fi][:, half * 256:half * 256 + 256], W[:, k, c * 128:(c + 1) * 128], memT[:, k, :],
                           k == 0, k == 7, BL_memT + [BW], B_psf[fi])
                P.op('dve', lambda e, fi=fi, j=j: e.tensor_copy(
                    out=KT[:, j * 4 + 2:j * 4 + 4, :], in_=psf[fi][:].rearrange("p (c m) -> p c m", c=2)),
                    reads=[B_psf[fi]], writes=BL_KT)
                ffree(fi)
            wdone()

        if stop < 5:
            P.disabled = True
        kvs = []; B_kvs = []; kts = []; B_kts = []
        for _off in (6144, 7168, 1024):
            _v, _b = arH.view(_off, 1024); kvs.append(_v.rearrange("p (b c d) -> p b c d", b=2, c=2)); B_kvs.append(_b)

        kv_next = [0]

        def kv_fill(upto):
            while kv_next[0] < min(upto, 64):
                G = kv_next[0]
                kv_next[0] += 1
                hh, L = G // 16, G % 16
                src_t = cmk if L < 8 else cmv
                b0 = (L % 8) * 2
                src = src_t[b0:b0 + 2, :, hh * 256:(hh + 1) * 256].rearrange("b (c p) d -> p b c d", p=128)
                ri = G % 3
                P.op('pool', lambda e, ri=ri, src=src: e.dma_start(out=kvs[ri], in_=src), writes=B_kvs[ri], dma_sem='kv%d' % ri)

        for i in range(2):
            _v, _b = arH.view(i * 512, 512); kts.append(_v.rearrange("p (c m) -> p c m", c=2)); B_kts.append(_b)
        kv_fill(3)
        for h in range(4):
            W, BW = wget()
            for g in range(5):
                P.tag = 'mem' if g < 4 else 'mem_s'
                n = GROUPS[g][1]
                qi = tha()
                for dch in range(2):
                    fi = falloc()
                    for k in range(8):
                        mm(psf[fi][:, 0:n], W[:, k, dch * 128:(dch + 1) * 128], hT[:, k, sl(g)], k == 0, k == 7, [B_hT, BW], B_psf[fi])
                    P.op('act', lambda e, fi=fi, qi=qi, dch=dch, n=n: e.copy(out=th_[qi][:, dch * 512:dch * 512 + n], in_=psf[fi][:, 0:n]),
                         reads=[B_psf[fi]], writes=[B_th[qi]])
                    ffree(fi)
                sgi = [tfa(), tfa()]
                for ech in range(2):
                    fi = falloc()
                    for k in range(8):
                        mm(psf[fi][:, 0:n], W[:, k, 256 + ech * 128:256 + (ech + 1) * 128], hT[:, k, sl(g)], k == 0, k == 7,
                           [B_hT, BW], B_psf[fi])
                    ti = sgi[ech]
                    P.op('act', lambda e, fi=fi, ti=ti, n=n: e.activation(out=tf[ti][:, 0:n], in_=psf[fi][:, 0:n], func=AF.Tanh, scale=0.5),
                         reads=[B_psf[fi]], writes=[B_tf[ti]])
                    P.op('dve', lambda e, fi=fi, ti=ti, n=n: e.scalar_tensor_tensor(
                        out=tf[ti][:, 0:n], in0=tf[ti][:, 0:n], scalar=1.0, in1=psf[fi][:, 0:n], op0=ALU.add, op1=ALU.mult),
                        reads=[B_psf[fi], B_tf[ti]], writes=[B_tf[ti]])
                    ffree(fi)
                pi = tha()
                fden = falloc()
                fom = [falloc(), falloc()]
                if g < 4:
                    for mch in range(2):
                        fi = falloc()
                        for dch in range(2):
                            mm(psf[fi][:, 0:n], KT[:, h * 2 + dch, mch * 128:(mch + 1) * 128], th_[qi][:, dch * 512:dch * 512 + n],
                               dch == 0, dch == 1, BL_KT + [B_th[qi]], B_psf[fi])
                        P.op('act', lambda e, fi=fi, pi=pi, mch=mch, n=n: e.activation(
                            out=th_[pi][:, mch * 512:mch * 512 + n], in_=psf[fi][:, 0:n], func=AF.Exp, scale=1.0 / 16),
                            reads=[B_psf[fi]], writes=[B_th[pi]])
                        ffree(fi)
                    for mch in range(2):
                        mm(psf[fden][:, 0:n], ones[:], th_[pi][:, mch * 512:mch * 512 + n], mch == 0, mch == 1, [B_th[pi]] + RC, B_psf[fden])
                    for ech in range(2):
                        for mch in range(2):
                            mm(psf[fom[ech]][:, 0:n], Vp[:, mch, h * 256 + ech * 128:h * 256 + (ech + 1) * 128],
                               th_[pi][:, mch * 512:mch * 512 + n], mch == 0, mch == 1, BL_Vp + [B_th[pi]], B_psf[fom[ech]])
                else:
                    fsc = falloc()
                    for b in range(NB + 1):
                        if b < NB:
                            if b % 2 == 0:
                                kv_fill(h * 16 + b // 2 + 3)
                            ri = (h * 16 + b // 2) % 3
                            kb = b % 2
                            bi = b % 2
                            hi = halloc()
                            for mch in range(2):
                                for dch in range(2):
                                    tr(psh[hi][:, dch * 256 + mch * 128:dch * 256 + (mch + 1) * 128],
                                       kvs[ri][:, kb, mch, dch * 128:(dch + 1) * 128], idb[:], B_kvs[ri], B_psh[hi])
                            P.op('dve', lambda e, hi=hi, bi=bi: e.tensor_copy(
                                out=kts[bi][:].rearrange("p c m -> p (c m)"), in_=psh[hi][:, 0:512]),
                                reads=[B_psh[hi]], writes=B_kts[bi])
                            hfree(hi)
                        if b >= 1:
                            bb = b - 1
                            bi = bb % 2
                            for mch in range(2):
                                for dch in range(2):
                                    mm(psf[fsc][:, mch * 128 + bb * 8:mch * 128 + bb * 8 + 8], kts[bi][:, dch, mch * 128:(mch + 1) * 128],
                                       th_[qi][:, dch * 512 + bb * 8:dch * 512 + bb * 8 + 8], dch == 0, dch == 1,
                                       B_kts[bi] + [B_th[qi]], B_psf[fsc])
                    for mch in range(2):
                        P.op('act', lambda e, fsc=fsc, pi=pi, mch=mch: e.activation(
                            out=th_[pi][:, mch * 512:mch * 512 + 128], in_=psf[fsc][:, mch * 128:(mch + 1) * 128], func=AF.Exp, scale=1.0 / 16),
                            reads=[B_psf[fsc]], writes=[B_th[pi]])
                    ffree(fsc)
                    for mch in range(2):
                        mm(psf[fden][:, 0:n], ones[:], th_[pi][:, mch * 512:mch * 512 + n], mch == 0, mch == 1, [B_th[pi]] + RC, B_psf[fden])
                    for b in range(NB):
                        if b % 2 == 0:
                            kv_fill(h * 16 + 8 + b // 2 + 3)
                        ri = (h * 16 + 8 + b // 2) % 3
                        kb = b % 2
                        for ech in range(2):
                            for mch in range(2):
                                mm(psf[fom[ech]][:, b * 8:b * 8 + 8], kvs[ri][:, kb, mch, ech * 128:(ech + 1) * 128],
                                   th_[pi][:, mch * 512 + b * 8:mch * 512 + b * 8 + 8], mch == 0, mch == 1,
                                   B_kvs[ri] + [B_th[pi]], B_psf[fom[ech]])
                    kv_fill(h * 16 + 16 + 3)
                    wdone()
                tr_ = tfa()
                P.op('dve', lambda e, fden=fden, tr_=tr_, n=n: e.reciprocal(out=tf[tr_][:, 0:n], in_=psf[fden][:, 0:n]),
                     reads=[B_psf[fden]], writes=[B_tf[tr_]])
                ffree(fden)
                for ech in range(2):
                    ti = sgi[ech]
                    P.op('dve', lambda e, ech=ech, tr_=tr_, ti=ti, n=n, fo=fom[ech]: e.scalar_tensor_tensor(
                        out=tf[ti][:, 0:n], in0=psf[fo][:, 0:n], scalar=0.5, in1=tf[ti][:, 0:n], op0=ALU.mult, op1=ALU.mult),
                        reads=[B_psf[fom[ech]], B_tf[ti]], writes=[B_tf[ti]])
                    ffree(fom[ech])
                    P.op('pool', lambda e, ech=ech, tr_=tr_, ti=ti, n=n, h=h, g=g: e.tensor_tensor(
                        out=ob[:, h * 2 + ech, sl(g)], in0=tf[ti][:, 0:n], in1=tf[tr_][:, 0:n], op=ALU.mult),
                        reads=[B_tf[ti], B_tf[tr_]], writes=[B_ob[h * 2 + ech][g]])
        if stop < 6:
            P.disabled = True
        project_branch(2, False)

        if stop < 7:
            P.disabled = True
        tf_n[0] = NTF + 2
        P.op('sp', lambda e: e.dma_start(out=gaux[:], in_=ret_norm_g.partition_broadcast(128)), writes=[B_gaux], dma_sem='gaux')
        gret = gaux
        _v, BL_S32 = arF.view(0, 512); S32 = _v.rearrange("p (c e) -> p c e", c=2)
        Ss32 = []; B_Ss32 = []; csb = []; B_csb = []
        for i in range(2):
            _v, _b = arF.view(1536 + i * 1024, 1024); csb.append(_v); B_csb.append(_b)
        for _off in (512, 1024, 2560, 3072):
            _v, _b = arF.view(_off, 512); Ss32.append(_v.rearrange("p (c e) -> p c e", c=2)); B_Ss32.append(_b)
        maskSf, BL_maskSf = arF.view(3584, 128)
        rmd, BL_rmd = arF.view(3840, 64)
        kdecX, BL_kdecX = arF.view(3904, 16)
        maskX = maskT[:].rearrange("p h i -> p (h i)")
        qdecX = qdec[:].rearrange("p h i -> p (h i)")
        B_mx = Buf('maskX'); B_qx = Buf('qdecX'); ht_done = set(); cs0_done = set()
        P.op('sp', lambda e: e.dma_start(out=rmd, in_=c_rmd), writes=BL_rmd, dma_sem='hc')
        P.op('sp', lambda e: e.dma_start(out=kdecX, in_=c_kdecX), writes=BL_kdecX, dma_sem='hc')
        gsg1 = xbuf[0][:].rearrange("p (c e) -> p c e", c=4)
        B_gsg1 = [B_x[0]]
        Sbf = []; B_Sbf = []; Ssbf = []; B_Ssbf = []; stmX = []; B_stmX = []
        for i in range(2):
            _v, _b = arH.view(i * 512, 512); Sbf.append(_v.rearrange("p (c e) -> p c e", c=2)); B_Sbf.append(_b)
            _v, _b = arH.view(1024 + i * 512, 512); Ssbf.append(_v.rearrange("p (c e) -> p c e", c=2)); B_Ssbf.append(_b)
            _v, _b = arH.view(2048 + i * 1536, 1280); stmX.append(_v); B_stmX.append(_b)
        _v, BL_kdm = arH.view(2048, 4096); kdm = _v.rearrange("p (b d) -> p b d", b=NB)
        stmS, BL_stmS = arH.view(6144, 128)
        vtmS, BL_vtmS = arH.view(6272, 256)
        _v, _b = arH.view(6656, 512); Ssbf.append(_v.rearrange("p (c e) -> p c e", c=2)); B_Ssbf.append(_b)
        vtm1 = hb[1][:].rearrange("p (c e) -> p c e", c=4); B_vtm1 = [B_hb[1]]
        _v, BL_kdt = arH.view(7168, 1024); kdt1 = _v.rearrange("p (c e) -> p c e", c=4)
        onb1 = hb[0][:].rearrange("p (c e) -> p c e", c=4); B_onb1 = [B_hb[0]]
        SOFF = [0, 512, 896, 1152]
        stgS = [(stg[0][:, 0:512], [B_stg[0]]), (stg[1][:, 0:512], [B_stg[1]])]
        for _off in (0, 7168):
            _v, _b = arH.view(_off, 1024); stgS.append((_v.bitcast(F32), _b))

        def rotary(fps, g, n, dst_i, Bdst):
            cs = csb[g % 2][:, 0:n]; sn = csb[g % 2][:, 512:512 + n]
            p1 = psf[fps[0]][:, 0:n]; p2 = psf[fps[1]][:, 0:n]
            b1 = tfa(); b2 = tfa(); a1 = tfa(); a2 = tfa()
            P.op('dve', lambda e: e.tensor_tensor(out=tf[b1][:, 0:n], in0=p1, in1=cs, op=ALU.mult),
                 reads=[B_psf[fps[0]]] + B_csb[g % 2], writes=[B_tf[b1]])
            P.op('dve', lambda e: e.tensor_tensor(out=tf[b2][:, 0:n], in0=p2, in1=sn, op=ALU.mult),
                 reads=[B_psf[fps[1]]] + B_csb[g % 2], writes=[B_tf[b2]])
            P.op('pool', lambda e: e.tensor_tensor(out=th_[dst_i][:, 0:n], in0=tf[b1][:, 0:n], in1=tf[b2][:, 0:n], op=ALU.subtract),
                 reads=[B_tf[b1], B_tf[b2]], writes=[Bdst])
            P.op('dve', lambda e: e.tensor_tensor(out=tf[a1][:, 0:n], in0=p1, in1=sn, op=ALU.mult),
                 reads=[B_psf[fps[0]]] + B_csb[g % 2], writes=[B_tf[a1]])
            P.op('dve', lambda e: e.tensor_tensor(out=tf[a2][:, 0:n], in0=p2, in1=cs, op=ALU.mult),
                 reads=[B_psf[fps[1]]] + B_csb[g % 2], writes=[B_tf[a2]])
            ffree(fps[0]); ffree(fps[1])
            P.op('pool', lambda e: e.tensor_tensor(out=th_[dst_i][:, 512:512 + n], in0=tf[a1][:, 0:n], in1=tf[a2][:, 0:n], op=ALU.add),
                 reads=[B_tf[a1], B_tf[a2]], writes=[Bdst])

        pending = []
        for h in range(4):
            WA, BWA = wget()
            WB, BWB = wget()
            def head_tables(hh):
                if hh < 4 and hh not in ht_done:
                    ht_done.add(hh)
                    P.op('sp', lambda e, hh=hh: e.dma_start(out=maskX, in_=c_maskX[hh]), writes=[B_mx], dma_sem='hc')
                    P.op('sp', lambda e, hh=hh: e.dma_start(out=qdecX, in_=c_qdecX[hh].partition_broadcast(128)), writes=[B_qx], dma_sem='hc2')

            head_tables(h)
            P.op('sp', lambda e, h=h: e.dma_start(out=maskSf, in_=c_maskSf[h]), writes=BL_maskSf, dma_sem='hc3')

            def s_issue(b, h=h):
                if b < NB:
                    P.op('pool', lambda e, b=b, h=h: e.dma_start(
                        out=Ss32[b % 4], in_=sret[b, h].rearrange("(c p) e -> p c e", p=128)),
                        writes=B_Ss32[b % 4], dma_sem='ss%d' % (b % 4))

            for g in range(5):
                P.tag = 'ret' if g < 4 else 'ret_s'
                n = GROUPS[g][1]
                if g == 3:
                    s_issue(0); s_issue(1)
                if g == 4:
                    s_issue(2); s_issue(3)
                if not (g == 0 and h in cs0_done):
                    P.op('sp', lambda e, g=g, n=n: e.dma_start(out=csb[g % 2][:, 0:n], in_=c_cos[:, sl(g)]), writes=B_csb[g % 2], dma_sem='cs%d' % (g % 2))
                    P.op('sp', lambda e, g=g, n=n: e.dma_start(out=csb[g % 2][:, 512:512 + n], in_=c_sin[:, sl(g)]), writes=B_csb[g % 2], dma_sem='cs%d' % (g % 2))
                qi = tha(); ki = tha(); qdi = tha()
                for (dst_i, off) in ((qi, 0), (ki, 256)):
                    fps = [falloc(), falloc()]
                    for dch in range(2):
                        for k in range(8):
                            mm(psf[fps[dch]][:, 0:n], WA[:, k, off + dch * 128:off + (dch + 1) * 128], hT[:, k, sl(g)],
                               k == 0, k == 7, [B_hT, BWA], B_psf[fps[dch]])
                    rotary(fps, g, n, dst_i, B_th[dst_i])
                if g == 4:
                    while pending:
                        pending.pop(0)()
                    head_tables(h + 1)
                    if h < 3:
                        cs0_done.add(h + 1)
                        P.op('sp', lambda e: e.dma_start(out=csb[0][:, 0:512], in_=c_cos[:, 0:512]), writes=B_csb[0], dma_sem='cs0')
                        P.op('sp', lambda e: e.dma_start(out=csb[0][:, 512:1024], in_=c_sin[:, 0:512]), writes=B_csb[0], dma_sem='cs0')
                for dch in range(2):
                    if g < 4:
                        o3 = th_[qdi][:, dch * 512:dch * 512 + 512]
                        i3 = th_[qi][:, dch * 512:dch * 512 + 512]
                        d3 = qdecX
                        rd = [B_th[qi], B_qx]
                    else:
                        o3 = th_[qdi][:, dch * 512:dch * 512 + 128].rearrange("p (b t) -> p b t", t=8)
                        i3 = th_[qi][:, dch * 512:dch * 512 + 128].rearrange("p (b t) -> p b t", t=8)
                        d3 = qdecS[:, h, :].unsqueeze(1).to_broadcast([128, NB, 8])
                        rd = [B_th[qi]] + RC
                    P.op('pool' if g < 4 else 'dve', lambda e, o3=o3, i3=i3, d3=d3: e.tensor_tensor(out=o3, in0=i3, in1=d3, op=ALU.mult),
                         reads=rd, writes=[B_th[qdi]])
                if g < 4:
                    fv = [falloc(), falloc()]
                    for c in range(4):
                        for k in range(8):
                            mm(psf[fv[c // 2]][:, (c % 2) * 256:(c % 2) * 256 + 256], hT[:, k, g * 512 + c * 128:g * 512 + (c + 1) * 128],
                               WB[:, k, 0:256], k == 0, k == 7, [B_hT, BWB], B_psf[fv[c // 2]])
                    for hf in range(2):
                        P.op('act', lambda e, hf=hf, f=fv[hf]: e.copy(
                            out=vtm1[:, hf * 2:hf * 2 + 2, :].rearrange("p c e -> p (c e)"), in_=psf[f][:]),
                            reads=[B_psf[fv[hf]]], writes=B_vtm1)
                        ffree(fv[hf])
                    fg_ = [falloc(), falloc()]
                    for c in range(4):
                        for k in range(8):
                            mm(psf[fg_[c // 2]][:, (c % 2) * 256:(c % 2) * 256 + 256], hT[:, k, g * 512 + c * 128:g * 512 + (c + 1) * 128],
                               WB[:, k, 256:512], k == 0, k == 7, [B_hT, BWB], B_psf[fg_[c // 2]])
                    for hf in range(2):
                        ti = tfa()
                        f = fg_[hf]
                        gv = gsg1[:, hf * 2:hf * 2 + 2, :]
                        P.op('act', lambda e, f=f, ti=ti: e.activation(out=tf[ti][:], in_=psf[f][:], func=AF.Silu),
                             reads=[B_psf[f]], writes=[B_tf[ti]])
                        ffree(f)
                        P.op('pool', lambda e, ti=ti, gv=gv, h=h: e.tensor_tensor(
                            out=gv, in0=tf[ti][:].rearrange("p (c e) -> p c e", c=2),
                            in1=gret[:, h * 256:(h + 1) * 256].unsqueeze(1).to_broadcast([128, 2, 256]), op=ALU.mult),
                            reads=[B_tf[ti], B_gaux], writes=B_gsg1)
                    hi = halloc()
                    for c in range(4):
                        for dch in range(2):
                            tr(psh[hi][:, c * 256 + dch * 128:c * 256 + (dch + 1) * 128],
                               th_[ki][:, dch * 512 + c * 128:dch * 512 + (c + 1) * 128], idb[:], [B_th[ki]], B_psh[hi])
                    for c in range(4):
                        P.op('act', lambda e, hi=hi, h=h, c=c: e.activation(
                            out=kdt1[:, c, :], in_=psh[hi][:, c * 256:(c + 1) * 256], func=AF.Copy, scale=kdecX[:, h * 4 + c:h * 4 + c + 1]),
                            reads=[B_psh[hi]] + BL_kdecX, writes=BL_kdt)
                    hfree(hi)
                    sx = g % 2
                    for cp in range(4):
                        nn = 512 - 128 * cp
                        fs = falloc()
                        for dch in range(2):
                            mm(psf[fs][:, 0:nn], th_[ki][:, dch * 512 + cp * 128:dch * 512 + cp * 128 + 128],
                               th_[qi][:, dch * 512 + cp * 128:dch * 512 + 512], dch == 0, dch == 1, [B_th[ki], B_th[qi]], B_psf[fs])
                        P.op('dve', lambda e, fs=fs, sx=sx, cp=cp, nn=nn: e.tensor_tensor(
                            out=stmX[sx][:, SOFF[cp]:SOFF[cp] + nn], in0=psf[fs][:, 0:nn], in1=maskX[:, 0:nn], op=ALU.mult),
                            reads=[B_psf[fs], B_mx], writes=B_stmX[sx])
                        ffree(fs)
                    while pending:
                        pending.pop(0)()
                    fo = [falloc(), falloc()]
                    scur = g % 2
                    for c in range(4):
                        oview = psf[fo[c // 2]][:, (c % 2) * 256:(c % 2) * 256 + 256]
                        nmm = (c + 1) + (2 if g > 0 else 0)
                        im = 0
                        for cp in range(c + 1):
                            mm(oview, stmX[sx][:, SOFF[cp] + (c - cp) * 128:SOFF[cp] + (c - cp) * 128 + 128], vtm1[:, cp, :],
                               im == 0, im == nmm - 1, [B_stmX[sx], B_vtm1], B_psf[fo[c // 2]])
                            im += 1
                        if g > 0:
                            for dch in range(2):
                                mm(oview, th_[qdi][:, dch * 512 + c * 128:dch * 512 + c * 128 + 128], Sbf[scur][:, dch, :], False, im == nmm - 1,
                                   [B_th[qdi], B_Sbf[scur]], B_psf[fo[c // 2]])
                                im += 1
                    fkv = falloc()
                    for dch in range(2):
                        for c in range(4):
                            mm(psf[fkv][:, dch * 256:dch * 256 + 256], kdt1[:, c, dch * 128:(dch + 1) * 128], vtm1[:, c, :],
                               c == 0, c == 3, [BL_kdt, B_vtm1], B_psf[fkv])
                    if g > 0:
                        P.op('dve', lambda e, fkv=fkv, h=h: e.scalar_tensor_tensor(
                            out=S32.rearrange("p c e -> p (c e)"), in0=S32.rearrange("p c e -> p (c e)"), scalar=CDX[h],
                            in1=psf[fkv][:], op0=ALU.mult, op1=ALU.add), reads=[B_psf[fkv], BL_S32], writes=[BL_S32])
                    else:
                        P.op('dve', lambda e, fkv=fkv: e.tensor_copy(out=S32.rearrange("p c e -> p (c e)"), in_=psf[fkv][:]),
                             reads=[B_psf[fkv]], writes=[BL_S32])
                    ffree(fkv)
                    if g < 3:
                        snx = (g + 1) % 2
                        P.op('act', lambda e, snx=snx: e.copy(out=Sbf[snx], in_=S32), reads=[BL_S32], writes=[B_Sbf[snx]])
                    else:
                        P.op('sp', lambda e, h=h: e.dma_start(out=nrp[h].rearrange("(c p) e -> p c e", p=128), in_=S32),
                             reads=[BL_S32], dma_sem='o_nrp', is_out=True)
                    st = stat()
                    for c in range(4):
                        oview = psf[fo[c // 2]][:, (c % 2) * 256:(c % 2) * 256 + 256]
                        P.op('act', lambda e, oview=oview, st=st, c=c: e.activation(
                            out=scr[:, 0:256], in_=oview, func=AF.Square, accum_out=ssb[:, st * 4 + c:st * 4 + c + 1]),
                            reads=[B_psf[fo[c // 2]]], writes=[B_scr, B_st[st]])
                    st2 = stat()
                    P.op('dve', lambda e, st=st, st2=st2: e.tensor_scalar(
                        out=ssb[:, st2 * 4:st2 * 4 + 4], in0=ssb[:, st * 4:st * 4 + 4], scalar1=1.0 / 256, scalar2=EPS,
                        op0=ALU.mult, op1=ALU.add), reads=[B_st[st]], writes=[B_st[st2]])
                    P.op('pool', lambda e, st2=st2: e.tensor_tensor(
                        out=ssb[:, st2 * 4:st2 * 4 + 4], in0=ssb[:, st2 * 4:st2 * 4 + 4], in1=mhalf[:, 0:4], op=ALU.pow),
                        reads=[B_st[st2]] + RC, writes=[B_st[st2]])
                    for c in range(4):
                        oview = psf[fo[c // 2]][:, (c % 2) * 256:(c % 2) * 256 + 256]
                        P.op('dve', lambda e, oview=oview, st2=st2, c=c: e.scalar_tensor_tensor(
                            out=onb1[:, c, :], in0=oview, scalar=ssb[:, st2 * 4 + c:st2 * 4 + c + 1], in1=gsg1[:, c, :],
                            op0=ALU.mult, op1=ALU.mult), reads=[B_psf[fo[c // 2]], B_st[st2], B_gsg1], writes=B_onb1)
                    ffree(fo[0]); ffree(fo[1])

                    def tail(h=h, g=g):
                        hi = halloc()
                        for ech in range(2):
                            for c in range(4):
                                tr(psh[hi][:, ech * 512 + c * 128:ech * 512 + (c + 1) * 128], onb1[:, c, ech * 128:(ech + 1) * 128],
                                   idb[:], B_onb1, B_psh[hi])
                        P.op('act', lambda e, hi=hi, h=h, g=g: e.copy(
                            out=ob[:, h * 2:h * 2 + 2, sl(g)], in_=psh[hi][:].rearrange("p (c t) -> p c t", c=2)),
                            reads=[B_psh[hi]], writes=[B_ob[h * 2][g], B_ob[h * 2 + 1][g]])
                        hfree(hi)
                    pending.append(tail)
                else:
                    fv = falloc()
                    for k in range(8):
                        mm(psf[fv][:, 0:256], hT[:, k, LP:NT], WB[:, k, 0:256], k == 0, k == 7, [B_hT, BWB], B_psf[fv])
                    P.op('act', lambda e, fv=fv: e.copy(out=vtmS, in_=psf[fv][:, 0:256]), reads=[B_psf[fv]], writes=BL_vtmS)
                    ffree(fv)
                    fs = falloc()
                    for dch in range(2):
                        mm(psf[fs][:, 0:128], th_[ki][:, dch * 512:dch * 512 + 128], th_[qi][:, dch * 512:dch * 512 + 128],
                           dch == 0, dch == 1, [B_th[ki], B_th[qi]], B_psf[fs])
                    P.op('dve', lambda e, fs=fs: e.tensor_tensor(out=stmS, in0=psf[fs][:, 0:128], in1=maskSf, op=ALU.mult),
                         reads=[B_psf[fs]] + BL_maskSf, writes=BL_stmS)
                    ffree(fs)
                    hi = halloc()
                    for dch in range(2):
                        tr(psh[hi][:, dch * 128:(dch + 1) * 128], th_[ki][:, dch * 512:dch * 512 + 128], idb[:], [B_th[ki]], B_psh[hi])
                    P.op('dve', lambda e, hi=hi, h=h: e.tensor_tensor(
                        out=kdm, in0=psh[hi][:, 0:256].unsqueeze(1).to_broadcast([128, NB, 256]),
                        in1=rmd[:, h * NB:(h + 1) * NB].unsqueeze(2).to_broadcast([128, NB, 256]), op=ALU.mult),
                        reads=[B_psh[hi]] + BL_rmd, writes=BL_kdm)
                    hfree(hi)
                    sgi = [tfa(), tfa()]
                    for ech in range(2):
                        fi = falloc()
                        for k in range(8):
                            mm(psf[fi][:, 0:128], WB[:, k, 256 + ech * 128:256 + (ech + 1) * 128], hT[:, k, LP:NT], k == 0, k == 7,
                               [B_hT, BWB], B_psf[fi])
                        ti = sgi[ech]
                        P.op('act', lambda e, fi=fi, ti=ti: e.activation(out=tf[ti][:, 0:128], in_=psf[fi][:, 0:128], func=AF.Tanh, scale=0.5),
                             reads=[B_psf[fi]], writes=[B_tf[ti]])
                        P.op('dve', lambda e, fi=fi, ti=ti: e.scalar_tensor_tensor(
                            out=tf[ti][:, 0:128], in0=tf[ti][:, 0:128], scalar=1.0, in1=psf[fi][:, 0:128], op0=ALU.add, op1=ALU.mult),
                            reads=[B_psf[fi], B_tf[ti]], writes=[B_tf[ti]])
                        ffree(fi)
                    foTs = [falloc(), falloc()]
                    for ech in range(2):
                        mm(psf[foTs[ech]][:, 0:128], vtmS[:, ech * 128:(ech + 1) * 128], stmS, True, False,
                           [BL_vtmS, BL_stmS], B_psf[foTs[ech]])
                    def s_cast(b):
                        if b < NB:
                            P.op('act', lambda e, b=b: e.copy(out=Ssbf[b % 3], in_=Ss32[b % 4]), reads=B_Ss32[b % 4], writes=B_Ssbf[b % 3])

                    s_cast(0)
                    s_cast(1)
                    s_cast(2)
                    for b in range(NB):
                        bi = b % 3
                        s4 = b % 4
                        sg_ap, sg_b = stgS[b % 4]
                        P.op('act', lambda e, s4=s4, sg_ap=sg_ap, h=h: e.activation(
                            out=sg_ap, in_=Ss32[s4].rearrange("p c e -> p (c e)"), func=AF.Copy, scale=CDS[h]),
                            reads=B_Ss32[s4], writes=sg_b)
                        s_issue(b + 4)
                        for ech in range(2):
                            ov = psf[foTs[ech]][:, b * 8:b * 8 + 8]
                            for dch in range(2):
                                mm(ov, Ssbf[bi][:, dch, ech * 128:(ech + 1) * 128], th_[qdi][:, dch * 512 + b * 8:dch * 512 + b * 8 + 8],
                                   False, (dch == 1 and b == NB - 1), [B_Ssbf[bi], B_th[qdi]], B_psf[foTs[ech]])
                        fkv = falloc()
                        for dch in range(2):
                            mm(psf[fkv][:, dch * 256:dch * 256 + 256], kdm[:, b, dch * 128:(dch + 1) * 128], vtmS, True, True,
                               [BL_kdm, BL_vtmS], B_psf[fkv])
                        s_cast(b + 3)
                        P.op('dve', lambda e, fkv=fkv, sg_ap=sg_ap: e.tensor_tensor(
                            out=sg_ap, in0=psf[fkv][:], in1=sg_ap, op=ALU.add), reads=[B_psf[fkv]] + sg_b, writes=sg_b)
                        ffree(fkv)
                        P.op('sp', lambda e, b=b, sg_ap=sg_ap, h=h: e.dma_start(
                            out=nrs[b, h].rearrange("(c p) e -> p c e", p=128), in_=sg_ap.rearrange("p (c e) -> p c e", c=2)),
                            reads=sg_b, dma_sem='o_nrs%d' % (b % 4), is_out=True)
                    wdone()
                    wdone()
                    sq = tha()
                    for ech in range(2):
                        P.op('act', lambda e, f=foTs[ech], sq=sq, ech=ech: e.activation(
                            out=th_[sq][:, ech * 128:(ech + 1) * 128], in_=psf[f][:, 0:128], func=AF.Square),
                            reads=[B_psf[foTs[ech]]], writes=[B_th[sq]])
                    fss = falloc()
                    for ech in range(2):
                        mm(psf[fss][:, 0:128], ones[:], th_[sq][:, ech * 128:(ech + 1) * 128], ech == 0, ech == 1, [B_th[sq]] + RC, B_psf[fss])
                    trs = tfa()
                    P.op('act', lambda e, fss=fss, trs=trs: e.activation(
                        out=tf[trs][:, 0:128], in_=psf[fss][:, 0:128], func=AF.Sqrt, bias=epsb4[:, 0:1], scale=4.0 / 256),
                        reads=[B_psf[fss]] + RC, writes=[B_tf[trs]])
                    ffree(fss)
                    P.op('dve', lambda e, trs=trs: e.reciprocal(out=tf[trs][:, 0:128], in_=tf[trs][:, 0:128]),
                         reads=[B_tf[trs]], writes=[B_tf[trs]])
                    for ech in range(2):
                        ti = sgi[ech]
                        P.op('dve', lambda e, ech=ech, ti=ti, trs=trs, f=foTs[ech], h=h: e.scalar_tensor_tensor(
                            out=tf[ti][:, 128:256], in0=psf[f][:, 0:128], scalar=gretT[:, h * 2 + ech:h * 2 + ech + 1],
                            in1=tf[trs][:, 0:128], op0=ALU.mult, op1=ALU.mult), reads=[B_psf[foTs[ech]], B_tf[trs], B_tf[ti]] + RC, writes=[B_tf[ti]])
                        P.op('pool', lambda e, ech=ech, ti=ti, h=h: e.tensor_tensor(
                            out=ob[:, h * 2 + ech, LP:NT], in0=tf[ti][:, 128:256], in1=tf[ti][:, 0:128], op=ALU.mult),
                            reads=[B_tf[ti]], writes=[B_ob[h * 2 + ech][4]])
                    ffree(foTs[0]); ffree(foTs[1])
        if stop < 8:
            P.disabled = True
        project_branch(0, False)

        if stop < 9:
            P.disabled = True
        tf_n[0] = NTF
        P.op('sp', lambda e: e.dma_start(out=gaux[:], in_=final_norm_g.partition_broadcast(128)), writes=[B_gaux], dma_sem='gaux')
        W0, BW0 = wget()
        W1, BW1 = wget()
        allmg = [B_mg[fc][g] for fc in range(8) for g in range(5)]
        P.tag = 'final'
        def xload(t):
            if t < NTILE:
                P.op('sp', lambda e, t=t: e.dma_start(out=xbuf[t % NXR][:], in_=xtile_src(t)), writes=[B_x[t % NXR]], dma_sem='x%d' % (t % NXR))

        for t in range(4):
            xload(t)
        for t in range(NTILE):
            xi = t % NXR
            xload(t + 4)
            g = min(t // 4, 4)
            fr = [falloc(), falloc()]
            for j, (W, BW) in enumerate(((W0, BW0), (W1, BW1))):
                for k in range(8):
                    mm(psf[fr[j]][:], mgd[:, k, t * 128:(t + 1) * 128], W[:, k, :], k == 0, k == 7, [B_mg[k][g], BW], B_psf[fr[j]])
            for j in range(2):
                P.op('dve', lambda e, j=j, xi=xi, f=fr[j]: e.scalar_tensor_tensor(
                    out=xbuf[xi][:, j * 512:(j + 1) * 512], in0=psf[f][:], scalar=0.5, in1=xbuf[xi][:, j * 512:(j + 1) * 512],
                    op0=ALU.mult, op1=ALU.add), reads=[B_psf[fr[j]], B_x[xi]], writes=[B_x[xi]])
                ffree(fr[j])
            st = stat()
            sa = ssb[:, st * 4:st * 4 + 1]; sb_ = ssb[:, st * 4 + 1:st * 4 + 2]
            P.op('act', lambda e, xi=xi, sa=sa: e.activation(out=hb[xi][:], in_=xbuf[xi][:], func=AF.Square, accum_out=sa),
                 reads=[B_x[xi]], writes=[B_hb[xi], B_st[st]])
            rstd_of(sa, sb_, D, [B_st[st]])
            P.op('dve', lambda e, xi=xi, sb_=sb_: e.scalar_tensor_tensor(
                out=xbuf[xi][:], in0=xbuf[xi][:], scalar=sb_, in1=gaux[:], op0=ALU.mult, op1=ALU.mult),
                reads=[B_x[xi], B_st[st], B_gaux], writes=[B_x[xi]])
            dst = yp[t * 128:(t + 1) * 128, :] if t < 16 else ys
            P.op('sp', lambda e, dst=dst, xi=xi: e.dma_start(out=dst, in_=xbuf[xi][:]), reads=[B_x[xi]],
                 dma_sem='o_x%d' % xi, is_out=True)

        P.emit(nc)
    return nc, consts


_CACHE = {}


def kernel(x_prompt, x_sample, state_ret, state_conv, cache_mem_k, cache_mem_v, mem_prompt,
           norm_g, w_in, b_gate, ret_norm_g, conv_w, conv_b, w_ret_o, w_conv_o, w_mem_o,
           w_out, mem_norm_g, w_mem_kv, final_norm_g):
    f = lambda a: np.ascontiguousarray(np.asarray(a, dtype=np.float32))
    if 'nc' not in _CACHE:
        _CACHE['nc'] = build_nc()
    nc, consts = _CACHE['nc']
    shared = {
        "norm_g": f(norm_g).reshape(D), "b_gate": f(b_gate).reshape(3072),
        "ret_norm_g": f(ret_norm_g).reshape(1024), "conv_w": f(conv_w).reshape(3, D), "conv_b": f(conv_b).reshape(D),
        "mem_norm_g": f(mem_norm_g).reshape(D), "final_norm_g": f(final_norm_g).reshape(D),
        "wb": _pack_weights({"w_in": f(w_in).reshape(D, 13312), "w_ret_o": f(w_ret_o).reshape(D, D),
                             "w_conv_o": f(w_conv_o).reshape(D, D), "w_mem_o": f(w_mem_o).reshape(D, D),
                             "w_out": f(w_out).reshape(D, D), "w_mem_kv": f(w_mem_kv).reshape(D, 2048)}),
        "c_cos": consts['cos'], "c_sin": consts['sin'], "c_maskX": consts['maskX'], "c_qdecX": consts['qdecX'],
        "c_kdecX": consts['kdecX'], "c_maskSf": consts['maskSf'], "c_rmd": consts['rmd'], "c_qdecS": consts['qdecS'],
        "c_ident": consts['ident'],
    }
    xpr = f(x_prompt); xsa = f(x_sample); sr = f(state_ret); sc = f(state_conv)
    ck = f(cache_mem_k); cv = f(cache_mem_v); mp = f(mem_prompt)
    in_maps = []
    for c in range(NCORES):
        bs = slice(c * NB, (c + 1) * NB)
        m = dict(shared)
        m["xp"] = xpr[c]
        m["xs"] = xsa[bs].reshape(NS, D)
        m["sret"] = sr[0, bs]
        m["sconv"] = sc[0, bs].reshape(NB * 2, D)
        m["cmk"] = ck[0, bs].reshape(NB, 256, D)
        m["cmv"] = cv[0, bs].reshape(NB, 256, D)
        m["memp"] = mp[c]
        in_maps.append(m)
    res = run_bass_kernel_spmd(nc, in_maps, core_ids=list(range(NCORES)))
    R = res.results
    y_prompt = np.stack([R[c]["yp"] for c in range(NCORES)], 0).reshape(8, LP, D)
    y_sample = np.concatenate([R[c]["ys"].reshape(NB, LS, D) for c in range(NCORES)], 0)
    nrp_ = np.stack([R[c]["nrp"] for c in range(NCORES)], 0)[None]
    nrs_ = np.concatenate([R[c]["nrs"] for c in range(NCORES)], 0)[None]
    ncp_ = np.stack([R[c]["ncp"] for c in range(NCORES)], 0)[None]
    ncs_ = np.concatenate([R[c]["ncs"].reshape(NB, 2, D) for c in range(NCORES)], 0)[None]
    nmk_ = np.stack([R[c]["nmk"].reshape(256, 4, 256) for c in range(NCORES)], 0)[None]
    nmv_ = np.stack([R[c]["nmv"].reshape(256, 4, 256) for c in range(NCORES)], 0)[None]
    return (y_prompt.astype(np.float32), y_sample.astype(np.float32), nrp_.astype(np.float32), nrs_.astype(np.float32),
            ncp_.astype(np.float32), ncs_.astype(np.float32), nmk_.astype(np.float32), nmv_.astype(np.float32))
```

```python
import contextlib
import numpy as np
import concourse.bass as bass
import concourse.mybir as mybir
from concourse.bass_utils import run_bass_kernel_spmd

F32 = mybir.dt.float32
BF16 = mybir.dt.bfloat16
AF = mybir.ActivationFunctionType
ALU = mybir.AluOpType

NCORES = 8
D = 1024
LP = 2048
NB = 16
LS = 8
NS = NB * LS
NT = LP + NS
NTILE = NT // 128
PAST = 16384
EPS = 1e-6
GROUPS = [(0, 512), (512, 512), (1024, 512), (1536, 512), (2048, 128)]

COMPUTE = ('pe', 'act', 'dve', 'pool')
ENGS = ('pe', 'act', 'dve', 'pool', 'sp')


class Buf:
    __slots__ = ('name', 'last_w', 'reads', 'excl')

    def __init__(self, name='', excl=False):
        self.name = name
        self.last_w = None
        self.reads = []
        self.excl = excl


class Op:
    __slots__ = ('eng', 'idx', 'fn', 'waits', 'signal', 'sem', 'val', 'is_dma', 'signo', 'tag')

    def __init__(self, eng, idx, fn, is_dma):
        self.eng = eng
        self.idx = idx
        self.fn = fn
        self.waits = []
        self.signal = False
        self.sem = None
        self.val = 0
        self.is_dma = is_dma
        self.signo = 0


def _flat(x):
    out = []
    for i in x:
        if isinstance(i, (list, tuple)):
            out.extend(_flat(i))
        else:
            out.append(i)
    return out


class Plan:
    def __init__(self):
        self.streams = {e: [] for e in ENGS}
        self.waited = {e: {} for e in ENGS}
        self.waited_dma = {e: {} for e in ENGS}
        self.dma_counts = {}
        self.out_sems = set()
        self.disabled = False
        self.tag = 'init'

    def _dep(self, ev, d, kind):
        if d is ev:
            return
        if d.is_dma:
            w = self.waited_dma[ev.eng]
            if w.get(d.sem, 0) >= d.val:
                return
            need = self.dma_counts[d.sem]
            w[d.sem] = need
            ev.waits.append((d.sem, need))
            return
        if d.eng == ev.eng and not ev.is_dma and d.eng == 'pe':
            return
        w = self.waited[ev.eng]
        if w.get(d.eng, -1) >= d.idx:
            return
        w[d.eng] = d.idx
        d.signal = True
        ev.waits.append(d)

    def op(self, eng, fn, reads=(), writes=(), dma_sem=None, is_out=False):
        reads = _flat(reads)
        writes = _flat(writes)
        st = self.streams[eng]
        ev = Op(eng, len(st), fn, dma_sem is not None)
        ev.tag = self.tag
        if self.disabled:
            return ev
        best = {}

        def cand(d, kind):
            key = ('d', d.sem) if d.is_dma else ('e', d.eng, kind == 'war')
            cur = best.get(key)
            if cur is None or (d.val > cur[0].val if d.is_dma else d.idx > cur[0].idx):
                best[key] = (d, kind)

        for b in reads:
            if b.last_w is not None:
                cand(b.last_w, 'raw')
            if b.excl:
                for r in b.reads:
                    cand(r, 'war')
        for b in writes:
            if b.last_w is not None:
                cand(b.last_w, 'waw')
            for r in b.reads:
                cand(r, 'war')
        for (d, kind) in best.values():
            self._dep(ev, d, kind)
        for b in reads:
            if not ev.is_dma:
                b.reads = [r for r in b.reads if r.is_dma or r.eng != ev.eng]
            b.reads.append(ev)
        for b in writes:
            b.last_w = ev
            b.reads = []
        if dma_sem is not None:
            v = self.dma_counts.get(dma_sem, 0) + 16
            self.dma_counts[dma_sem] = v
            ev.sem = dma_sem
            ev.val = v
            if is_out:
                self.out_sems.add(dma_sem)
        st.append(ev)
        return ev

    def emit(self, nc):
        engobj_names = {'pe': 'tensor', 'act': 'scalar', 'dve': 'vector', 'pool': 'gpsimd', 'sp': 'sync'}
        for e in COMPUTE:
            n = 0
            for o in self.streams[e]:
                if o.signal and not o.is_dma:
                    n += 1
                    o.signo = n
        with contextlib.ExitStack() as es:
            esem = {e: es.enter_context(nc.semaphore('c_' + e)) for e in COMPUTE}
            dsem = {k: es.enter_context(nc.semaphore('d_%s' % (k,))) for k in self.dma_counts}
            block = es.enter_context(nc.Block())

            def run(ename):
                def body(eng):
                    for o in self.streams[ename]:
                        for d in o.waits:
                            if isinstance(d, tuple):
                                eng.wait_ge(dsem[d[0]], d[1])
                            else:
                                eng.wait_ge(esem[d.eng], d.signo)
                        ins = o.fn(eng)
                        if o.is_dma:
                            ins.then_inc(dsem[o.sem], 16)
                        elif o.signal:
                            ins.then_inc(esem[o.eng], 1)
                    if ename == 'sp':
                        for k in sorted(self.out_sems):
                            eng.wait_ge(dsem[k], self.dma_counts[k])
                return body

            for e in ENGS:
                getattr(block, engobj_names[e])(run(e))


def _consts():
    half = 128
    inv = (np.float32(10000.0) ** (-(np.arange(half, dtype=np.float32)) / np.float32(half))).astype(np.float32)
    pos = np.concatenate([np.arange(LP, dtype=np.float32),
                          np.tile(PAST + np.arange(LS, dtype=np.float32), NB)]).astype(np.float32)
    ang = (pos[None, :] * inv[:, None]).astype(np.float32)
    cos = np.cos(ang.astype(np.float64)).astype(np.float32)
    sin = np.sin(ang.astype(np.float64)).astype(np.float32)
    lg = np.log1p(-np.exp2(-5.0 - np.arange(4, dtype=np.float32))).astype(np.float32)

    def dec(C):
        idx = np.arange(C, dtype=np.float32)
        diff = idx[:, None] - idx[None, :]
        inner = np.where(diff[None] >= 0, np.exp(lg[:, None, None] * np.maximum(diff, 0.0)[None]), 0.0)
        inner = inner.astype(np.float32)
        qd = np.exp(lg[None, :] * (idx[:, None] + 1.0)).astype(np.float32)
        kd = np.exp(lg[None, :] * (C - 1.0 - idx[:, None])).astype(np.float32)
        cd = np.exp(lg * C).astype(np.float32)
        return inner, qd, kd, cd

    lg64 = lg.astype(np.float64)
    r = np.arange(512, dtype=np.float64)
    j = np.arange(128, dtype=np.float64)
    diff = r[None, :] - j[:, None]
    maskX = np.where(diff[None] >= 0, np.exp(lg64[:, None, None] * np.maximum(diff, 0.0)[None]), 0.0) / 16.0
    qdecX = np.exp(lg64[:, None] * (r[None, :] + 1.0))
    kdecX = np.zeros((128, 16))
    for h in range(4):
        for c in range(4):
            kdecX[:, h * 4 + c] = np.exp(lg64[h] * (511.0 - (c * 128 + j))) / 16.0
    cdX = np.exp(lg64 * 512.0)
    t = np.arange(128)
    bb = t // 8; tt = (t % 8).astype(np.float64)
    same = (bb[:, None] == bb[None, :])
    d8 = tt[None, :] - tt[:, None]
    maskSf = np.where(same[None] & (d8[None] >= 0), np.exp(lg64[:, None, None] * np.maximum(d8, 0.0)[None]), 0.0) / 16.0
    rmd = np.zeros((128, 64))
    for h in range(4):
        for b in range(NB):
            sel = bb == b
            rmd[sel, h * NB + b] = np.exp(lg64[h] * (7.0 - tt[sel])) / 16.0
    qdS = np.exp(lg64[:, None] * (np.arange(LS, dtype=np.float64)[None, :] + 1.0))
    cdS = np.exp(lg64 * LS)
    f32 = lambda a: np.ascontiguousarray(a, dtype=np.float32)
    return dict(cos=cos, sin=sin, maskX=f32(maskX), qdecX=f32(qdecX), kdecX=f32(kdecX), maskSf=f32(maskSf), rmd=f32(rmd),
                qdecS=f32(qdS), ident=np.eye(128, dtype=np.float32)), [float(x) for x in cdX], [float(x) for x in cdS]


def _wblocks():
    blocks = []
    for c in range(8):
        blocks.append([("w_in", 4096 + q * 1024 + c * 128, 128, q * 128) for q in range(4)])
    for j in range(4):
        blocks.append([("w_conv_o", j * 256, 256, 0), ("w_in", 10240 + 1024 + j * 256, 256, 256)])
    for j in range(4):
        blocks.append([("w_mem_kv", j * 512, 512, 0)])
    for h in range(4):
        blocks.append([("w_in", 8192 + h * 256, 256, 0), ("w_in", 9216 + h * 256, 256, 256)])
    for j in range(4):
        blocks.append([("w_mem_o", j * 256, 256, 0), ("w_in", 10240 + 2048 + j * 256, 256, 256)])
    for h in range(4):
        blocks.append([("w_in", h * 256, 256, 0), ("w_in", 1024 + h * 256, 256, 256)])
        blocks.append([("w_in", 2048 + h * 256, 256, 0), ("w_in", 3072 + h * 256, 256, 256)])
    for j in range(4):
        blocks.append([("w_ret_o", j * 256, 256, 0), ("w_in", 10240 + j * 256, 256, 256)])
    for j in range(2):
        blocks.append([("w_out", j * 512, 512, 0)])
    return blocks


def _pack_weights(ws):
    blocks = _wblocks()
    wb = np.empty((len(blocks), 128, 8, 512), dtype=np.float32)
    for i, blk in enumerate(blocks):
        for (name, c0, n, off) in blk:
            wb[i, :, :, off:off + n] = ws[name][:, c0:c0 + n].reshape(8, 128, n).transpose(1, 0, 2)
    return wb.reshape(len(blocks), 128, 4096)


def build_nc(stop=99, dbg=0):
    consts, CDX, CDS = _consts()
    nc = bass.Bass("TRN2", target_bir_lowering=False)

    def din(name, shape):
        return nc.dram_tensor(name, list(shape), F32, kind="ExternalInput").ap()

    def dout(name, shape):
        return nc.dram_tensor(name, list(shape), F32, kind="ExternalOutput").ap()

    xp = din("xp", [LP, D]); xs = din("xs", [NS, D])
    sret = din("sret", [NB, 4, 256, 256]); sconv = din("sconv", [NB * 2, D])
    cmk = din("cmk", [NB, 256, D]); cmv = din("cmv", [NB, 256, D]); memp = din("memp", [256, D])
    norm_g = din("norm_g", [D]); b_gate = din("b_gate", [3072])
    wb = din("wb", [len(_wblocks()), 128, 4096])
    ret_norm_g = din("ret_norm_g", [1024]); conv_w = din("conv_w", [3, D]); conv_b = din("conv_b", [D])
    mem_norm_g = din("mem_norm_g", [D])
    final_norm_g = din("final_norm_g", [D])
    c_cos = din("c_cos", [128, NT]); c_sin = din("c_sin", [128, NT])
    c_maskX = din("c_maskX", [4, 128, 512]); c_qdecX = din("c_qdecX", [4, 512]); c_kdecX = din("c_kdecX", [128, 16])
    c_maskSf = din("c_maskSf", [4, 128, 128]); c_rmd = din("c_rmd", [128, 64]); c_qdecS = din("c_qdecS", [4, 8])
    c_ident = din("c_ident", [128, 128])

    yp = dout("yp", [LP, D]); ys = dout("ys", [NS, D])
    nrp = dout("nrp", [4, 256, 256]); nrs = dout("nrs", [NB, 4, 256, 256])
    ncp = dout("ncp", [2, D]); ncs = dout("ncs", [NB * 2, D])
    nmk = dout("nmk", [256, D]); nmv = dout("nmv", [256, D])

    P = Plan()
    with contextlib.ExitStack() as es:
        def sb(name, shape, dt=F32):
            return es.enter_context(nc.sbuf_tensor(name, list(shape), dt))

        def psb(name, shape, dt=F32):
            return es.enter_context(nc.psum_tensor(name, list(shape), dt))

        hT = sb("hT", [128, 8, NT], BF16)
        ob = sb("ob", [128, 8, NT], BF16)
        mgd = sb("mgd", [128, 8, NT], BF16)
        B_hTa = Buf('hTa'); B_hTb = Buf('hTb'); B_hT = [B_hTa, B_hTb]
        B_ob = [[Buf('ob') for _ in GROUPS] for _ in range(8)]
        B_mg = [[Buf('mg') for _ in GROUPS] for _ in range(8)]
        B_mgt = [Buf('mgt') for _ in range(NTILE)]
        NSLOT = 4
        wsl = [sb("w%d" % i, [128, 8, 512], BF16) for i in range(NSLOT)]
        B_w = [Buf('w%d' % i) for i in range(NSLOT)]
        gaux = sb("gaux", [128, D])
        maskT = sb("maskT", [128, 4, 128]); qdec = sb("qdec", [128, 4, 128])
        qdecS = sb("qdecS", [128, 4, 8])
        idf = sb("idf", [128, 128]); idb = sb("idb", [128, 128], BF16); ones = sb("ones", [128, 128], BF16)
        mhalf = sb("mhalf", [128, 8])
        epsb = sb("epsb", [128, 2])
        epsb4 = sb("epsb4", [128, 2])
        svec = sb("svec", [128, 64]); hbg = sb("hbg", [128, 24])
        bg = svec[:, 0:24]; cbias = svec[:, 24:32]; gretT = svec[:, 32:40]
        cw = svec[:, 40:64].rearrange("p (t c) -> p t c", t=3)
        B_c = Buf('consts')
        B_gaux = Buf('gaux')
        psf = [psb("psf%d" % i, [128, 512], F32) for i in range(6)]
        psh = [psb("psh%d" % i, [128, 1024], BF16) for i in range(2)]
        B_psf = [Buf('psf%d' % i, True) for i in range(6)]
        B_psh = [Buf('psh%d' % i, True) for i in range(2)]
        free_f = list(range(6))
        free_h = list(range(2))

        def falloc():
            return free_f.pop(0)

        def ffree(i):
            free_f.append(i)

        def halloc():
            return free_h.pop(0)

        def hfree(i):
            free_h.append(i)

        class Arena:
            def __init__(self, t, n, rs):
                self.t = t
                self.rs = rs
                self.bufs = [Buf('ar') for _ in range((n + rs - 1) // rs)]

            def view(self, off, size, parts=128):
                return self.t[0:parts, off:off + size], self.bufs[off // self.rs:(off + size - 1) // self.rs + 1]

        AF_N = 4096
        AH_N = 8192
        arF = Arena(sb("arF", [128, AF_N]), AF_N, 256)
        arH = Arena(sb("arH", [128, AH_N], BF16), AH_N, 512)
        xbuf = [sb("xbuf%d" % i, [128, D]) for i in range(2)]
        B_x = [[Buf('x0a'), Buf('x0b')], [Buf('x1a'), Buf('x1b')]]
        hb = [sb("hb%d" % i, [128, D], BF16) for i in range(2)]
        B_hb = [Buf('hb0'), Buf('hb1')]
        scr = None
        ssb = sb("ssb", [128, 64])
        stat_i = [0]

        def stat():
            i = stat_i[0] % 8
            stat_i[0] += 1
            return i

        B_st = [Buf('st%d' % i) for i in range(8)]
        stg = [sb("stg%d" % i, [128, 512]) for i in range(2)]
        B_stg = [Buf('stg0'), Buf('stg1')]
        scr = stg[0][:, 0:128].bitcast(BF16)
        B_scr = B_stg[0]
        NTF = 4
        tf = [sb("tf%d" % i, [128, 512])[:] for i in range(NTF)]
        B_tf = [Buf('tf%d' % i) for i in range(NTF)]
        tf += [xbuf[1][:, 0:512], xbuf[1][:, 512:1024], xbuf[0][:, 0:512], xbuf[0][:, 512:1024],
               hb[0][:].bitcast(F32), hb[1][:].bitcast(F32)]
        B_tf += [B_x[1][0], B_x[1][1], B_x[0][0], B_x[0][1], B_hb[0], B_hb[1]]
        tf_i = [0]
        tf_n = [NTF]

        def tfa():
            i = tf_i[0] % tf_n[0]
            tf_i[0] += 1
            return i

        NTH = 4
        th_ = [sb("th%d" % i, [128, 1024], BF16) for i in range(NTH)]
        B_th = [Buf('th%d' % i) for i in range(NTH)]
        th_i = [0]

        def tha():
            i = th_i[0] % NTH
            th_i[0] += 1
            return i

        def sl(g):
            return slice(GROUPS[g][0], GROUPS[g][0] + GROUPS[g][1])

        def ld(dst, src, sem, extra=()):
            P.op('sp', lambda e: e.dma_start(out=dst, in_=src), writes=[B_c] + list(extra), dma_sem=sem)

        ld(qdecS[:].rearrange("p h i -> p (h i)"), c_qdecS.rearrange("h i -> (h i)").partition_broadcast(128), 'c0')
        ld(idf[:], c_ident, 'c0')
        svt = tf[0][0:64, 0:128]
        ld(svt[0:24, :], b_gate.rearrange("(j p) -> j p", p=128), 'c0', [B_tf[0]])
        ld(svt[24:32, :], conv_b.rearrange("(c p) -> c p", p=128), 'c0', [B_tf[0]])
        ld(svt[32:40, :], ret_norm_g.rearrange("(c p) -> c p", p=128), 'c0', [B_tf[0]])
        ld(svt[40:64, :], conv_w.rearrange("t (c p) -> (t c) p", p=128), 'c0', [B_tf[0]])
        if dbg != 1:
            P.op('pe', lambda e: e.matmul(psf[0][:, 0:64], lhsT=svt, rhs=idf[0:64, 0:64], start=True, stop=True), reads=[B_c, B_tf[0]], writes=[B_psf[0]])
            P.op('dve', lambda e: e.tensor_copy(out=svec[:], in_=psf[0][:, 0:64]), reads=[B_psf[0]], writes=[B_c])
        P.op('dve', lambda e: e.tensor_copy(out=idb[:], in_=idf[:]), reads=[B_c], writes=[B_c])
        P.op('pool', lambda e: e.memset(ones[:], 1.0), writes=[B_c])
        P.op('pool', lambda e: e.memset(mhalf[:], -0.5), writes=[B_c])
        P.op('pool', lambda e: e.memset(epsb[:], EPS), writes=[B_c])
        P.op('pool', lambda e: e.memset(epsb4[:], 4.0 * EPS), writes=[B_c])
        P.op('dve', lambda e: e.tensor_scalar(out=hbg[:], in0=bg, scalar1=0.5, scalar2=None, op0=ALU.mult),
             reads=[B_c], writes=[B_c])
        RC = [B_c]

        blocks = _wblocks()
        wstate = {'issued': 0, 'next': 0}

        def wissue(upto, after=()):
            while wstate['issued'] < min(upto, len(blocks)):
                i = wstate['issued']
                s = i % NSLOT
                P.op('pool', lambda e, i=i, s=s: e.dma_start(out=wsl[s][:].rearrange("p k n -> p (k n)"), in_=wb[i]),
                     reads=list(after), writes=[B_w[s]], dma_sem='w%d' % s)
                wstate['issued'] += 1

        def wget():
            i = wstate['next']
            wstate['next'] += 1
            assert i < wstate['done'] + NSLOT
            wissue(i + 1)
            s = i % NSLOT
            return wsl[s], B_w[s]

        def wdone():
            wstate['done'] += 1
            wissue(wstate['done'] + NSLOT)

        wstate['done'] = 0
        wissue(1)

        def mm(out, lhsT, rhs, start, stop, reads, wbuf):
            P.op('pe', lambda e: e.matmul(out, lhsT=lhsT, rhs=rhs, start=start, stop=stop), reads=reads, writes=[wbuf])

        def tr(out, in_, ident, reads, wbuf):
            P.op('pe', lambda e: e.transpose(out=out, in_=in_, identity=ident), reads=reads + RC, writes=[wbuf])

        def rstd_of(sumsq_ap, out_ap, n, bufs, pre=1.0):
            s = 1.0 / (pre * pre)
            P.op('act', lambda e: e.activation(out=out_ap, in_=sumsq_ap, func=AF.Sqrt, bias=epsb[:, 0:1], scale=s / n),
                 reads=bufs + RC, writes=bufs)
            P.op('dve', lambda e: e.reciprocal(out=out_ap, in_=out_ap), reads=bufs, writes=bufs)

        def norm_tile(src_ap, g_tile, gbuf, xi, nparts=128):
            st = stat()
            sa = ssb[:, st * 4:st * 4 + 1]
            sb_ = ssb[:, st * 4 + 1:st * 4 + 2]
            P.op('act', lambda e: e.activation(out=hb[xi][:], in_=xbuf[xi][:], func=AF.Square, accum_out=sa),
                 reads=[B_x[xi]], writes=[B_hb[xi], B_st[st]])
            rstd_of(sa, sb_, D, [B_st[st]])
            P.op('dve', lambda e: e.scalar_tensor_tensor(out=hb[xi][:], in0=xbuf[xi][:], scalar=sb_, in1=g_tile[:],
                                                         op0=ALU.mult, op1=ALU.mult),
                 reads=[B_x[xi], B_st[st], gbuf], writes=[B_hb[xi]])

        def transpose8(xi, dst3, dbufs, dbufs2=None):
            hi = halloc()
            for k in range(8):
                tr(psh[hi][:, k * 128:(k + 1) * 128], hb[xi][:, k * 128:(k + 1) * 128], idb[:], [B_hb[xi]], B_psh[hi])
            src3 = psh[hi][:].rearrange("p (k t) -> p k t", k=8)
            P.op('act', lambda e: e.copy(out=dst3[:, 0:4, :], in_=src3[:, 0:4, :]), reads=[B_psh[hi]], writes=dbufs)
            P.op('dve', lambda e: e.tensor_copy(out=dst3[:, 4:8, :], in_=src3[:, 4:8, :]), reads=[B_psh[hi]],
                 writes=dbufs if dbufs2 is None else dbufs2)
            hfree(hi)

        if stop < 2:
            P.disabled = True
        def xtile_src(t):
            return xp[t * 128:(t + 1) * 128, :] if t < 16 else xs

        P.op('sp', lambda e: e.dma_start(out=gaux[:], in_=norm_g.partition_broadcast(128)), writes=[B_gaux], dma_sem='gaux')
        P.tag = 'p2'
        for i in range(4):
            _v, _b = arF.view(i * 1024, 1024); xbuf.append(_v); B_x.append(_b)
            _v, _b = arH.view(i * 1024, 1024); hb.append(_v); B_hb.append(_b)
        NXR = 6
        SKEW = 3
        for t in range(NTILE + SKEW):
            if t < NTILE:
                xi = t % NXR
                P.op('sp', lambda e, t=t, xi=xi: e.dma_start(out=xbuf[xi][:], in_=xtile_src(t)),
                     writes=[B_x[xi]], dma_sem='x%d' % xi)
                norm_tile(None, gaux, B_gaux, xi)
            if t >= SKEW:
                tt = t - SKEW
                transpose8(tt % NXR, hT[:, :, tt * 128:(tt + 1) * 128], [B_hTa], [B_hTb])
        wissue(NSLOT, after=B_hT)

        def project_branch(bidx, first):
            P.tag = 'proj%d' % bidx
            for j in range(4):
                Wo, BWo = wget()
                Wg, BWg = Wo, BWo
                for c2 in range(2):
                    c = c2
                    fc = j * 2 + c2
                    for g in range(5):
                        n = GROUPS[g][1]
                        fy = falloc()
                        for k in range(8):
                            mm(psf[fy][:, 0:n], Wo[:, k, c * 128:(c + 1) * 128], ob[:, k, sl(g)], k == 0, k == 7,
                               [B_ob[k][g], BWo], B_psf[fy])
                        fg = falloc()
                        for k in range(8):
                            mm(psf[fg][:, 0:n], Wg[:, k, 256 + c * 128:256 + (c + 1) * 128], hT[:, k, sl(g)], k == 0, k == 7,
                               [B_hT, BWg], B_psf[fg])
                        ti = tfa()
                        bcol = bidx * 8 + fc
                        P.op('act', lambda e, fg=fg, ti=ti, n=n, bcol=bcol: e.activation(
                            out=tf[ti][:, 0:n], in_=psf[fg][:, 0:n], func=AF.Tanh, bias=hbg[:, bcol:bcol + 1], scale=0.5),
                            reads=[B_psf[fg]] + RC, writes=[B_tf[ti]])
                        ffree(fg)
                        if first:
                            P.op('dve', lambda e, fy=fy, ti=ti, n=n, fc=fc, g=g: e.scalar_tensor_tensor(
                                out=mgd[:, fc, sl(g)], in0=tf[ti][:, 0:n], scalar=1.0, in1=psf[fy][:, 0:n],
                                op0=ALU.add, op1=ALU.mult),
                                reads=[B_tf[ti], B_psf[fy]], writes=[B_mg[fc][g]])
                        else:
                            t2 = tfa()
                            P.op('dve', lambda e, fy=fy, ti=ti, t2=t2, n=n: e.scalar_tensor_tensor(
                                out=tf[t2][:, 0:n], in0=tf[ti][:, 0:n], scalar=1.0, in1=psf[fy][:, 0:n],
                                op0=ALU.add, op1=ALU.mult),
                                reads=[B_tf[ti], B_psf[fy]], writes=[B_tf[t2]])
                            P.op('pool', lambda e, t2=t2, n=n, fc=fc, g=g: e.tensor_tensor(
                                out=mgd[:, fc, sl(g)], in0=tf[t2][:, 0:n], in1=mgd[:, fc, sl(g)], op=ALU.add),
                                reads=[B_tf[t2], B_mg[fc][g]], writes=[B_mg[fc][g]])
                        ffree(fy)
                wdone()

        if stop < 3:
            P.disabled = True
        tf_n[0] = NTF + 6
        _v, BL_scT = arF.view(0, 256); scT = _v.rearrange("p (c r) -> p c r", c=8)
        sctm, BL_sctm = arF.view(256, 1024, parts=32)
        pre = []; B_pre = []
        for i in range(2):
            _v, _b = arF.view(1280 + i * 768, 514); pre.append(_v); B_pre.append(_b)
        _v, BL_preS = arF.view(2816, 160); preS = _v.rearrange("p (b r) -> p b r", r=10)
        nct = []; B_nct = []
        for _off in (3072, 256):
            _v, _b = arF.view(_off, 1024, parts=16); nct.append(_v); B_nct.append(_b)
        P.op('sp', lambda e: e.dma_start(out=sctm[:], in_=sconv), writes=BL_sctm, dma_sem='sctm')
        fi = falloc()
        for c in range(8):
            P.op('pe', lambda e, c=c, fi=fi: e.matmul(psf[fi][:, c * 32:(c + 1) * 32], lhsT=sctm[:, c * 128:(c + 1) * 128],
                                                     rhs=idf[0:32, 0:32], start=True, stop=True),
                 reads=BL_sctm + RC, writes=[B_psf[fi]])
        P.op('dve', lambda e, fi=fi: e.tensor_copy(out=scT[:].rearrange("p c r -> p (c r)"), in_=psf[fi][:, 0:256]),
             reads=[B_psf[fi]], writes=BL_scT)
        ffree(fi)

        for c in range(8):
            W, BW = wget()
            P.tag = 'conv'
            for g in range(5):
                n = GROUPS[g][1]
                fcu = falloc(); fcc = falloc(); fcb = falloc(); fcg = falloc()
                for (fx, off) in ((fcu, 0), (fcc, 256), (fcb, 128), (fcg, 384)):
                    for k in range(8):
                        mm(psf[fx][:, 0:n], W[:, k, off:off + 128], hT[:, k, sl(g)], k == 0, k == 7, [B_hT, BW], B_psf[fx])
                t_cu = tfa()
                P.op('act', lambda e, fcu=fcu, t_cu=t_cu, n=n: e.copy(out=tf[t_cu][:, 0:n], in_=psf[fcu][:, 0:n]),
                     reads=[B_psf[fcu]], writes=[B_tf[t_cu]])
                ffree(fcu)
                if g < 4:
                    pi = g % 2
                    pb, Bp = pre[pi], B_pre[pi]
                    if g == 0:
                        P.op('pool', lambda e, pb=pb: e.memset(pb[:, 0:2], 0.0), writes=Bp)
                    else:
                        po = pre[1 - pi]
                        P.op('pool', lambda e, pb=pb, po=po: e.tensor_copy(out=pb[:, 0:2], in_=po[:, 512:514]),
                             reads=B_pre[1 - pi], writes=Bp)
                    cur = pb[:, 2:514]; m1 = pb[:, 1:513]; m2 = pb[:, 0:512]

                    def v3(a):
                        return a
                else:
                    pb, Bp = preS, BL_preS
                    P.op('pool', lambda e, c=c: e.tensor_copy(out=preS[:, :, 0:2],
                                                              in_=scT[:, c, :].rearrange("p (b r) -> p b r", r=2)),
                         reads=BL_scT, writes=Bp)
                    cur = pb[:, :, 2:10]; m1 = pb[:, :, 1:9]; m2 = pb[:, :, 0:8]

                    def v3(a):
                        return a.rearrange("p (b t) -> p b t", t=8)
                P.op('dve', lambda e, fcc=fcc, t_cu=t_cu, n=n, cur=cur, v3=v3: e.tensor_tensor(
                    out=cur, in0=v3(psf[fcc][:, 0:n]), in1=v3(tf[t_cu][:, 0:n]), op=ALU.mult),
                    reads=[B_psf[fcc], B_tf[t_cu]], writes=Bp)
                ffree(fcc)
                t_cv = tfa()
                P.op('act', lambda e, t_cv=t_cv, cur=cur, n=n, c=c, v3=v3: e.activation(
                    out=v3(tf[t_cv][:, 0:n]), in_=cur, func=AF.Identity, bias=cbias[:, c:c + 1], scale=cw[:, 2, c:c + 1]),
                    reads=Bp + RC, writes=[B_tf[t_cv]])
                P.op('dve', lambda e, t_cv=t_cv, m1=m1, n=n, c=c, v3=v3: e.scalar_tensor_tensor(
                    out=v3(tf[t_cv][:, 0:n]), in0=m1, scalar=cw[:, 1, c:c + 1], in1=v3(tf[t_cv][:, 0:n]),
                    op0=ALU.mult, op1=ALU.add), reads=Bp + [B_tf[t_cv]] + RC, writes=[B_tf[t_cv]])
                P.op('dve', lambda e, t_cv=t_cv, m2=m2, n=n, c=c, v3=v3: e.scalar_tensor_tensor(
                    out=v3(tf[t_cv][:, 0:n]), in0=m2, scalar=cw[:, 0, c:c + 1], in1=v3(tf[t_cv][:, 0:n]),
                    op0=ALU.mult, op1=ALU.add), reads=Bp + [B_tf[t_cv]] + RC, writes=[B_tf[t_cv]])
                t_th = tfa()
                P.op('act', lambda e, fcg=fcg, t_th=t_th, n=n: e.activation(
                    out=tf[t_th][:, 0:n], in_=psf[fcg][:, 0:n], func=AF.Silu),
                    reads=[B_psf[fcg]], writes=[B_tf[t_th]])
                ffree(fcg)
                P.op('dve', lambda e, fcb=fcb, t_cv=t_cv, n=n: e.tensor_tensor(
                    out=tf[t_cv][:, 0:n], in0=psf[fcb][:, 0:n], in1=tf[t_cv][:, 0:n], op=ALU.mult),
                    reads=[B_psf[fcb], B_tf[t_cv]], writes=[B_tf[t_cv]])
                ffree(fcb)
                P.op('pool', lambda e, t_cv=t_cv, t_th=t_th, n=n, c=c, g=g: e.tensor_tensor(
                    out=ob[:, c, sl(g)], in0=tf[t_cv][:, 0:n], in1=tf[t_th][:, 0:n], op=ALU.mult),
                    reads=[B_tf[t_cv], B_tf[t_th]], writes=[B_ob[c][g]])
            for (ntok, src2d, rdb, dst, Bd) in (
                    (2, pre[1][:, 512:514], B_pre[1], None, None),
                    (16, preS[:, :, 8], BL_preS, nct[0], B_nct[0]),
                    (16, preS[:, :, 9], BL_preS, nct[1], B_nct[1])):
                f1 = falloc()
                P.op('pe', lambda e, f1=f1, ntok=ntok, src2d=src2d: e.matmul(
                    psf[f1][0:ntok, 0:128], lhsT=src2d, rhs=idf[:, :], start=True, stop=True),
                    reads=[rdb] + RC, writes=[B_psf[f1]])
                if dst is None:
                    dsl = stg[c // 4][0:2, (c % 4) * 128:(c % 4 + 1) * 128]
                    Bd = [B_stg[c // 4]]
                else:
                    dsl = dst[0:ntok, c * 128:(c + 1) * 128]
                P.op('act', lambda e, f1=f1, ntok=ntok, dsl=dsl: e.copy(out=dsl, in_=psf[f1][0:ntok, 0:128]),
                     reads=[B_psf[f1]], writes=Bd)
                ffree(f1)
            wdone()
        for hf in range(2):
            P.op('sp', lambda e, hf=hf: e.dma_start(out=ncp[:, hf * 512:(hf + 1) * 512], in_=stg[hf][0:2, :]), reads=[B_stg[hf]],
                 dma_sem='o_stg%d' % hf, is_out=True)
        for t2_ in range(2):
            P.op('sp', lambda e, t2_=t2_: e.dma_start(out=ncs.rearrange("(b t) d -> t b d", t=2)[t2_], in_=nct[t2_]), reads=B_nct[t2_],
                 dma_sem='o_nc%d' % t2_, is_out=True)
        if stop < 4:
            P.disabled = True
        project_branch(1, True)

        if stop < 1:
            P.disabled = True
        P.tag = 'p1'
        _v, BL_memT = arH.view(0, 2048); memT = _v.rearrange("p (k m) -> p k m", k=8)
        _v, BL_KT = arH.view(2048, 2048); KT = _v.rearrange("p (k m) -> p k m", k=8)
        _v, BL_Vp = arH.view(4096, 2048); Vp = _v.rearrange("p (t d) -> p t d", t=2)
        P.op('sp', lambda e: e.dma_start(out=gaux[:], in_=mem_norm_g.partition_broadcast(128)), writes=[B_gaux], dma_sem='gaux')
        for t in range(2):
            P.op('sp', lambda e, t=t: e.dma_start(out=xbuf[t][:], in_=memp[t * 128:(t + 1) * 128, :]),
                 writes=[B_x[t]], dma_sem='x%d' % t)
            norm_tile(None, gaux, B_gaux, t)
            transpose8(t, memT[:, :, t * 128:(t + 1) * 128], BL_memT)
        for j in range(4):
            W, BW = wget()
            isK = j < 2
            for t in range(2):
                fi = falloc()
                for k in range(8):
                    mm(psf[fi][:], memT[:, k, t * 128:(t + 1) * 128], W[:, k, :], k == 0, k == 7, BL_memT + [BW], B_psf[fi])
                si = (j * 2 + t) % 2
                P.op('act', lambda e, fi=fi, si=si: e.copy(out=stg[si][:, 0:512], in_=psf[fi][:]),
                     reads=[B_psf[fi]], writes=[B_stg[si]])
                if not isK:
                    P.op('dve', lambda e, fi=fi, t=t, j=j: e.tensor_copy(out=Vp[:, t, (j - 2) * 512:(j - 1) * 512], in_=psf[fi][:]),
                         reads=[B_psf[fi]], writes=BL_Vp)
                ffree(fi)
                dst = (nmk if isK else nmv)[t * 128:(t + 1) * 128, (j % 2) * 512:(j % 2) * 512 + 512]
                P.op('sp', lambda e, dst=dst, si=si: e.dma_start(out=dst, in_=stg[si][:, 0:512]),
                     reads=[B_stg[si]], dma_sem='o_stg%d' % si, is_out=True)
            if isK:
                fi = falloc()
                for c in range(4):
                    if c == 2:
                        pass
                    half = c % 2
                    if c == 2:
                        P.op('dve', lambda e, fi=fi, j=j: e.tensor_copy(
                            out=KT[:, j * 4:j * 4 + 2, :], in_=psf[fi][:].rearrange("p (c m) -> p c m", c=2)),
                            reads=[B_psf[fi]], writes=BL_KT)
                        ffree(fi)
                        fi = falloc()
                    for k in range(8):
                        mm(psf[fi][:, half * 256:half * 256 + 256], W[:, k, c * 128:(c + 1) * 128], memT[:, k, :],
                           k == 0, k == 7, BL_memT + [BW], B_psf[fi])
                P.op('dve', lambda e, fi=fi, j=j: e.tensor_copy(
                    out=KT[:, j * 4 + 2:j * 4 + 4, :], in_=psf[fi][:].rearrange("p (c m) -> p c m", c=2)),
                    reads=[B_psf[fi]], writes=BL_KT)
                ffree(fi)
            wdone()

        if stop < 5:
            P.disabled = True
        kvs = []; B_kvs = []; kts = []; B_kts = []
        for _off in (6144, 7168, 1024):
            _v, _b = arH.view(_off, 1024); kvs.append(_v.rearrange("p (b c d) -> p b c d", b=2, c=2)); B_kvs.append(_b)

        kv_next = [0]

        def kv_fill(upto):
            while kv_next[0] < min(upto, 64):
                G = kv_next[0]
                kv_next[0] += 1
                hh, L = G // 16, G % 16
                src_t = cmk if L < 8 else cmv
                b0 = (L % 8) * 2
                src = src_t[b0:b0 + 2, :, hh * 256:(hh + 1) * 256].rearrange("b (c p) d -> p b c d", p=128)
                ri = G % 3
                P.op('pool', lambda e, ri=ri, src=src: e.dma_start(out=kvs[ri], in_=src), writes=B_kvs[ri], dma_sem='kv%d' % ri)

        for i in range(2):
            _v, _b = arH.view(i * 512, 512); kts.append(_v.rearrange("p (c m) -> p c m", c=2)); B_kts.append(_b)
        kv_fill(3)
        for h in range(4):
            W, BW = wget()
            def do_mq(g, W=W, BW=BW):
                n = GROUPS[g][1]
                qi = tha()
                for dch in range(2):
                    fi = falloc()
                    for k in range(8):
                        mm(psf[fi][:, 0:n], W[:, k, dch * 128:(dch + 1) * 128], hT[:, k, sl(g)], k == 0, k == 7, [B_hT, BW], B_psf[fi])
                    P.op('act', lambda e, fi=fi, qi=qi, dch=dch, n=n: e.copy(out=th_[qi][:, dch * 512:dch * 512 + n], in_=psf[fi][:, 0:n]),
                         reads=[B_psf[fi]], writes=[B_th[qi]])
                    ffree(fi)
                return qi

            qnext = None
            for g in range(5):
                P.tag = 'mem' if g < 4 else 'mem_s'
                n = GROUPS[g][1]
                qi = qnext if qnext is not None else do_mq(g)
                qnext = None
                sgi = [tfa(), tfa()]
                for ech in range(2):
                    fi = falloc()
                    for k in range(8):
                        mm(psf[fi][:, 0:n], W[:, k, 256 + ech * 128:256 + (ech + 1) * 128], hT[:, k, sl(g)], k == 0, k == 7,
                           [B_hT, BW], B_psf[fi])
                    ti = sgi[ech]
                    P.op('act', lambda e, fi=fi, ti=ti, n=n: e.activation(out=tf[ti][:, 0:n], in_=psf[fi][:, 0:n], func=AF.Tanh, scale=0.5),
                         reads=[B_psf[fi]], writes=[B_tf[ti]])
                    P.op('dve', lambda e, fi=fi, ti=ti, n=n: e.scalar_tensor_tensor(
                        out=tf[ti][:, 0:n], in0=tf[ti][:, 0:n], scalar=1.0, in1=psf[fi][:, 0:n], op0=ALU.add, op1=ALU.mult),
                        reads=[B_psf[fi], B_tf[ti]], writes=[B_tf[ti]])
                    ffree(fi)
                pi = tha()
                fden = falloc()
                fom = [falloc(), falloc()]
                if g < 4:
                    for mch in range(2):
                        fi = falloc()
                        for dch in range(2):
                            mm(psf[fi][:, 0:n], KT[:, h * 2 + dch, mch * 128:(mch + 1) * 128], th_[qi][:, dch * 512:dch * 512 + n],
                               dch == 0, dch == 1, BL_KT + [B_th[qi]], B_psf[fi])
                        P.op('act', lambda e, fi=fi, pi=pi, mch=mch, n=n: e.activation(
                            out=th_[pi][:, mch * 512:mch * 512 + n], in_=psf[fi][:, 0:n], func=AF.Exp, scale=1.0 / 16),
                            reads=[B_psf[fi]], writes=[B_th[pi]])
                        ffree(fi)
                    P.tag = 'mem' if g + 1 < 4 else 'mem_s'
                    qnext = do_mq(g + 1)
                    P.tag = 'mem'
                    for mch in range(2):
                        mm(psf[fden][:, 0:n], ones[:], th_[pi][:, mch * 512:mch * 512 + n], mch == 0, mch == 1, [B_th[pi]] + RC, B_psf[fden])
                    for ech in range(2):
                        for mch in range(2):
                            mm(psf[fom[ech]][:, 0:n], Vp[:, mch, h * 256 + ech * 128:h * 256 + (ech + 1) * 128],
                               th_[pi][:, mch * 512:mch * 512 + n], mch == 0, mch == 1, BL_Vp + [B_th[pi]], B_psf[fom[ech]])
                else:
                    fsc = falloc()
                    for b in range(NB + 1):
                        if b < NB:
                            if b % 2 == 0:
                                kv_fill(h * 16 + b // 2 + 3)
                            ri = (h * 16 + b // 2) % 3
                            kb = b % 2
                            bi = b % 2
                            hi = halloc()
                            for mch in range(2):
                                for dch in range(2):
                                    tr(psh[hi][:, dch * 256 + mch * 128:dch * 256 + (mch + 1) * 128],
                                       kvs[ri][:, kb, mch, dch * 128:(dch + 1) * 128], idb[:], B_kvs[ri], B_psh[hi])
                            P.op('dve', lambda e, hi=hi, bi=bi: e.tensor_copy(
                                out=kts[bi][:].rearrange("p c m -> p (c m)"), in_=psh[hi][:, 0:512]),
                                reads=[B_psh[hi]], writes=B_kts[bi])
                            hfree(hi)
                        if b >= 1:
                            bb = b - 1
                            bi = bb % 2
                            for mch in range(2):
                                for dch in range(2):
                                    mm(psf[fsc][:, mch * 128 + bb * 8:mch * 128 + bb * 8 + 8], kts[bi][:, dch, mch * 128:(mch + 1) * 128],
                                       th_[qi][:, dch * 512 + bb * 8:dch * 512 + bb * 8 + 8], dch == 0, dch == 1,
                                       B_kts[bi] + [B_th[qi]], B_psf[fsc])
                    for mch in range(2):
                        P.op('act', lambda e, fsc=fsc, pi=pi, mch=mch: e.activation(
                            out=th_[pi][:, mch * 512:mch * 512 + 128], in_=psf[fsc][:, mch * 128:(mch + 1) * 128], func=AF.Exp, scale=1.0 / 16),
                            reads=[B_psf[fsc]], writes=[B_th[pi]])
                    ffree(fsc)
                    for mch in range(2):
                        mm(psf[fden][:, 0:n], ones[:], th_[pi][:, mch * 512:mch * 512 + n], mch == 0, mch == 1, [B_th[pi]] + RC, B_psf[fden])
                    for b in range(NB):
                        if b % 2 == 0:
                            kv_fill(h * 16 + 8 + b // 2 + 3)
                        ri = (h * 16 + 8 + b // 2) % 3
                        kb = b % 2
                        for ech in range(2):
                            for mch in range(2):
                                mm(psf[fom[ech]][:, b * 8:b * 8 + 8], kvs[ri][:, kb, mch, ech * 128:(ech + 1) * 128],
                                   th_[pi][:, mch * 512 + b * 8:mch * 512 + b * 8 + 8], mch == 0, mch == 1,
                                   B_kvs[ri] + [B_th[pi]], B_psf[fom[ech]])
                    kv_fill(h * 16 + 16 + 3)
                    wdone()
                tr_ = tfa()
                P.op('dve', lambda e, fden=fden, tr_=tr_, n=n: e.reciprocal(out=tf[tr_][:, 0:n], in_=psf[fden][:, 0:n]),
                     reads=[B_psf[fden]], writes=[B_tf[tr_]])
                ffree(fden)
                for ech in range(2):
                    ti = sgi[ech]
                    P.op('dve', lambda e, ech=ech, tr_=tr_, ti=ti, n=n, fo=fom[ech]: e.scalar_tensor_tensor(
                        out=tf[ti][:, 0:n], in0=psf[fo][:, 0:n], scalar=0.5, in1=tf[ti][:, 0:n], op0=ALU.mult, op1=ALU.mult),
                        reads=[B_psf[fom[ech]], B_tf[ti]], writes=[B_tf[ti]])
                    ffree(fom[ech])
                    P.op('pool', lambda e, ech=ech, tr_=tr_, ti=ti, n=n, h=h, g=g: e.tensor_tensor(
                        out=ob[:, h * 2 + ech, sl(g)], in0=tf[ti][:, 0:n], in1=tf[tr_][:, 0:n], op=ALU.mult),
                        reads=[B_tf[ti], B_tf[tr_]], writes=[B_ob[h * 2 + ech][g]])
        if stop < 6:
            P.disabled = True
        project_branch(2, False)

        if stop < 7:
            P.disabled = True
        tf_n[0] = NTF + 2
        P.op('sp', lambda e: e.dma_start(out=gaux[:], in_=ret_norm_g.partition_broadcast(128)), writes=[B_gaux], dma_sem='gaux')
        gret = gaux
        _v, BL_S32 = arF.view(0, 512); S32 = _v.rearrange("p (c e) -> p c e", c=2)
        Ss32 = []; B_Ss32 = []; csb = []; B_csb = []
        for i in range(2):
            _v, _b = arF.view(1536 + i * 1024, 1024); csb.append(_v); B_csb.append(_b)
        for _off in (512, 1024, 2560, 3072):
            _v, _b = arF.view(_off, 512); Ss32.append(_v.rearrange("p (c e) -> p c e", c=2)); B_Ss32.append(_b)
        maskSf, BL_maskSf = arF.view(3584, 128)
        rmd, BL_rmd = arF.view(3840, 64)
        kdecX, BL_kdecX = arF.view(3904, 16)
        maskX = maskT[:].rearrange("p h i -> p (h i)")
        qdecX = qdec[:].rearrange("p h i -> p (h i)")
        B_mx = Buf('maskX'); B_qx = Buf('qdecX'); ht_done = set(); cs0_done = set()
        P.op('sp', lambda e: e.dma_start(out=rmd, in_=c_rmd), writes=BL_rmd, dma_sem='hc')
        P.op('sp', lambda e: e.dma_start(out=kdecX, in_=c_kdecX), writes=BL_kdecX, dma_sem='hc')
        gsg1 = xbuf[0][:].rearrange("p (c e) -> p c e", c=4)
        B_gsg1 = [B_x[0]]
        Sbf = []; B_Sbf = []; Ssbf = []; B_Ssbf = []; stmX = []; B_stmX = []
        for i in range(2):
            _v, _b = arH.view(i * 512, 512); Sbf.append(_v.rearrange("p (c e) -> p c e", c=2)); B_Sbf.append(_b)
            _v, _b = arH.view(1024 + i * 512, 512); Ssbf.append(_v.rearrange("p (c e) -> p c e", c=2)); B_Ssbf.append(_b)
            _v, _b = arH.view(2048 + i * 1536, 1280); stmX.append(_v); B_stmX.append(_b)
        _v, BL_kdm = arH.view(2048, 4096); kdm = _v.rearrange("p (b d) -> p b d", b=NB)
        stmS, BL_stmS = arH.view(6144, 128)
        vtmS, BL_vtmS = arH.view(6272, 256)
        _v, _b = arH.view(6656, 512); Ssbf.append(_v.rearrange("p (c e) -> p c e", c=2)); B_Ssbf.append(_b)
        vtm1 = hb[1][:].rearrange("p (c e) -> p c e", c=4); B_vtm1 = [B_hb[1]]
        _v, BL_kdt = arH.view(7168, 1024); kdt1 = _v.rearrange("p (c e) -> p c e", c=4)
        onb1 = hb[0][:].rearrange("p (c e) -> p c e", c=4); B_onb1 = [B_hb[0]]
        SOFF = [0, 512, 896, 1152]
        stgS = [(stg[0][:, 0:512], [B_stg[0]]), (stg[1][:, 0:512], [B_stg[1]])]
        for _off in (0, 7168):
            _v, _b = arH.view(_off, 1024); stgS.append((_v.bitcast(F32), _b))

        def rotary(fps, g, n, dst_i, Bdst):
            cs = csb[g % 2][:, 0:n]; sn = csb[g % 2][:, 512:512 + n]
            p1 = psf[fps[0]][:, 0:n]; p2 = psf[fps[1]][:, 0:n]
            b1 = tfa(); b2 = tfa(); a1 = tfa(); a2 = tfa()
            P.op('dve', lambda e: e.tensor_tensor(out=tf[b1][:, 0:n], in0=p1, in1=cs, op=ALU.mult),
                 reads=[B_psf[fps[0]]] + B_csb[g % 2], writes=[B_tf[b1]])
            P.op('dve', lambda e: e.tensor_tensor(out=tf[b2][:, 0:n], in0=p2, in1=sn, op=ALU.mult),
                 reads=[B_psf[fps[1]]] + B_csb[g % 2], writes=[B_tf[b2]])
            P.op('pool', lambda e: e.tensor_tensor(out=th_[dst_i][:, 0:n], in0=tf[b1][:, 0:n], in1=tf[b2][:, 0:n], op=ALU.subtract),
                 reads=[B_tf[b1], B_tf[b2]], writes=[Bdst])
            P.op('dve', lambda e: e.tensor_tensor(out=tf[a1][:, 0:n], in0=p1, in1=sn, op=ALU.mult),
                 reads=[B_psf[fps[0]]] + B_csb[g % 2], writes=[B_tf[a1]])
            P.op('dve', lambda e: e.tensor_tensor(out=tf[a2][:, 0:n], in0=p2, in1=cs, op=ALU.mult),
                 reads=[B_psf[fps[1]]] + B_csb[g % 2], writes=[B_tf[a2]])
            ffree(fps[0]); ffree(fps[1])
            P.op('pool', lambda e: e.tensor_tensor(out=th_[dst_i][:, 512:512 + n], in0=tf[a1][:, 0:n], in1=tf[a2][:, 0:n], op=ALU.add),
                 reads=[B_tf[a1], B_tf[a2]], writes=[Bdst])

        pending = []
        for h in range(4):
            WA, BWA = wget()
            WB, BWB = wget()
            def head_tables(hh):
                if hh < 4 and hh not in ht_done:
                    ht_done.add(hh)
                    P.op('sp', lambda e, hh=hh: e.dma_start(out=maskX, in_=c_maskX[hh]), writes=[B_mx], dma_sem='hc')
                    P.op('sp', lambda e, hh=hh: e.dma_start(out=qdecX, in_=c_qdecX[hh].partition_broadcast(128)), writes=[B_qx], dma_sem='hc2')

            head_tables(h)
            P.op('sp', lambda e, h=h: e.dma_start(out=maskSf, in_=c_maskSf[h]), writes=BL_maskSf, dma_sem='hc3')

            def s_issue(b, h=h):
                if b < NB:
                    P.op('pool', lambda e, b=b, h=h: e.dma_start(
                        out=Ss32[b % 4], in_=sret[b, h].rearrange("(c p) e -> p c e", p=128)),
                        writes=B_Ss32[b % 4], dma_sem='ss%d' % (b % 4))

            for g in range(5):
                P.tag = 'ret' if g < 4 else 'ret_s'
                n = GROUPS[g][1]
                if g == 3:
                    s_issue(0); s_issue(1)
                if g == 4:
                    s_issue(2); s_issue(3)
                if not (g == 0 and h in cs0_done):
                    P.op('sp', lambda e, g=g, n=n: e.dma_start(out=csb[g % 2][:, 0:n], in_=c_cos[:, sl(g)]), writes=B_csb[g % 2], dma_sem='cs%d' % (g % 2))
                    P.op('sp', lambda e, g=g, n=n: e.dma_start(out=csb[g % 2][:, 512:512 + n], in_=c_sin[:, sl(g)]), writes=B_csb[g % 2], dma_sem='cs%d' % (g % 2))
                qi = tha(); ki = tha(); qdi = tha()
                for (dst_i, off) in ((qi, 0), (ki, 256)):
                    fps = [falloc(), falloc()]
                    for dch in range(2):
                        for k in range(8):
                            mm(psf[fps[dch]][:, 0:n], WA[:, k, off + dch * 128:off + (dch + 1) * 128], hT[:, k, sl(g)],
                               k == 0, k == 7, [B_hT, BWA], B_psf[fps[dch]])
                    rotary(fps, g, n, dst_i, B_th[dst_i])
                if g == 4:
                    while pending:
                        pending.pop(0)()
                    head_tables(h + 1)
                    if h < 3:
                        cs0_done.add(h + 1)
                        P.op('sp', lambda e: e.dma_start(out=csb[0][:, 0:512], in_=c_cos[:, 0:512]), writes=B_csb[0], dma_sem='cs0')
                        P.op('sp', lambda e: e.dma_start(out=csb[0][:, 512:1024], in_=c_sin[:, 0:512]), writes=B_csb[0], dma_sem='cs0')
                for dch in range(2):
                    if g < 4:
                        o3 = th_[qdi][:, dch * 512:dch * 512 + 512]
                        i3 = th_[qi][:, dch * 512:dch * 512 + 512]
                        d3 = qdecX
                        rd = [B_th[qi], B_qx]
                    else:
                        o3 = th_[qdi][:, dch * 512:dch * 512 + 128].rearrange("p (b t) -> p b t", t=8)
                        i3 = th_[qi][:, dch * 512:dch * 512 + 128].rearrange("p (b t) -> p b t", t=8)
                        d3 = qdecS[:, h, :].unsqueeze(1).to_broadcast([128, NB, 8])
                        rd = [B_th[qi]] + RC
                    P.op('pool' if g < 4 else 'dve', lambda e, o3=o3, i3=i3, d3=d3: e.tensor_tensor(out=o3, in0=i3, in1=d3, op=ALU.mult),
                         reads=rd, writes=[B_th[qdi]])
                if g < 4:
                    fv = [falloc(), falloc()]
                    for c in range(4):
                        for k in range(8):
                            mm(psf[fv[c // 2]][:, (c % 2) * 256:(c % 2) * 256 + 256], hT[:, k, g * 512 + c * 128:g * 512 + (c + 1) * 128],
                               WB[:, k, 0:256], k == 0, k == 7, [B_hT, BWB], B_psf[fv[c // 2]])
                    for hf in range(2):
                        P.op('act', lambda e, hf=hf, f=fv[hf]: e.copy(
                            out=vtm1[:, hf * 2:hf * 2 + 2, :].rearrange("p c e -> p (c e)"), in_=psf[f][:]),
                            reads=[B_psf[fv[hf]]], writes=B_vtm1)
                        ffree(fv[hf])
                    fg_ = [falloc(), falloc()]
                    for c in range(4):
                        for k in range(8):
                            mm(psf[fg_[c // 2]][:, (c % 2) * 256:(c % 2) * 256 + 256], hT[:, k, g * 512 + c * 128:g * 512 + (c + 1) * 128],
                               WB[:, k, 256:512], k == 0, k == 7, [B_hT, BWB], B_psf[fg_[c // 2]])
                    for hf in range(2):
                        ti = tfa()
                        f = fg_[hf]
                        gv = gsg1[:, hf * 2:hf * 2 + 2, :]
                        P.op('act', lambda e, f=f, ti=ti: e.activation(out=tf[ti][:], in_=psf[f][:], func=AF.Silu),
                             reads=[B_psf[f]], writes=[B_tf[ti]])
                        ffree(f)
                        P.op('pool', lambda e, ti=ti, gv=gv, h=h: e.tensor_tensor(
                            out=gv, in0=tf[ti][:].rearrange("p (c e) -> p c e", c=2),
                            in1=gret[:, h * 256:(h + 1) * 256].unsqueeze(1).to_broadcast([128, 2, 256]), op=ALU.mult),
                            reads=[B_tf[ti], B_gaux], writes=B_gsg1)
                    hi = halloc()
                    for c in range(4):
                        for dch in range(2):
                            tr(psh[hi][:, c * 256 + dch * 128:c * 256 + (dch + 1) * 128],
                               th_[ki][:, dch * 512 + c * 128:dch * 512 + (c + 1) * 128], idb[:], [B_th[ki]], B_psh[hi])
                    for c in range(4):
                        P.op('act', lambda e, hi=hi, h=h, c=c: e.activation(
                            out=kdt1[:, c, :], in_=psh[hi][:, c * 256:(c + 1) * 256], func=AF.Copy, scale=kdecX[:, h * 4 + c:h * 4 + c + 1]),
                            reads=[B_psh[hi]] + BL_kdecX, writes=BL_kdt)
                    hfree(hi)
                    sx = g % 2
                    for cp in range(4):
                        nn = 512 - 128 * cp
                        fs = falloc()
                        for dch in range(2):
                            mm(psf[fs][:, 0:nn], th_[ki][:, dch * 512 + cp * 128:dch * 512 + cp * 128 + 128],
                               th_[qi][:, dch * 512 + cp * 128:dch * 512 + 512], dch == 0, dch == 1, [B_th[ki], B_th[qi]], B_psf[fs])
                        P.op('dve', lambda e, fs=fs, sx=sx, cp=cp, nn=nn: e.tensor_tensor(
                            out=stmX[sx][:, SOFF[cp]:SOFF[cp] + nn], in0=psf[fs][:, 0:nn], in1=maskX[:, 0:nn], op=ALU.mult),
                            reads=[B_psf[fs], B_mx], writes=B_stmX[sx])
                        ffree(fs)
                    while pending:
                        pending.pop(0)()
                    fo = [falloc(), falloc()]
                    scur = g % 2
                    for c in range(4):
                        oview = psf[fo[c // 2]][:, (c % 2) * 256:(c % 2) * 256 + 256]
                        nmm = (c + 1) + (2 if g > 0 else 0)
                        im = 0
                        for cp in range(c + 1):
                            mm(oview, stmX[sx][:, SOFF[cp] + (c - cp) * 128:SOFF[cp] + (c - cp) * 128 + 128], vtm1[:, cp, :],
                               im == 0, im == nmm - 1, [B_stmX[sx], B_vtm1], B_psf[fo[c // 2]])
                            im += 1
                        if g > 0:
                            for dch in range(2):
                                mm(oview, th_[qdi][:, dch * 512 + c * 128:dch * 512 + c * 128 + 128], Sbf[scur][:, dch, :], False, im == nmm - 1,
                                   [B_th[qdi], B_Sbf[scur]], B_psf[fo[c // 2]])
                                im += 1
                    fkv = falloc()
                    for dch in range(2):
                        for c in range(4):
                            mm(psf[fkv][:, dch * 256:dch * 256 + 256], kdt1[:, c, dch * 128:(dch + 1) * 128], vtm1[:, c, :],
                               c == 0, c == 3, [BL_kdt, B_vtm1], B_psf[fkv])
                    if g > 0:
                        P.op('dve', lambda e, fkv=fkv, h=h: e.scalar_tensor_tensor(
                            out=S32.rearrange("p c e -> p (c e)"), in0=S32.rearrange("p c e -> p (c e)"), scalar=CDX[h],
                            in1=psf[fkv][:], op0=ALU.mult, op1=ALU.add), reads=[B_psf[fkv], BL_S32], writes=[BL_S32])
                    else:
                        P.op('dve', lambda e, fkv=fkv: e.tensor_copy(out=S32.rearrange("p c e -> p (c e)"), in_=psf[fkv][:]),
                             reads=[B_psf[fkv]], writes=[BL_S32])
                    ffree(fkv)
                    if g < 3:
                        snx = (g + 1) % 2
                        P.op('act', lambda e, snx=snx: e.copy(out=Sbf[snx], in_=S32), reads=[BL_S32], writes=[B_Sbf[snx]])
                    else:
                        P.op('sp', lambda e, h=h: e.dma_start(out=nrp[h].rearrange("(c p) e -> p c e", p=128), in_=S32),
                             reads=[BL_S32], dma_sem='o_nrp', is_out=True)
                    st = stat()
                    for c in range(4):
                        oview = psf[fo[c // 2]][:, (c % 2) * 256:(c % 2) * 256 + 256]
                        P.op('act', lambda e, oview=oview, st=st, c=c: e.activation(
                            out=scr[:, 0:256], in_=oview, func=AF.Square, accum_out=ssb[:, st * 4 + c:st * 4 + c + 1]),
                            reads=[B_psf[fo[c // 2]]], writes=[B_scr, B_st[st]])
                    st2 = stat()
                    P.op('dve', lambda e, st=st, st2=st2: e.tensor_scalar(
                        out=ssb[:, st2 * 4:st2 * 4 + 4], in0=ssb[:, st * 4:st * 4 + 4], scalar1=1.0 / 256, scalar2=EPS,
                        op0=ALU.mult, op1=ALU.add), reads=[B_st[st]], writes=[B_st[st2]])
                    P.op('pool', lambda e, st2=st2: e.tensor_tensor(
                        out=ssb[:, st2 * 4:st2 * 4 + 4], in0=ssb[:, st2 * 4:st2 * 4 + 4], in1=mhalf[:, 0:4], op=ALU.pow),
                        reads=[B_st[st2]] + RC, writes=[B_st[st2]])
                    for c in range(4):
                        oview = psf[fo[c // 2]][:, (c % 2) * 256:(c % 2) * 256 + 256]
                        P.op('dve', lambda e, oview=oview, st2=st2, c=c: e.scalar_tensor_tensor(
                            out=onb1[:, c, :], in0=oview, scalar=ssb[:, st2 * 4 + c:st2 * 4 + c + 1], in1=gsg1[:, c, :],
                            op0=ALU.mult, op1=ALU.mult), reads=[B_psf[fo[c // 2]], B_st[st2], B_gsg1], writes=B_onb1)
                    ffree(fo[0]); ffree(fo[1])

                    def tail(h=h, g=g):
                        hi = halloc()
                        for ech in range(2):
                            for c in range(4):
                                tr(psh[hi][:, ech * 512 + c * 128:ech * 512 + (c + 1) * 128], onb1[:, c, ech * 128:(ech + 1) * 128],
                                   idb[:], B_onb1, B_psh[hi])
                        P.op('act', lambda e, hi=hi, h=h, g=g: e.copy(
                            out=ob[:, h * 2:h * 2 + 2, sl(g)], in_=psh[hi][:].rearrange("p (c t) -> p c t", c=2)),
                            reads=[B_psh[hi]], writes=[B_ob[h * 2][g], B_ob[h * 2 + 1][g]])
                        hfree(hi)
                    pending.append(tail)
                else:
                    fv = falloc()
                    for k in range(8):
                        mm(psf[fv][:, 0:256], hT[:, k, LP:NT], WB[:, k, 0:256], k == 0, k == 7, [B_hT, BWB], B_psf[fv])
                    P.op('act', lambda e, fv=fv: e.copy(out=vtmS, in_=psf[fv][:, 0:256]), reads=[B_psf[fv]], writes=BL_vtmS)
                    ffree(fv)
                    fs = falloc()
                    for dch in range(2):
                        mm(psf[fs][:, 0:128], th_[ki][:, dch * 512:dch * 512 + 128], th_[qi][:, dch * 512:dch * 512 + 128],
                           dch == 0, dch == 1, [B_th[ki], B_th[qi]], B_psf[fs])
                    P.op('dve', lambda e, fs=fs: e.tensor_tensor(out=stmS, in0=psf[fs][:, 0:128], in1=maskSf, op=ALU.mult),
                         reads=[B_psf[fs]] + BL_maskSf, writes=BL_stmS)
                    ffree(fs)
                    hi = halloc()
                    for dch in range(2):
                        tr(psh[hi][:, dch * 128:(dch + 1) * 128], th_[ki][:, dch * 512:dch * 512 + 128], idb[:], [B_th[ki]], B_psh[hi])
                    P.op('dve', lambda e, hi=hi, h=h: e.tensor_tensor(
                        out=kdm, in0=psh[hi][:, 0:256].unsqueeze(1).to_broadcast([128, NB, 256]),
                        in1=rmd[:, h * NB:(h + 1) * NB].unsqueeze(2).to_broadcast([128, NB, 256]), op=ALU.mult),
                        reads=[B_psh[hi]] + BL_rmd, writes=BL_kdm)
                    hfree(hi)
                    sgi = [tfa(), tfa()]
                    for ech in range(2):
                        fi = falloc()
                        for k in range(8):
                            mm(psf[fi][:, 0:128], WB[:, k, 256 + ech * 128:256 + (ech + 1) * 128], hT[:, k, LP:NT], k == 0, k == 7,
                               [B_hT, BWB], B_psf[fi])
                        ti = sgi[ech]
                        P.op('act', lambda e, fi=fi, ti=ti: e.activation(out=tf[ti][:, 0:128], in_=psf[fi][:, 0:128], func=AF.Tanh, scale=0.5),
                             reads=[B_psf[fi]], writes=[B_tf[ti]])
                        P.op('dve', lambda e, fi=fi, ti=ti: e.scalar_tensor_tensor(
                            out=tf[ti][:, 0:128], in0=tf[ti][:, 0:128], scalar=1.0, in1=psf[fi][:, 0:128], op0=ALU.add, op1=ALU.mult),
                            reads=[B_psf[fi], B_tf[ti]], writes=[B_tf[ti]])
                        ffree(fi)
                    foTs = [falloc(), falloc()]
                    for ech in range(2):
                        mm(psf[foTs[ech]][:, 0:128], vtmS[:, ech * 128:(ech + 1) * 128], stmS, True, False,
                           [BL_vtmS, BL_stmS], B_psf[foTs[ech]])
                    def s_cast(b):
                        if b < NB:
                            P.op('act', lambda e, b=b: e.copy(out=Ssbf[b % 3], in_=Ss32[b % 4]), reads=B_Ss32[b % 4], writes=B_Ssbf[b % 3])

                    s_cast(0)
                    s_cast(1)
                    s_cast(2)
                    for b in range(NB):
                        bi = b % 3
                        s4 = b % 4
                        sg_ap, sg_b = stgS[b % 4]
                        for ech in range(2):
                            ov = psf[foTs[ech]][:, b * 8:b * 8 + 8]
                            for dch in range(2):
                                mm(ov, Ssbf[bi][:, dch, ech * 128:(ech + 1) * 128], th_[qdi][:, dch * 512 + b * 8:dch * 512 + b * 8 + 8],
                                   False, (dch == 1 and b == NB - 1), [B_Ssbf[bi], B_th[qdi]], B_psf[foTs[ech]])
                        fkv = falloc()
                        for dch in range(2):
                            mm(psf[fkv][:, dch * 256:dch * 256 + 256], kdm[:, b, dch * 128:(dch + 1) * 128], vtmS, True, True,
                               [BL_kdm, BL_vtmS], B_psf[fkv])
                        s_cast(b + 3)
                        P.op('dve', lambda e, fkv=fkv, sg_ap=sg_ap, s4=s4, h=h: e.scalar_tensor_tensor(
                            out=sg_ap, in0=Ss32[s4].rearrange("p c e -> p (c e)"), scalar=CDS[h], in1=psf[fkv][:],
                            op0=ALU.mult, op1=ALU.add), reads=[B_psf[fkv]] + B_Ss32[s4], writes=sg_b)
                        ffree(fkv)
                        s_issue(b + 4)
                        P.op('sp', lambda e, b=b, sg_ap=sg_ap, h=h: e.dma_start(
                            out=nrs[b, h].rearrange("(c p) e -> p c e", p=128), in_=sg_ap.rearrange("p (c e) -> p c e", c=2)),
                            reads=sg_b, dma_sem='o_nrs%d' % (b % 4), is_out=True)
                    wdone()
                    wdone()
                    sq = tha()
                    for ech in range(2):
                        P.op('act', lambda e, f=foTs[ech], sq=sq, ech=ech: e.activation(
                            out=th_[sq][:, ech * 128:(ech + 1) * 128], in_=psf[f][:, 0:128], func=AF.Square),
                            reads=[B_psf[foTs[ech]]], writes=[B_th[sq]])
                    fss = falloc()
                    for ech in range(2):
                        mm(psf[fss][:, 0:128], ones[:], th_[sq][:, ech * 128:(ech + 1) * 128], ech == 0, ech == 1, [B_th[sq]] + RC, B_psf[fss])
                    trs = tfa()
                    P.op('act', lambda e, fss=fss, trs=trs: e.activation(
                        out=tf[trs][:, 0:128], in_=psf[fss][:, 0:128], func=AF.Sqrt, bias=epsb4[:, 0:1], scale=4.0 / 256),
                        reads=[B_psf[fss]] + RC, writes=[B_tf[trs]])
                    ffree(fss)
                    P.op('dve', lambda e, trs=trs: e.reciprocal(out=tf[trs][:, 0:128], in_=tf[trs][:, 0:128]),
                         reads=[B_tf[trs]], writes=[B_tf[trs]])
                    for ech in range(2):
                        ti = sgi[ech]
                        P.op('dve', lambda e, ech=ech, ti=ti, trs=trs, f=foTs[ech], h=h: e.scalar_tensor_tensor(
                            out=tf[ti][:, 128:256], in0=psf[f][:, 0:128], scalar=gretT[:, h * 2 + ech:h * 2 + ech + 1],
                            in1=tf[trs][:, 0:128], op0=ALU.mult, op1=ALU.mult), reads=[B_psf[foTs[ech]], B_tf[trs], B_tf[ti]] + RC, writes=[B_tf[ti]])
                        P.op('pool', lambda e, ech=ech, ti=ti, h=h: e.tensor_tensor(
                            out=ob[:, h * 2 + ech, LP:NT], in0=tf[ti][:, 128:256], in1=tf[ti][:, 0:128], op=ALU.mult),
                            reads=[B_tf[ti]], writes=[B_ob[h * 2 + ech][4]])
                    ffree(foTs[0]); ffree(foTs[1])
        if stop < 8:
            P.disabled = True
        project_branch(0, False)

        if stop < 9:
            P.disabled = True
        tf_n[0] = NTF
        P.op('sp', lambda e: e.dma_start(out=gaux[:], in_=final_norm_g.partition_broadcast(128)), writes=[B_gaux], dma_sem='gaux')
        W0, BW0 = wget()
        W1, BW1 = wget()
        allmg = [B_mg[fc][g] for fc in range(8) for g in range(5)]
        P.tag = 'final'
        def xload(t):
            if t < NTILE:
                P.op('sp', lambda e, t=t: e.dma_start(out=xbuf[t % NXR][:], in_=xtile_src(t)), writes=[B_x[t % NXR]], dma_sem='x%d' % (t % NXR))

        for t in range(4):
            xload(t)
        for t in range(NTILE):
            xi = t % NXR
            xload(t + 4)
            g = min(t // 4, 4)
            fr = [falloc(), falloc()]
            for j, (W, BW) in enumerate(((W0, BW0), (W1, BW1))):
                for k in range(8):
                    mm(psf[fr[j]][:], mgd[:, k, t * 128:(t + 1) * 128], W[:, k, :], k == 0, k == 7, [B_mg[k][g], BW], B_psf[fr[j]])
            for j in range(2):
                P.op('dve', lambda e, j=j, xi=xi, f=fr[j]: e.scalar_tensor_tensor(
                    out=xbuf[xi][:, j * 512:(j + 1) * 512], in0=psf[f][:], scalar=0.5, in1=xbuf[xi][:, j * 512:(j + 1) * 512],
                    op0=ALU.mult, op1=ALU.add), reads=[B_psf[fr[j]], B_x[xi]], writes=[B_x[xi]])
                ffree(fr[j])
            st = stat()
            sa = ssb[:, st * 4:st * 4 + 1]; sb_ = ssb[:, st * 4 + 1:st * 4 + 2]
            P.op('act', lambda e, xi=xi, sa=sa: e.activation(out=hb[xi][:], in_=xbuf[xi][:], func=AF.Square, accum_out=sa),
                 reads=[B_x[xi]], writes=[B_hb[xi], B_st[st]])
            rstd_of(sa, sb_, D, [B_st[st]])
            P.op('dve', lambda e, xi=xi, sb_=sb_: e.scalar_tensor_tensor(
                out=xbuf[xi][:], in0=xbuf[xi][:], scalar=sb_, in1=gaux[:], op0=ALU.mult, op1=ALU.mult),
                reads=[B_x[xi], B_st[st], B_gaux], writes=[B_x[xi]])
            dst = yp[t * 128:(t + 1) * 128, :] if t < 16 else ys
            P.op('sp', lambda e, dst=dst, xi=xi: e.dma_start(out=dst, in_=xbuf[xi][:]), reads=[B_x[xi]],
                 dma_sem='o_x%d' % xi, is_out=True)

        P.emit(nc)
    return nc, consts


_CACHE = {}


def kernel(x_prompt, x_sample, state_ret, state_conv, cache_mem_k, cache_mem_v, mem_prompt,
           norm_g, w_in, b_gate, ret_norm_g, conv_w, conv_b, w_ret_o, w_conv_o, w_mem_o,
           w_out, mem_norm_g, w_mem_kv, final_norm_g):
    f = lambda a: np.ascontiguousarray(np.asarray(a, dtype=np.float32))
    if 'nc' not in _CACHE:
        _CACHE['nc'] = build_nc()
    nc, consts = _CACHE['nc']
    shared = {
        "norm_g": f(norm_g).reshape(D), "b_gate": f(b_gate).reshape(3072),
        "ret_norm_g": f(ret_norm_g).reshape(1024), "conv_w": f(conv_w).reshape(3, D), "conv_b": f(conv_b).reshape(D),
        "mem_norm_g": f(mem_norm_g).reshape(D), "final_norm_g": f(final_norm_g).reshape(D),
        "wb": _pack_weights({"w_in": f(w_in).reshape(D, 13312), "w_ret_o": f(w_ret_o).reshape(D, D),
                             "w_conv_o": f(w_conv_o).reshape(D, D), "w_mem_o": f(w_mem_o).reshape(D, D),
                             "w_out": f(w_out).reshape(D, D), "w_mem_kv": f(w_mem_kv).reshape(D, 2048)}),
        "c_cos": consts['cos'], "c_sin": consts['sin'], "c_maskX": consts['maskX'], "c_qdecX": consts['qdecX'],
        "c_kdecX": consts['kdecX'], "c_maskSf": consts['maskSf'], "c_rmd": consts['rmd'], "c_qdecS": consts['qdecS'],
        "c_ident": consts['ident'],
    }
    xpr = f(x_prompt); xsa = f(x_sample); sr = f(state_ret); sc = f(state_conv)
    ck = f(cache_mem_k); cv = f(cache_mem_v); mp = f(mem_prompt)
    in_maps = []
    for c in range(NCORES):
        bs = slice(c * NB, (c + 1) * NB)
        m = dict(shared)
        m["xp"] = xpr[c]
        m["xs"] = xsa[bs].reshape(NS, D)
        m["sret"] = sr[0, bs]
        m["sconv"] = sc[0, bs].reshape(NB * 2, D)
        m["cmk"] = ck[0, bs].reshape(NB, 256, D)
        m["cmv"] = cv[0, bs].reshape(NB, 256, D)
        m["memp"] = mp[c]
        in_maps.append(m)
    res = run_bass_kernel_spmd(nc, in_maps, core_ids=list(range(NCORES)))
    R = res.results
    y_prompt = np.stack([R[c]["yp"] for c in range(NCORES)], 0).reshape(8, LP, D)
    y_sample = np.concatenate([R[c]["ys"].reshape(NB, LS, D) for c in range(NCORES)], 0)
    nrp_ = np.stack([R[c]["nrp"] for c in range(NCORES)], 0)[None]
    nrs_ = np.concatenate([R[c]["nrs"] for c in range(NCORES)], 0)[None]
    ncp_ = np.stack([R[c]["ncp"] for c in range(NCORES)], 0)[None]
    ncs_ = np.concatenate([R[c]["ncs"].reshape(NB, 2, D) for c in range(NCORES)], 0)[None]
    nmk_ = np.stack([R[c]["nmk"].reshape(256, 4, 256) for c in range(NCORES)], 0)[None]
    nmv_ = np.stack([R[c]["nmv"].reshape(256, 4, 256) for c in range(NCORES)], 0)[None]
    return (y_prompt.astype(np.float32), y_sample.astype(np.float32), nrp_.astype(np.float32), nrs_.astype(np.float32),
            ncp_.astype(np.float32), ncs_.astype(np.float32), nmk_.astype(np.float32), nmv_.astype(np.float32))
```

```python
import contextlib
import numpy as np
import concourse.bass as bass
import concourse.mybir as mybir
from concourse.bass_utils import run_bass_kernel_spmd

F32 = mybir.dt.float32
BF16 = mybir.dt.bfloat16
AF = mybir.ActivationFunctionType
ALU = mybir.AluOpType

NCORES = 8
D = 1024
LP = 2048
NB = 16
LS = 8
NS = NB * LS
NT = LP + NS
NTILE = NT // 128
PAST = 16384
EPS = 1e-6
GROUPS = [(0, 512), (512, 512), (1024, 512), (1536, 512), (2048, 128)]

COMPUTE = ('pe', 'act', 'dve', 'pool')
ENGS = ('pe', 'act', 'dve', 'pool', 'sp')


class Buf:
    __slots__ = ('name', 'last_w', 'reads', 'excl')

    def __init__(self, name='', excl=False):
        self.name = name
        self.last_w = None
        self.reads = []
        self.excl = excl


class Op:
    __slots__ = ('eng', 'idx', 'fn', 'waits', 'signal', 'sem', 'val', 'is_dma', 'signo', 'tag')

    def __init__(self, eng, idx, fn, is_dma):
        self.eng = eng
        self.idx = idx
        self.fn = fn
        self.waits = []
        self.signal = False
        self.sem = None
        self.val = 0
        self.is_dma = is_dma
        self.signo = 0


def _flat(x):
    out = []
    for i in x:
        if isinstance(i, (list, tuple)):
            out.extend(_flat(i))
        else:
            out.append(i)
    return out


class Plan:
    def __init__(self):
        self.streams = {e: [] for e in ENGS}
        self.waited = {e: {} for e in ENGS}
        self.waited_dma = {e: {} for e in ENGS}
        self.dma_counts = {}
        self.out_sems = set()
        self.disabled = False
        self.tag = 'init'

    def _dep(self, ev, d, kind):
        if d is ev:
            return
        if d.is_dma:
            w = self.waited_dma[ev.eng]
            if w.get(d.sem, 0) >= d.val:
                return
            need = self.dma_counts[d.sem]
            w[d.sem] = need
            ev.waits.append((d.sem, need))
            return
        if d.eng == ev.eng and not ev.is_dma and d.eng == 'pe':
            return
        w = self.waited[ev.eng]
        if w.get(d.eng, -1) >= d.idx:
            return
        w[d.eng] = d.idx
        d.signal = True
        ev.waits.append(d)

    def op(self, eng, fn, reads=(), writes=(), dma_sem=None, is_out=False):
        reads = _flat(reads)
        writes = _flat(writes)
        st = self.streams[eng]
        ev = Op(eng, len(st), fn, dma_sem is not None)
        ev.tag = self.tag
        if self.disabled:
            return ev
        best = {}

        def cand(d, kind):
            key = ('d', d.sem) if d.is_dma else ('e', d.eng, kind == 'war')
            cur = best.get(key)
            if cur is None or (d.val > cur[0].val if d.is_dma else d.idx > cur[0].idx):
                best[key] = (d, kind)

        for b in reads:
            if b.last_w is not None:
                cand(b.last_w, 'raw')
            if b.excl:
                for r in b.reads:
                    cand(r, 'war')
        for b in writes:
            if b.last_w is not None:
                cand(b.last_w, 'waw')
            for r in b.reads:
                cand(r, 'war')
        for (d, kind) in best.values():
            self._dep(ev, d, kind)
        for b in reads:
            if not ev.is_dma:
                b.reads = [r for r in b.reads if r.is_dma or r.eng != ev.eng]
            b.reads.append(ev)
        for b in writes:
            b.last_w = ev
            b.reads = []
        if dma_sem is not None:
            v = self.dma_counts.get(dma_sem, 0) + 16
            self.dma_counts[dma_sem] = v
            ev.sem = dma_sem
            ev.val = v
            if is_out:
                self.out_sems.add(dma_sem)
        st.append(ev)
        return ev

    def emit(self, nc):
        engobj_names = {'pe': 'tensor', 'act': 'scalar', 'dve': 'vector', 'pool': 'gpsimd', 'sp': 'sync'}
        for e in COMPUTE:
            n = 0
            for o in self.streams[e]:
                if o.signal and not o.is_dma:
                    n += 1
                    o.signo = n
        with contextlib.ExitStack() as es:
            esem = {e: es.enter_context(nc.semaphore('c_' + e)) for e in COMPUTE}
            dsem = {k: es.enter_context(nc.semaphore('d_%s' % (k,))) for k in self.dma_counts}
            block = es.enter_context(nc.Block())

            def run(ename):
                def body(eng):
                    for o in self.streams[ename]:
                        for d in o.waits:
                            if isinstance(d, tuple):
                                eng.wait_ge(dsem[d[0]], d[1])
                            else:
                                eng.wait_ge(esem[d.eng], d.signo)
                        ins = o.fn(eng)
                        if o.is_dma:
                            ins.then_inc(dsem[o.sem], 16)
                        elif o.signal:
                            ins.then_inc(esem[o.eng], 1)
                    if ename == 'sp':
                        for k in sorted(self.out_sems):
                            eng.wait_ge(dsem[k], self.dma_counts[k])
                return body

            for e in ENGS:
                getattr(block, engobj_names[e])(run(e))


def _consts():
    half = 128
    inv = (np.float32(10000.0) ** (-(np.arange(half, dtype=np.float32)) / np.float32(half))).astype(np.float32)
    pos = np.concatenate([np.arange(LP, dtype=np.float32),
                          np.tile(PAST + np.arange(LS, dtype=np.float32), NB)]).astype(np.float32)
    ang = (pos[None, :] * inv[:, None]).astype(np.float32)
    cos = np.cos(ang.astype(np.float64)).astype(np.float32)
    sin = np.sin(ang.astype(np.float64)).astype(np.float32)
    lg = np.log1p(-np.exp2(-5.0 - np.arange(4, dtype=np.float32))).astype(np.float32)

    def dec(C):
        idx = np.arange(C, dtype=np.float32)
        diff = idx[:, None] - idx[None, :]
        inner = np.where(diff[None] >= 0, np.exp(lg[:, None, None] * np.maximum(diff, 0.0)[None]), 0.0)
        inner = inner.astype(np.float32)
        qd = np.exp(lg[None, :] * (idx[:, None] + 1.0)).astype(np.float32)
        kd = np.exp(lg[None, :] * (C - 1.0 - idx[:, None])).astype(np.float32)
        cd = np.exp(lg * C).astype(np.float32)
        return inner, qd, kd, cd

    lg64 = lg.astype(np.float64)
    r = np.arange(512, dtype=np.float64)
    j = np.arange(128, dtype=np.float64)
    diff = r[None, :] - j[:, None]
    maskX = np.where(diff[None] >= 0, np.exp(lg64[:, None, None] * np.maximum(diff, 0.0)[None]), 0.0) / 16.0
    qdecX = np.exp(lg64[:, None] * (r[None, :] + 1.0))
    kdecX = np.zeros((128, 16))
    for h in range(4):
        for c in range(4):
            kdecX[:, h * 4 + c] = np.exp(lg64[h] * (511.0 - (c * 128 + j))) / 16.0
    cdX = np.exp(lg64 * 512.0)
    t = np.arange(128)
    bb = t // 8; tt = (t % 8).astype(np.float64)
    same = (bb[:, None] == bb[None, :])
    d8 = tt[None, :] - tt[:, None]
    maskSf = np.where(same[None] & (d8[None] >= 0), np.exp(lg64[:, None, None] * np.maximum(d8, 0.0)[None]), 0.0) / 16.0
    rmd = np.zeros((128, 64))
    for h in range(4):
        for b in range(NB):
            sel = bb == b
            rmd[sel, h * NB + b] = np.exp(lg64[h] * (7.0 - tt[sel])) / 16.0
    qdS = np.exp(lg64[:, None] * (np.arange(LS, dtype=np.float64)[None, :] + 1.0))
    cdS = np.exp(lg64 * LS)
    f32 = lambda a: np.ascontiguousarray(a, dtype=np.float32)
    return dict(cos=cos, sin=sin, maskX=f32(maskX), qdecX=f32(qdecX), kdecX=f32(kdecX), maskSf=f32(maskSf), rmd=f32(rmd),
                qdecS=f32(qdS), ident=np.eye(128, dtype=np.float32)), [float(x) for x in cdX], [float(x) for x in cdS]


def _wblocks():
    blocks = []
    for c in range(8):
        blocks.append([("w_in", 4096 + q * 1024 + c * 128, 128, q * 128) for q in range(4)])
    for j in range(4):
        blocks.append([("w_conv_o", j * 256, 256, 0), ("w_in", 10240 + 1024 + j * 256, 256, 256)])
    for j in range(4):
        blocks.append([("w_mem_kv", j * 512, 512, 0)])
    for h in range(4):
        blocks.append([("w_in", 8192 + h * 256, 256, 0), ("w_in", 9216 + h * 256, 256, 256)])
    for j in range(4):
        blocks.append([("w_mem_o", j * 256, 256, 0), ("w_in", 10240 + 2048 + j * 256, 256, 256)])
    for h in range(4):
        blocks.append([("w_in", h * 256, 256, 0), ("w_in", 1024 + h * 256, 256, 256)])
        blocks.append([("w_in", 2048 + h * 256, 256, 0), ("w_in", 3072 + h * 256, 256, 256)])
    for j in range(4):
        blocks.append([("w_ret_o", j * 256, 256, 0), ("w_in", 10240 + j * 256, 256, 256)])
    for j in range(2):
        blocks.append([("w_out", j * 512, 512, 0)])
    return blocks


def _pack_weights(ws):
    blocks = _wblocks()
    wb = np.empty((len(blocks), 128, 8, 512), dtype=np.float32)
    for i, blk in enumerate(blocks):
        for (name, c0, n, off) in blk:
            wb[i, :, :, off:off + n] = ws[name][:, c0:c0 + n].reshape(8, 128, n).transpose(1, 0, 2)
    return wb.reshape(len(blocks), 128, 4096)


def build_nc(stop=99, dbg=0):
    consts, CDX, CDS = _consts()
    nc = bass.Bass("TRN2", target_bir_lowering=False)

    def din(name, shape):
        return nc.dram_tensor(name, list(shape), F32, kind="ExternalInput").ap()

    def dout(name, shape):
        return nc.dram_tensor(name, list(shape), F32, kind="ExternalOutput").ap()

    xp = din("xp", [LP, D]); xs = din("xs", [NS, D])
    sret = din("sret", [NB, 4, 256, 256]); sconv = din("sconv", [NB * 2, D])
    cmk = din("cmk", [NB, 256, D]); cmv = din("cmv", [NB, 256, D]); memp = din("memp", [256, D])
    norm_g = din("norm_g", [D]); b_gate = din("b_gate", [3072])
    wb = din("wb", [len(_wblocks()), 128, 4096])
    ret_norm_g = din("ret_norm_g", [1024]); conv_w = din("conv_w", [3, D]); conv_b = din("conv_b", [D])
    mem_norm_g = din("mem_norm_g", [D])
    final_norm_g = din("final_norm_g", [D])
    c_cos = din("c_cos", [128, NT]); c_sin = din("c_sin", [128, NT])
    c_maskX = din("c_maskX", [4, 128, 512]); c_qdecX = din("c_qdecX", [4, 512]); c_kdecX = din("c_kdecX", [128, 16])
    c_maskSf = din("c_maskSf", [4, 128, 128]); c_rmd = din("c_rmd", [128, 64]); c_qdecS = din("c_qdecS", [4, 8])
    c_ident = din("c_ident", [128, 128])

    yp = dout("yp", [LP, D]); ys = dout("ys", [NS, D])
    nrp = dout("nrp", [4, 256, 256]); nrs = dout("nrs", [NB, 4, 256, 256])
    ncp = dout("ncp", [2, D]); ncs = dout("ncs", [NB * 2, D])
    nmk = dout("nmk", [256, D]); nmv = dout("nmv", [256, D])

    P = Plan()
    with contextlib.ExitStack() as es:
        def sb(name, shape, dt=F32):
            return es.enter_context(nc.sbuf_tensor(name, list(shape), dt))

        def psb(name, shape, dt=F32):
            return es.enter_context(nc.psum_tensor(name, list(shape), dt))

        hT = sb("hT", [128, 8, NT], BF16)
        ob = sb("ob", [128, 8, NT], BF16)
        mgd = sb("mgd", [128, 8, NT], BF16)
        B_hTa = Buf('hTa'); B_hTb = Buf('hTb'); B_hT = [B_hTa, B_hTb]
        B_ob = [[Buf('ob') for _ in GROUPS] for _ in range(8)]
        B_mg = [[Buf('mg') for _ in GROUPS] for _ in range(8)]
        B_mgt = [Buf('mgt') for _ in range(NTILE)]
        NSLOT = 4
        wsl = [sb("w%d" % i, [128, 8, 512], BF16) for i in range(NSLOT)]
        B_w = [Buf('w%d' % i) for i in range(NSLOT)]
        gaux = sb("gaux", [128, D])
        maskT = sb("maskT", [128, 4, 128]); qdec = sb("qdec", [128, 4, 128])
        qdecS = sb("qdecS", [128, 4, 8])
        idf = sb("idf", [128, 128]); idb = sb("idb", [128, 128], BF16); ones = sb("ones", [128, 128], BF16)
        mhalf = sb("mhalf", [128, 8])
        epsb = sb("epsb", [128, 2])
        epsb4 = sb("epsb4", [128, 2])
        svec = sb("svec", [128, 64]); hbg = sb("hbg", [128, 24])
        bg = svec[:, 0:24]; cbias = svec[:, 24:32]; gretT = svec[:, 32:40]
        cw = svec[:, 40:64].rearrange("p (t c) -> p t c", t=3)
        B_c = Buf('consts')
        B_gaux = Buf('gaux')
        psf = [psb("psf%d" % i, [128, 512], F32) for i in range(6)]
        psh = [psb("psh%d" % i, [128, 1024], BF16) for i in range(2)]
        B_psf = [Buf('psf%d' % i, True) for i in range(6)]
        B_psh = [Buf('psh%d' % i, True) for i in range(2)]
        free_f = list(range(6))
        free_h = list(range(2))

        def falloc():
            return free_f.pop(0)

        def ffree(i):
            free_f.append(i)

        def halloc():
            return free_h.pop(0)

        def hfree(i):
            free_h.append(i)

        class Arena:
            def __init__(self, t, n, rs):
                self.t = t
                self.rs = rs
                self.bufs = [Buf('ar') for _ in range((n + rs - 1) // rs)]

            def view(self, off, size, parts=128):
                return self.t[0:parts, off:off + size], self.bufs[off // self.rs:(off + size - 1) // self.rs + 1]

        AF_N = 4096
        AH_N = 8192
        arF = Arena(sb("arF", [128, AF_N]), AF_N, 256)
        arH = Arena(sb("arH", [128, AH_N], BF16), AH_N, 512)
        xbuf = [sb("xbuf%d" % i, [128, D]) for i in range(2)]
        B_x = [[Buf('x0a'), Buf('x0b')], [Buf('x1a'), Buf('x1b')]]
        hb = [sb("hb%d" % i, [128, D], BF16) for i in range(2)]
        B_hb = [Buf('hb0'), Buf('hb1')]
        scr = None
        ssb = sb("ssb", [128, 64])
        stat_i = [0]

        def stat():
            i = stat_i[0] % 8
            stat_i[0] += 1
            return i

        B_st = [Buf('st%d' % i) for i in range(8)]
        stg = [sb("stg%d" % i, [128, 512]) for i in range(2)]
        B_stg = [Buf('stg0'), Buf('stg1')]
        scr = stg[0][:, 0:128].bitcast(BF16)
        B_scr = B_stg[0]
        NTF = 4
        tf = [sb("tf%d" % i, [128, 512])[:] for i in range(NTF)]
        B_tf = [Buf('tf%d' % i) for i in range(NTF)]
        tf += [xbuf[1][:, 0:512], xbuf[1][:, 512:1024], xbuf[0][:, 0:512], xbuf[0][:, 512:1024],
               hb[0][:].bitcast(F32), hb[1][:].bitcast(F32)]
        B_tf += [B_x[1][0], B_x[1][1], B_x[0][0], B_x[0][1], B_hb[0], B_hb[1]]
        tf_i = [0]
        tf_n = [NTF]

        def tfa():
            i = tf_i[0] % tf_n[0]
            tf_i[0] += 1
            return i

        NTH = 4
        th_ = [sb("th%d" % i, [128, 1024], BF16) for i in range(NTH)]
        B_th = [Buf('th%d' % i) for i in range(NTH)]
        th_i = [0]

        def tha():
            i = th_i[0] % NTH
            th_i[0] += 1
            return i

        def sl(g):
            return slice(GROUPS[g][0], GROUPS[g][0] + GROUPS[g][1])

        def ld(dst, src, sem, extra=()):
            P.op('sp', lambda e: e.dma_start(out=dst, in_=src), writes=[B_c] + list(extra), dma_sem=sem)

        ld(qdecS[:].rearrange("p h i -> p (h i)"), c_qdecS.rearrange("h i -> (h i)").partition_broadcast(128), 'c0')
        ld(idf[:], c_ident, 'c0')
        svt = tf[0][0:64, 0:128]
        ld(svt[0:24, :], b_gate.rearrange("(j p) -> j p", p=128), 'c0', [B_tf[0]])
        ld(svt[24:32, :], conv_b.rearrange("(c p) -> c p", p=128), 'c0', [B_tf[0]])
        ld(svt[32:40, :], ret_norm_g.rearrange("(c p) -> c p", p=128), 'c0', [B_tf[0]])
        ld(svt[40:64, :], conv_w.rearrange("t (c p) -> (t c) p", p=128), 'c0', [B_tf[0]])
        if dbg != 1:
            P.op('pe', lambda e: e.matmul(psf[0][:, 0:64], lhsT=svt, rhs=idf[0:64, 0:64], start=True, stop=True), reads=[B_c, B_tf[0]], writes=[B_psf[0]])
            P.op('dve', lambda e: e.tensor_copy(out=svec[:], in_=psf[0][:, 0:64]), reads=[B_psf[0]], writes=[B_c])
        P.op('dve', lambda e: e.tensor_copy(out=idb[:], in_=idf[:]), reads=[B_c], writes=[B_c])
        P.op('pool', lambda e: e.memset(ones[:], 1.0), writes=[B_c])
        P.op('pool', lambda e: e.memset(mhalf[:], -0.5), writes=[B_c])
        P.op('pool', lambda e: e.memset(epsb[:], EPS), writes=[B_c])
        P.op('pool', lambda e: e.memset(epsb4[:], 4.0 * EPS), writes=[B_c])
        P.op('dve', lambda e: e.tensor_scalar(out=hbg[:], in0=bg, scalar1=0.5, scalar2=None, op0=ALU.mult),
             reads=[B_c], writes=[B_c])
        RC = [B_c]

        blocks = _wblocks()
        wstate = {'issued': 0, 'next': 0}

        def wissue(upto, after=()):
            while wstate['issued'] < min(upto, len(blocks)):
                i = wstate['issued']
                s = i % NSLOT
                P.op('pool', lambda e, i=i, s=s: e.dma_start(out=wsl[s][:].rearrange("p k n -> p (k n)"), in_=wb[i]),
                     reads=list(after), writes=[B_w[s]], dma_sem='w%d' % s)
                wstate['issued'] += 1

        def wget():
            i = wstate['next']
            wstate['next'] += 1
            assert i < wstate['done'] + NSLOT
            wissue(i + 1)
            s = i % NSLOT
            return wsl[s], B_w[s]

        def wdone():
            wstate['done'] += 1
            wissue(wstate['done'] + NSLOT)

        wstate['done'] = 0
        wissue(1)

        def mm(out, lhsT, rhs, start, stop, reads, wbuf):
            P.op('pe', lambda e: e.matmul(out, lhsT=lhsT, rhs=rhs, start=start, stop=stop), reads=reads, writes=[wbuf])

        def tr(out, in_, ident, reads, wbuf):
            P.op('pe', lambda e: e.transpose(out=out, in_=in_, identity=ident), reads=reads + RC, writes=[wbuf])

        def rstd_of(sumsq_ap, out_ap, n, bufs, pre=1.0):
            s = 1.0 / (pre * pre)
            P.op('act', lambda e: e.activation(out=out_ap, in_=sumsq_ap, func=AF.Sqrt, bias=epsb[:, 0:1], scale=s / n),
                 reads=bufs + RC, writes=bufs)
            P.op('dve', lambda e: e.reciprocal(out=out_ap, in_=out_ap), reads=bufs, writes=bufs)

        def norm_tile(src_ap, g_tile, gbuf, xi, nparts=128):
            st = stat()
            sa = ssb[:, st * 4:st * 4 + 1]
            sb_ = ssb[:, st * 4 + 1:st * 4 + 2]
            P.op('act', lambda e: e.activation(out=hb[xi][:], in_=xbuf[xi][:], func=AF.Square, accum_out=sa),
                 reads=[B_x[xi]], writes=[B_hb[xi], B_st[st]])
            rstd_of(sa, sb_, D, [B_st[st]])
            P.op('dve', lambda e: e.scalar_tensor_tensor(out=hb[xi][:], in0=xbuf[xi][:], scalar=sb_, in1=g_tile[:],
                                                         op0=ALU.mult, op1=ALU.mult),
                 reads=[B_x[xi], B_st[st], gbuf], writes=[B_hb[xi]])

        def transpose8(xi, dst3, dbufs, dbufs2=None):
            hi = halloc()
            for k in range(8):
                tr(psh[hi][:, k * 128:(k + 1) * 128], hb[xi][:, k * 128:(k + 1) * 128], idb[:], [B_hb[xi]], B_psh[hi])
            src3 = psh[hi][:].rearrange("p (k t) -> p k t", k=8)
            P.op('act', lambda e: e.copy(out=dst3[:, 0:4, :], in_=src3[:, 0:4, :]), reads=[B_psh[hi]], writes=dbufs)
            P.op('dve', lambda e: e.tensor_copy(out=dst3[:, 4:8, :], in_=src3[:, 4:8, :]), reads=[B_psh[hi]],
                 writes=dbufs if dbufs2 is None else dbufs2)
            hfree(hi)

        if stop < 2:
            P.disabled = True
        def xtile_src(t):
            return xp[t * 128:(t + 1) * 128, :] if t < 16 else xs

        P.op('sp', lambda e: e.dma_start(out=gaux[:], in_=norm_g.partition_broadcast(128)), writes=[B_gaux], dma_sem='gaux')
        P.tag = 'p2'
        for i in range(4):
            _v, _b = arF.view(i * 1024, 1024); xbuf.append(_v); B_x.append(_b)
            _v, _b = arH.view(i * 1024, 1024); hb.append(_v); B_hb.append(_b)
        NXR = 6
        SKEW = 3
        for t in range(NTILE + SKEW):
            if t < NTILE:
                xi = t % NXR
                P.op('sp', lambda e, t=t, xi=xi: e.dma_start(out=xbuf[xi][:], in_=xtile_src(t)),
                     writes=[B_x[xi]], dma_sem='x%d' % xi)
                norm_tile(None, gaux, B_gaux, xi)
            if t >= SKEW:
                tt = t - SKEW
                transpose8(tt % NXR, hT[:, :, tt * 128:(tt + 1) * 128], [B_hTa], [B_hTb])
        wissue(NSLOT, after=B_hT)

        def project_branch(bidx, first):
            P.tag = 'proj%d' % bidx
            for j in range(4):
                Wo, BWo = wget()
                Wg, BWg = Wo, BWo
                for c2 in range(2):
                    c = c2
                    fc = j * 2 + c2
                    for g in range(5):
                        n = GROUPS[g][1]
                        fy = falloc()
                        for k in range(8):
                            mm(psf[fy][:, 0:n], Wo[:, k, c * 128:(c + 1) * 128], ob[:, k, sl(g)], k == 0, k == 7,
                               [B_ob[k][g], BWo], B_psf[fy])
                        fg = falloc()
                        for k in range(8):
                            mm(psf[fg][:, 0:n], Wg[:, k, 256 + c * 128:256 + (c + 1) * 128], hT[:, k, sl(g)], k == 0, k == 7,
                               [B_hT, BWg], B_psf[fg])
                        ti = tfa()
                        bcol = bidx * 8 + fc
                        P.op('act', lambda e, fg=fg, ti=ti, n=n, bcol=bcol: e.activation(
                            out=tf[ti][:, 0:n], in_=psf[fg][:, 0:n], func=AF.Tanh, bias=hbg[:, bcol:bcol + 1], scale=0.5),
                            reads=[B_psf[fg]] + RC, writes=[B_tf[ti]])
                        ffree(fg)
                        if first:
                            P.op('dve', lambda e, fy=fy, ti=ti, n=n, fc=fc, g=g: e.scalar_tensor_tensor(
                                out=mgd[:, fc, sl(g)], in0=tf[ti][:, 0:n], scalar=1.0, in1=psf[fy][:, 0:n],
                                op0=ALU.add, op1=ALU.mult),
                                reads=[B_tf[ti], B_psf[fy]], writes=[B_mg[fc][g]])
                        else:
                            t2 = tfa()
                            P.op('dve', lambda e, fy=fy, ti=ti, t2=t2, n=n: e.scalar_tensor_tensor(
                                out=tf[t2][:, 0:n], in0=tf[ti][:, 0:n], scalar=1.0, in1=psf[fy][:, 0:n],
                                op0=ALU.add, op1=ALU.mult),
                                reads=[B_tf[ti], B_psf[fy]], writes=[B_tf[t2]])
                            P.op('pool', lambda e, t2=t2, n=n, fc=fc, g=g: e.tensor_tensor(
                                out=mgd[:, fc, sl(g)], in0=tf[t2][:, 0:n], in1=mgd[:, fc, sl(g)], op=ALU.add),
                                reads=[B_tf[t2], B_mg[fc][g]], writes=[B_mg[fc][g]])
                        ffree(fy)
                wdone()

        if stop < 3:
            P.disabled = True
        tf_n[0] = NTF + 6
        _v, BL_scT = arF.view(0, 256); scT = _v.rearrange("p (c r) -> p c r", c=8)
        sctm, BL_sctm = arF.view(256, 1024, parts=32)
        pre = []; B_pre = []
        for i in range(2):
            _v, _b = arF.view(1280 + i * 768, 514); pre.append(_v); B_pre.append(_b)
        _v, BL_preS = arF.view(2816, 160); preS = _v.rearrange("p (b r) -> p b r", r=10)
        nct = []; B_nct = []
        for _off in (3072, 256):
            _v, _b = arF.view(_off, 1024, parts=16); nct.append(_v); B_nct.append(_b)
        P.op('sp', lambda e: e.dma_start(out=sctm[:], in_=sconv), writes=BL_sctm, dma_sem='sctm')
        fi = falloc()
        for c in range(8):
            P.op('pe', lambda e, c=c, fi=fi: e.matmul(psf[fi][:, c * 32:(c + 1) * 32], lhsT=sctm[:, c * 128:(c + 1) * 128],
                                                     rhs=idf[0:32, 0:32], start=True, stop=True),
                 reads=BL_sctm + RC, writes=[B_psf[fi]])
        P.op('dve', lambda e, fi=fi: e.tensor_copy(out=scT[:].rearrange("p c r -> p (c r)"), in_=psf[fi][:, 0:256]),
             reads=[B_psf[fi]], writes=BL_scT)
        ffree(fi)

        for c in range(8):
            W, BW = wget()
            P.tag = 'conv'
            for g in range(5):
                n = GROUPS[g][1]
                fcu = falloc(); fcc = falloc(); fcb = falloc(); fcg = falloc()
                for (fx, off) in ((fcu, 0), (fcc, 256), (fcb, 128), (fcg, 384)):
                    for k in range(8):
                        mm(psf[fx][:, 0:n], W[:, k, off:off + 128], hT[:, k, sl(g)], k == 0, k == 7, [B_hT, BW], B_psf[fx])
                t_cu = tfa()
                P.op('act', lambda e, fcu=fcu, t_cu=t_cu, n=n: e.copy(out=tf[t_cu][:, 0:n], in_=psf[fcu][:, 0:n]),
                     reads=[B_psf[fcu]], writes=[B_tf[t_cu]])
                ffree(fcu)
                if g < 4:
                    pi = g % 2
                    pb, Bp = pre[pi], B_pre[pi]
                    if g == 0:
                        P.op('pool', lambda e, pb=pb: e.memset(pb[:, 0:2], 0.0), writes=Bp)
                    else:
                        po = pre[1 - pi]
                        P.op('pool', lambda e, pb=pb, po=po: e.tensor_copy(out=pb[:, 0:2], in_=po[:, 512:514]),
                             reads=B_pre[1 - pi], writes=Bp)
                    cur = pb[:, 2:514]; m1 = pb[:, 1:513]; m2 = pb[:, 0:512]

                    def v3(a):
                        return a
                else:
                    pb, Bp = preS, BL_preS
                    P.op('pool', lambda e, c=c: e.tensor_copy(out=preS[:, :, 0:2],
                                                              in_=scT[:, c, :].rearrange("p (b r) -> p b r", r=2)),
                         reads=BL_scT, writes=Bp)
                    cur = pb[:, :, 2:10]; m1 = pb[:, :, 1:9]; m2 = pb[:, :, 0:8]

                    def v3(a):
                        return a.rearrange("p (b t) -> p b t", t=8)
                P.op('dve', lambda e, fcc=fcc, t_cu=t_cu, n=n, cur=cur, v3=v3: e.tensor_tensor(
                    out=cur, in0=v3(psf[fcc][:, 0:n]), in1=v3(tf[t_cu][:, 0:n]), op=ALU.mult),
                    reads=[B_psf[fcc], B_tf[t_cu]], writes=Bp)
                ffree(fcc)
                t_cv = tfa()
                P.op('act', lambda e, t_cv=t_cv, cur=cur, n=n, c=c, v3=v3: e.activation(
                    out=v3(tf[t_cv][:, 0:n]), in_=cur, func=AF.Identity, bias=cbias[:, c:c + 1], scale=cw[:, 2, c:c + 1]),
                    reads=Bp + RC, writes=[B_tf[t_cv]])
                P.op('dve', lambda e, t_cv=t_cv, m1=m1, n=n, c=c, v3=v3: e.scalar_tensor_tensor(
                    out=v3(tf[t_cv][:, 0:n]), in0=m1, scalar=cw[:, 1, c:c + 1], in1=v3(tf[t_cv][:, 0:n]),
                    op0=ALU.mult, op1=ALU.add), reads=Bp + [B_tf[t_cv]] + RC, writes=[B_tf[t_cv]])
                P.op('dve', lambda e, t_cv=t_cv, m2=m2, n=n, c=c, v3=v3: e.scalar_tensor_tensor(
                    out=v3(tf[t_cv][:, 0:n]), in0=m2, scalar=cw[:, 0, c:c + 1], in1=v3(tf[t_cv][:, 0:n]),
                    op0=ALU.mult, op1=ALU.add), reads=Bp + [B_tf[t_cv]] + RC, writes=[B_tf[t_cv]])
                t_th = tfa()
                P.op('act', lambda e, fcg=fcg, t_th=t_th, n=n: e.activation(
                    out=tf[t_th][:, 0:n], in_=psf[fcg][:, 0:n], func=AF.Silu),
                    reads=[B_psf[fcg]], writes=[B_tf[t_th]])
                ffree(fcg)
                P.op('dve', lambda e, fcb=fcb, t_cv=t_cv, n=n: e.tensor_tensor(
                    out=tf[t_cv][:, 0:n], in0=psf[fcb][:, 0:n], in1=tf[t_cv][:, 0:n], op=ALU.mult),
                    reads=[B_psf[fcb], B_tf[t_cv]], writes=[B_tf[t_cv]])
                ffree(fcb)
                P.op('pool', lambda e, t_cv=t_cv, t_th=t_th, n=n, c=c, g=g: e.tensor_tensor(
                    out=ob[:, c, sl(g)], in0=tf[t_cv][:, 0:n], in1=tf[t_th][:, 0:n], op=ALU.mult),
                    reads=[B_tf[t_cv], B_tf[t_th]], writes=[B_ob[c][g]])
            for (ntok, src2d, rdb, dst, Bd) in (
                    (2, pre[1][:, 512:514], B_pre[1], None, None),
                    (16, preS[:, :, 8], BL_preS, nct[0], B_nct[0]),
                    (16, preS[:, :, 9], BL_preS, nct[1], B_nct[1])):
                f1 = falloc()
                P.op('pe', lambda e, f1=f1, ntok=ntok, src2d=src2d: e.matmul(
                    psf[f1][0:ntok, 0:128], lhsT=src2d, rhs=idf[:, :], start=True, stop=True),
                    reads=[rdb] + RC, writes=[B_psf[f1]])
                if dst is None:
                    dsl = stg[c // 4][0:2, (c % 4) * 128:(c % 4 + 1) * 128]
                    Bd = [B_stg[c // 4]]
                else:
                    dsl = dst[0:ntok, c * 128:(c + 1) * 128]
                P.op('act', lambda e, f1=f1, ntok=ntok, dsl=dsl: e.copy(out=dsl, in_=psf[f1][0:ntok, 0:128]),
                     reads=[B_psf[f1]], writes=Bd)
                ffree(f1)
            wdone()
        for hf in range(2):
            P.op('sp', lambda e, hf=hf: e.dma_start(out=ncp[:, hf * 512:(hf + 1) * 512], in_=stg[hf][0:2, :]), reads=[B_stg[hf]],
                 dma_sem='o_stg%d' % hf, is_out=True)
        for t2_ in range(2):
            P.op('sp', lambda e, t2_=t2_: e.dma_start(out=ncs.rearrange("(b t) d -> t b d", t=2)[t2_], in_=nct[t2_]), reads=B_nct[t2_],
                 dma_sem='o_nc%d' % t2_, is_out=True)
        if stop < 4:
            P.disabled = True
        project_branch(1, True)

        if stop < 1:
            P.disabled = True
        P.tag = 'p1'
        _v, BL_memT = arH.view(0, 2048); memT = _v.rearrange("p (k m) -> p k m", k=8)
        _v, BL_KT = arH.view(2048, 2048); KT = _v.rearrange("p (k m) -> p k m", k=8)
        _v, BL_Vp = arH.view(4096, 2048); Vp = _v.rearrange("p (t d) -> p t d", t=2)
        P.op('sp', lambda e: e.dma_start(out=gaux[:], in_=mem_norm_g.partition_broadcast(128)), writes=[B_gaux], dma_sem='gaux')
        for t in range(2):
            P.op('sp', lambda e, t=t: e.dma_start(out=xbuf[t][:], in_=memp[t * 128:(t + 1) * 128, :]),
                 writes=[B_x[t]], dma_sem='x%d' % t)
            norm_tile(None, gaux, B_gaux, t)
            transpose8(t, memT[:, :, t * 128:(t + 1) * 128], BL_memT)
        for j in range(4):
            W, BW = wget()
            isK = j < 2
            for t in range(2):
                fi = falloc()
                for k in range(8):
                    mm(psf[fi][:], memT[:, k, t * 128:(t + 1) * 128], W[:, k, :], k == 0, k == 7, BL_memT + [BW], B_psf[fi])
                si = (j * 2 + t) % 2
                P.op('act', lambda e, fi=fi, si=si: e.copy(out=stg[si][:, 0:512], in_=psf[fi][:]),
                     reads=[B_psf[fi]], writes=[B_stg[si]])
                if not isK:
                    P.op('dve', lambda e, fi=fi, t=t, j=j: e.tensor_copy(out=Vp[:, t, (j - 2) * 512:(j - 1) * 512], in_=psf[fi][:]),
                         reads=[B_psf[fi]], writes=BL_Vp)
                ffree(fi)
                dst = (nmk if isK else nmv)[t * 128:(t + 1) * 128, (j % 2) * 512:(j % 2) * 512 + 512]
                P.op('sp', lambda e, dst=dst, si=si: e.dma_start(out=dst, in_=stg[si][:, 0:512]),
                     reads=[B_stg[si]], dma_sem='o_stg%d' % si, is_out=True)
            if isK:
                fi = falloc()
                for c in range(4):
                    if c == 2:
                        pass
                    half = c % 2
                    if c == 2:
                        P.op('dve', lambda e, fi=fi, j=j: e.tensor_copy(
                            out=KT[:, j * 4:j * 4 + 2, :], in_=psf[fi][:].rearrange("p (c m) -> p c m", c=2)),
                            reads=[B_psf[fi]], writes=BL_KT)
                        ffree(fi)
                        fi = falloc()
                    for k in range(8):
                        mm(psf[fi][:, half * 256:half * 256 + 256], W[:, k, c * 128:(c + 1) * 128], memT[:, k, :],
                           k == 0, k == 7, BL_memT + [BW], B_psf[fi])
                P.op('dve', lambda e, fi=fi, j=j: e.tensor_copy(
                    out=KT[:, j * 4 + 2:j * 4 + 4, :], in_=psf[fi][:].rearrange("p (c m) -> p c m", c=2)),
                    reads=[B_psf[fi]], writes=BL_KT)
                ffree(fi)
            wdone()

        if stop < 5:
            P.disabled = True
        kvs = []; B_kvs = []; kts = []; B_kts = []
        for _off in (6144, 7168, 1024):
            _v, _b = arH.view(_off, 1024); kvs.append(_v.rearrange("p (b c d) -> p b c d", b=2, c=2)); B_kvs.append(_b)

        kv_next = [0]

        def kv_fill(upto):
            while kv_next[0] < min(upto, 64):
                G = kv_next[0]
                kv_next[0] += 1
                hh, L = G // 16, G % 16
                src_t = cmk if L < 8 else cmv
                b0 = (L % 8) * 2
                src = src_t[b0:b0 + 2, :, hh * 256:(hh + 1) * 256].rearrange("b (c p) d -> p b c d", p=128)
                ri = G % 3
                P.op('pool', lambda e, ri=ri, src=src: e.dma_start(out=kvs[ri], in_=src), writes=B_kvs[ri], dma_sem='kv%d' % ri)

        for i in range(2):
            _v, _b = arH.view(i * 512, 512); kts.append(_v.rearrange("p (c m) -> p c m", c=2)); B_kts.append(_b)
        kv_fill(3)
        for h in range(4):
            W, BW = wget()
            def do_mq(g, W=W, BW=BW):
                n = GROUPS[g][1]
                qi = tha()
                for dch in range(2):
                    fi = falloc()
                    for k in range(8):
                        mm(psf[fi][:, 0:n], W[:, k, dch * 128:(dch + 1) * 128], hT[:, k, sl(g)], k == 0, k == 7, [B_hT, BW], B_psf[fi])
                    P.op('act', lambda e, fi=fi, qi=qi, dch=dch, n=n: e.copy(out=th_[qi][:, dch * 512:dch * 512 + n], in_=psf[fi][:, 0:n]),
                         reads=[B_psf[fi]], writes=[B_th[qi]])
                    ffree(fi)
                return qi

            qnext = None
            for g in range(5):
                P.tag = 'mem' if g < 4 else 'mem_s'
                n = GROUPS[g][1]
                qi = qnext if qnext is not None else do_mq(g)
                qnext = None
                sgi = [tfa(), tfa()]
                for ech in range(2):
                    fi = falloc()
                    for k in range(8):
                        mm(psf[fi][:, 0:n], W[:, k, 256 + ech * 128:256 + (ech + 1) * 128], hT[:, k, sl(g)], k == 0, k == 7,
                           [B_hT, BW], B_psf[fi])
                    ti = sgi[ech]
                    P.op('act', lambda e, fi=fi, ti=ti, n=n: e.activation(out=tf[ti][:, 0:n], in_=psf[fi][:, 0:n], func=AF.Tanh, scale=0.5),
                         reads=[B_psf[fi]], writes=[B_tf[ti]])
                    P.op('dve', lambda e, fi=fi, ti=ti, n=n: e.scalar_tensor_tensor(
                        out=tf[ti][:, 0:n], in0=tf[ti][:, 0:n], scalar=1.0, in1=psf[fi][:, 0:n], op0=ALU.add, op1=ALU.mult),
                        reads=[B_psf[fi], B_tf[ti]], writes=[B_tf[ti]])
                    ffree(fi)
                pi = tha()
                fden = falloc()
                fom = [falloc(), falloc()]
                if g < 4:
                    for mch in range(2):
                        fi = falloc()
                        for dch in range(2):
                            mm(psf[fi][:, 0:n], KT[:, h * 2 + dch, mch * 128:(mch + 1) * 128], th_[qi][:, dch * 512:dch * 512 + n],
                               dch == 0, dch == 1, BL_KT + [B_th[qi]], B_psf[fi])
                        P.op('act', lambda e, fi=fi, pi=pi, mch=mch, n=n: e.activation(
                            out=th_[pi][:, mch * 512:mch * 512 + n], in_=psf[fi][:, 0:n], func=AF.Exp, scale=1.0 / 16),
                            reads=[B_psf[fi]], writes=[B_th[pi]])
                        ffree(fi)
                    P.tag = 'mem' if g + 1 < 4 else 'mem_s'
                    qnext = do_mq(g + 1)
                    P.tag = 'mem'
                    for mch in range(2):
                        mm(psf[fden][:, 0:n], ones[:], th_[pi][:, mch * 512:mch * 512 + n], mch == 0, mch == 1, [B_th[pi]] + RC, B_psf[fden])
                    for ech in range(2):
                        for mch in range(2):
                            mm(psf[fom[ech]][:, 0:n], Vp[:, mch, h * 256 + ech * 128:h * 256 + (ech + 1) * 128],
                               th_[pi][:, mch * 512:mch * 512 + n], mch == 0, mch == 1, BL_Vp + [B_th[pi]], B_psf[fom[ech]])
                else:
                    fsc = falloc()
                    for b in range(NB + 1):
                        if b < NB:
                            if b % 2 == 0:
                                kv_fill(h * 16 + b // 2 + 3)
                            ri = (h * 16 + b // 2) % 3
                            kb = b % 2
                            bi = b % 2
                            hi = halloc()
                            for mch in range(2):
                                for dch in range(2):
                                    tr(psh[hi][:, dch * 256 + mch * 128:dch * 256 + (mch + 1) * 128],
                                       kvs[ri][:, kb, mch, dch * 128:(dch + 1) * 128], idb[:], B_kvs[ri], B_psh[hi])
                            P.op('dve', lambda e, hi=hi, bi=bi: e.tensor_copy(
                                out=kts[bi][:].rearrange("p c m -> p (c m)"), in_=psh[hi][:, 0:512]),
                                reads=[B_psh[hi]], writes=B_kts[bi])
                            hfree(hi)
                        if b >= 1:
                            bb = b - 1
                            bi = bb % 2
                            for mch in range(2):
                                for dch in range(2):
                                    mm(psf[fsc][:, mch * 128 + bb * 8:mch * 128 + bb * 8 + 8], kts[bi][:, dch, mch * 128:(mch + 1) * 128],
                                       th_[qi][:, dch * 512 + bb * 8:dch * 512 + bb * 8 + 8], dch == 0, dch == 1,
                                       B_kts[bi] + [B_th[qi]], B_psf[fsc])
                    for mch in range(2):
                        P.op('act', lambda e, fsc=fsc, pi=pi, mch=mch: e.activation(
                            out=th_[pi][:, mch * 512:mch * 512 + 128], in_=psf[fsc][:, mch * 128:(mch + 1) * 128], func=AF.Exp, scale=1.0 / 16),
                            reads=[B_psf[fsc]], writes=[B_th[pi]])
                    ffree(fsc)
                    for mch in range(2):
                        mm(psf[fden][:, 0:n], ones[:], th_[pi][:, mch * 512:mch * 512 + n], mch == 0, mch == 1, [B_th[pi]] + RC, B_psf[fden])
                    for b in range(NB):
                        if b % 2 == 0:
                            kv_fill(h * 16 + 8 + b // 2 + 3)
                        ri = (h * 16 + 8 + b // 2) % 3
                        kb = b % 2
                        for ech in range(2):
                            for mch in range(2):
                                mm(psf[fom[ech]][:, b * 8:b * 8 + 8], kvs[ri][:, kb, mch, ech * 128:(ech + 1) * 128],
                                   th_[pi][:, mch * 512 + b * 8:mch * 512 + b * 8 + 8], mch == 0, mch == 1,
                                   B_kvs[ri] + [B_th[pi]], B_psf[fom[ech]])
                    kv_fill(h * 16 + 16 + 3)
                    wdone()
                tr_ = tfa()
                P.op('dve', lambda e, fden=fden, tr_=tr_, n=n: e.reciprocal(out=tf[tr_][:, 0:n], in_=psf[fden][:, 0:n]),
                     reads=[B_psf[fden]], writes=[B_tf[tr_]])
                ffree(fden)
                for ech in range(2):
                    ti = sgi[ech]
                    P.op('dve', lambda e, ech=ech, tr_=tr_, ti=ti, n=n, fo=fom[ech]: e.scalar_tensor_tensor(
                        out=tf[ti][:, 0:n], in0=psf[fo][:, 0:n], scalar=0.5, in1=tf[ti][:, 0:n], op0=ALU.mult, op1=ALU.mult),
                        reads=[B_psf[fom[ech]], B_tf[ti]], writes=[B_tf[ti]])
                    ffree(fom[ech])
                    P.op('pool', lambda e, ech=ech, tr_=tr_, ti=ti, n=n, h=h, g=g: e.tensor_tensor(
                        out=ob[:, h * 2 + ech, sl(g)], in0=tf[ti][:, 0:n], in1=tf[tr_][:, 0:n], op=ALU.mult),
                        reads=[B_tf[ti], B_tf[tr_]], writes=[B_ob[h * 2 + ech][g]])
        if stop < 6:
            P.disabled = True
        project_branch(2, False)

        if stop < 7:
            P.disabled = True
        tf_n[0] = NTF + 2
        P.op('sp', lambda e: e.dma_start(out=gaux[:], in_=ret_norm_g.partition_broadcast(128)), writes=[B_gaux], dma_sem='gaux')
        gret = gaux
        _v, BL_S32 = arF.view(0, 512); S32 = _v.rearrange("p (c e) -> p c e", c=2)
        Ss32 = []; B_Ss32 = []; csb = []; B_csb = []
        for i in range(2):
            _v, _b = arF.view(1536 + i * 1024, 1024); csb.append(_v); B_csb.append(_b)
        for _off in (512, 1024, 2560, 3072):
            _v, _b = arF.view(_off, 512); Ss32.append(_v.rearrange("p (c e) -> p c e", c=2)); B_Ss32.append(_b)
        maskSf, BL_maskSf = arF.view(3584, 128)
        rmd, BL_rmd = arF.view(3840, 64)
        kdecX, BL_kdecX = arF.view(3904, 16)
        maskX = maskT[:].rearrange("p h i -> p (h i)")
        qdecX = qdec[:].rearrange("p h i -> p (h i)")
        B_mx = Buf('maskX'); B_qx = Buf('qdecX'); ht_done = set(); cs0_done = set()
        P.op('sp', lambda e: e.dma_start(out=rmd, in_=c_rmd), writes=BL_rmd, dma_sem='hc')
        P.op('sp', lambda e: e.dma_start(out=kdecX, in_=c_kdecX), writes=BL_kdecX, dma_sem='hc')
        gsg1 = xbuf[0][:].rearrange("p (c e) -> p c e", c=4)
        B_gsg1 = [B_x[0]]
        Sbf = []; B_Sbf = []; Ssbf = []; B_Ssbf = []; stmX = []; B_stmX = []
        for i in range(2):
            _v, _b = arH.view(i * 512, 512); Sbf.append(_v.rearrange("p (c e) -> p c e", c=2)); B_Sbf.append(_b)
            _v, _b = arH.view(1024 + i * 512, 512); Ssbf.append(_v.rearrange("p (c e) -> p c e", c=2)); B_Ssbf.append(_b)
            _v, _b = arH.view(2048 + i * 1536, 1280); stmX.append(_v); B_stmX.append(_b)
        _v, BL_kdm = arH.view(2048, 4096); kdm = _v.rearrange("p (b d) -> p b d", b=NB)
        stmS, BL_stmS = arH.view(6144, 128)
        vtmS, BL_vtmS = arH.view(6272, 256)
        _v, _b = arH.view(6656, 512); Ssbf.append(_v.rearrange("p (c e) -> p c e", c=2)); B_Ssbf.append(_b)
        vtm1 = hb[1][:].rearrange("p (c e) -> p c e", c=4); B_vtm1 = [B_hb[1]]
        _v, BL_kdt = arH.view(7168, 1024); kdt1 = _v.rearrange("p (c e) -> p c e", c=4)
        onb1 = hb[0][:].rearrange("p (c e) -> p c e", c=4); B_onb1 = [B_hb[0]]
        SOFF = [0, 512, 896, 1152]
        stgS = [(stg[0][:, 0:512], [B_stg[0]]), (stg[1][:, 0:512], [B_stg[1]])]
        for _off in (0, 7168):
            _v, _b = arH.view(_off, 1024); stgS.append((_v.bitcast(F32), _b))

        def rotary(fps, g, n, dst_i, Bdst):
            cs = csb[g % 2][:, 0:n]; sn = csb[g % 2][:, 512:512 + n]
            p1 = psf[fps[0]][:, 0:n]; p2 = psf[fps[1]][:, 0:n]
            b1 = tfa(); b2 = tfa(); a1 = tfa(); a2 = tfa()
            P.op('dve', lambda e: e.tensor_tensor(out=tf[b1][:, 0:n], in0=p1, in1=cs, op=ALU.mult),
                 reads=[B_psf[fps[0]]] + B_csb[g % 2], writes=[B_tf[b1]])
            P.op('dve', lambda e: e.tensor_tensor(out=tf[b2][:, 0:n], in0=p2, in1=sn, op=ALU.mult),
                 reads=[B_psf[fps[1]]] + B_csb[g % 2], writes=[B_tf[b2]])
            P.op('pool', lambda e: e.tensor_tensor(out=th_[dst_i][:, 0:n], in0=tf[b1][:, 0:n], in1=tf[b2][:, 0:n], op=ALU.subtract),
                 reads=[B_tf[b1], B_tf[b2]], writes=[Bdst])
            P.op('dve', lambda e: e.tensor_tensor(out=tf[a1][:, 0:n], in0=p1, in1=sn, op=ALU.mult),
                 reads=[B_psf[fps[0]]] + B_csb[g % 2], writes=[B_tf[a1]])
            P.op('dve', lambda e: e.tensor_tensor(out=tf[a2][:, 0:n], in0=p2, in1=cs, op=ALU.mult),
                 reads=[B_psf[fps[1]]] + B_csb[g % 2], writes=[B_tf[a2]])
            ffree(fps[0]); ffree(fps[1])
            P.op('pool', lambda e: e.tensor_tensor(out=th_[dst_i][:, 512:512 + n], in0=tf[a1][:, 0:n], in1=tf[a2][:, 0:n], op=ALU.add),
                 reads=[B_tf[a1], B_tf[a2]], writes=[Bdst])

        pending = []
        for h in range(4):
            WA, BWA = wget()
            WB, BWB = wget()
            def head_tables(hh):
                if hh < 4 and hh not in ht_done:
                    ht_done.add(hh)
                    P.op('sp', lambda e, hh=hh: e.dma_start(out=maskX, in_=c_maskX[hh]), writes=[B_mx], dma_sem='hc')
                    P.op('sp', lambda e, hh=hh: e.dma_start(out=qdecX, in_=c_qdecX[hh].partition_broadcast(128)), writes=[B_qx], dma_sem='hc2')

            head_tables(h)
            P.op('sp', lambda e, h=h: e.dma_start(out=maskSf, in_=c_maskSf[h]), writes=BL_maskSf, dma_sem='hc3')

            def s_issue(b, h=h):
                if b < NB:
                    P.op('pool', lambda e, b=b, h=h: e.dma_start(
                        out=Ss32[b % 4], in_=sret[b, h].rearrange("(c p) e -> p c e", p=128)),
                        writes=B_Ss32[b % 4], dma_sem='ss%d' % (b % 4))

            def load_cs(g, h=h):
                if not (g == 0 and h in cs0_done):
                    n = GROUPS[g][1]
                    P.op('sp', lambda e, g=g, n=n: e.dma_start(out=csb[g % 2][:, 0:n], in_=c_cos[:, sl(g)]), writes=B_csb[g % 2], dma_sem='cs%d' % (g % 2))
                    P.op('sp', lambda e, g=g, n=n: e.dma_start(out=csb[g % 2][:, 512:512 + n], in_=c_sin[:, sl(g)]), writes=B_csb[g % 2], dma_sem='cs%d' % (g % 2))

            def proj_rot(g, off, dst_i, WA=WA, BWA=BWA):
                n = GROUPS[g][1]
                fps = [falloc(), falloc()]
                for dch in range(2):
                    for k in range(8):
                        mm(psf[fps[dch]][:, 0:n], WA[:, k, off + dch * 128:off + (dch + 1) * 128], hT[:, k, sl(g)],
                           k == 0, k == 7, [B_hT, BWA], B_psf[fps[dch]])
                rotary(fps, g, n, dst_i, B_th[dst_i])

            qnext = None
            for g in range(5):
                P.tag = 'ret' if g < 4 else 'ret_s'
                n = GROUPS[g][1]
                if g == 3:
                    s_issue(0); s_issue(1)
                if g == 4:
                    s_issue(2); s_issue(3)
                if qnext is None:
                    load_cs(g)
                    qi = tha()
                    proj_rot(g, 0, qi)
                else:
                    qi = qnext
                qnext = None
                ki = tha(); qdi = tha()
                proj_rot(g, 256, ki)
                if g == 4:
                    while pending:
                        pending.pop(0)()
                    head_tables(h + 1)
                    if h < 3:
                        cs0_done.add(h + 1)
                        P.op('sp', lambda e: e.dma_start(out=csb[0][:, 0:512], in_=c_cos[:, 0:512]), writes=B_csb[0], dma_sem='cs0')
                        P.op('sp', lambda e: e.dma_start(out=csb[0][:, 512:1024], in_=c_sin[:, 0:512]), writes=B_csb[0], dma_sem='cs0')
                for dch in range(2):
                    if g < 4:
                        o3 = th_[qdi][:, dch * 512:dch * 512 + 512]
                        i3 = th_[qi][:, dch * 512:dch * 512 + 512]
                        d3 = qdecX
                        rd = [B_th[qi], B_qx]
                    else:
                        o3 = th_[qdi][:, dch * 512:dch * 512 + 128].rearrange("p (b t) -> p b t", t=8)
                        i3 = th_[qi][:, dch * 512:dch * 512 + 128].rearrange("p (b t) -> p b t", t=8)
                        d3 = qdecS[:, h, :].unsqueeze(1).to_broadcast([128, NB, 8])
                        rd = [B_th[qi]] + RC
                    P.op('pool' if g < 4 else 'dve', lambda e, o3=o3, i3=i3, d3=d3: e.tensor_tensor(out=o3, in0=i3, in1=d3, op=ALU.mult),
                         reads=rd, writes=[B_th[qdi]])
                if g < 4:
                    fv = [falloc(), falloc()]
                    for c in range(4):
                        for k in range(8):
                            mm(psf[fv[c // 2]][:, (c % 2) * 256:(c % 2) * 256 + 256], hT[:, k, g * 512 + c * 128:g * 512 + (c + 1) * 128],
                               WB[:, k, 0:256], k == 0, k == 7, [B_hT, BWB], B_psf[fv[c // 2]])
                    for hf in range(2):
                        P.op('act', lambda e, hf=hf, f=fv[hf]: e.copy(
                            out=vtm1[:, hf * 2:hf * 2 + 2, :].rearrange("p c e -> p (c e)"), in_=psf[f][:]),
                            reads=[B_psf[fv[hf]]], writes=B_vtm1)
                        ffree(fv[hf])
                    fg_ = [falloc(), falloc()]
                    for c in range(4):
                        for k in range(8):
                            mm(psf[fg_[c // 2]][:, (c % 2) * 256:(c % 2) * 256 + 256], hT[:, k, g * 512 + c * 128:g * 512 + (c + 1) * 128],
                               WB[:, k, 256:512], k == 0, k == 7, [B_hT, BWB], B_psf[fg_[c // 2]])
                    for hf in range(2):
                        ti = tfa()
                        f = fg_[hf]
                        gv = gsg1[:, hf * 2:hf * 2 + 2, :]
                        P.op('act', lambda e, f=f, ti=ti: e.activation(out=tf[ti][:], in_=psf[f][:], func=AF.Silu),
                             reads=[B_psf[f]], writes=[B_tf[ti]])
                        ffree(f)
                        P.op('pool', lambda e, ti=ti, gv=gv, h=h: e.tensor_tensor(
                            out=gv, in0=tf[ti][:].rearrange("p (c e) -> p c e", c=2),
                            in1=gret[:, h * 256:(h + 1) * 256].unsqueeze(1).to_broadcast([128, 2, 256]), op=ALU.mult),
                            reads=[B_tf[ti], B_gaux], writes=B_gsg1)
                    hi = halloc()
                    for c in range(4):
                        for dch in range(2):
                            tr(psh[hi][:, c * 256 + dch * 128:c * 256 + (dch + 1) * 128],
                               th_[ki][:, dch * 512 + c * 128:dch * 512 + (c + 1) * 128], idb[:], [B_th[ki]], B_psh[hi])
                    for c in range(4):
                        P.op('act', lambda e, hi=hi, h=h, c=c: e.activation(
                            out=kdt1[:, c, :], in_=psh[hi][:, c * 256:(c + 1) * 256], func=AF.Copy, scale=kdecX[:, h * 4 + c:h * 4 + c + 1]),
                            reads=[B_psh[hi]] + BL_kdecX, writes=BL_kdt)
                    hfree(hi)
                    sx = g % 2
                    for cp in range(4):
                        nn = 512 - 128 * cp
                        fs = falloc()
                        for dch in range(2):
                            mm(psf[fs][:, 0:nn], th_[ki][:, dch * 512 + cp * 128:dch * 512 + cp * 128 + 128],
                               th_[qi][:, dch * 512 + cp * 128:dch * 512 + 512], dch == 0, dch == 1, [B_th[ki], B_th[qi]], B_psf[fs])
                        P.op('dve', lambda e, fs=fs, sx=sx, cp=cp, nn=nn: e.tensor_tensor(
                            out=stmX[sx][:, SOFF[cp]:SOFF[cp] + nn], in0=psf[fs][:, 0:nn], in1=maskX[:, 0:nn], op=ALU.mult),
                            reads=[B_psf[fs], B_mx], writes=B_stmX[sx])
                        ffree(fs)
                    P.tag = 'ret' if g + 1 < 4 else 'ret_s'
                    load_cs(g + 1)
                    qnext = tha()
                    proj_rot(g + 1, 0, qnext)
                    P.tag = 'ret'
                    while pending:
                        pending.pop(0)()
                    fo = [falloc(), falloc()]
                    scur = g % 2
                    for c in range(4):
                        oview = psf[fo[c // 2]][:, (c % 2) * 256:(c % 2) * 256 + 256]
                        nmm = (c + 1) + (2 if g > 0 else 0)
                        im = 0
                        for cp in range(c + 1):
                            mm(oview, stmX[sx][:, SOFF[cp] + (c - cp) * 128:SOFF[cp] + (c - cp) * 128 + 128], vtm1[:, cp, :],
                               im == 0, im == nmm - 1, [B_stmX[sx], B_vtm1], B_psf[fo[c // 2]])
                            im += 1
                        if g > 0:
                            for dch in range(2):
                                mm(oview, th_[qdi][:, dch * 512 + c * 128:dch * 512 + c * 128 + 128], Sbf[scur][:, dch, :], False, im == nmm - 1,
                                   [B_th[qdi], B_Sbf[scur]], B_psf[fo[c // 2]])
                                im += 1
                    fkv = falloc()
                    for dch in range(2):
                        for c in range(4):
                            mm(psf[fkv][:, dch * 256:dch * 256 + 256], kdt1[:, c, dch * 128:(dch + 1) * 128], vtm1[:, c, :],
                               c == 0, c == 3, [BL_kdt, B_vtm1], B_psf[fkv])
                    if g > 0:
                        P.op('dve', lambda e, fkv=fkv, h=h: e.scalar_tensor_tensor(
                            out=S32.rearrange("p c e -> p (c e)"), in0=S32.rearrange("p c e -> p (c e)"), scalar=CDX[h],
                            in1=psf[fkv][:], op0=ALU.mult, op1=ALU.add), reads=[B_psf[fkv], BL_S32], writes=[BL_S32])
                    else:
                        P.op('dve', lambda e, fkv=fkv: e.tensor_copy(out=S32.rearrange("p c e -> p (c e)"), in_=psf[fkv][:]),
                             reads=[B_psf[fkv]], writes=[BL_S32])
                    ffree(fkv)
                    if g < 3:
                        snx = (g + 1) % 2
                        P.op('act', lambda e, snx=snx: e.copy(out=Sbf[snx], in_=S32), reads=[BL_S32], writes=[B_Sbf[snx]])
                    else:
                        P.op('sp', lambda e, h=h: e.dma_start(out=nrp[h].rearrange("(c p) e -> p c e", p=128), in_=S32),
                             reads=[BL_S32], dma_sem='o_nrp', is_out=True)
                    st = stat()
                    for c in range(4):
                        oview = psf[fo[c // 2]][:, (c % 2) * 256:(c % 2) * 256 + 256]
                        P.op('act', lambda e, oview=oview, st=st, c=c: e.activation(
                            out=scr[:, 0:256], in_=oview, func=AF.Square, accum_out=ssb[:, st * 4 + c:st * 4 + c + 1]),
                            reads=[B_psf[fo[c // 2]]], writes=[B_scr, B_st[st]])
                    st2 = stat()
                    P.op('dve', lambda e, st=st, st2=st2: e.tensor_scalar(
                        out=ssb[:, st2 * 4:st2 * 4 + 4], in0=ssb[:, st * 4:st * 4 + 4], scalar1=1.0 / 256, scalar2=EPS,
                        op0=ALU.mult, op1=ALU.add), reads=[B_st[st]], writes=[B_st[st2]])
                    P.op('pool', lambda e, st2=st2: e.tensor_tensor(
                        out=ssb[:, st2 * 4:st2 * 4 + 4], in0=ssb[:, st2 * 4:st2 * 4 + 4], in1=mhalf[:, 0:4], op=ALU.pow),
                        reads=[B_st[st2]] + RC, writes=[B_st[st2]])
                    for c in range(4):
                        oview = psf[fo[c // 2]][:, (c % 2) * 256:(c % 2) * 256 + 256]
                        P.op('dve', lambda e, oview=oview, st2=st2, c=c: e.scalar_tensor_tensor(
                            out=onb1[:, c, :], in0=oview, scalar=ssb[:, st2 * 4 + c:st2 * 4 + c + 1], in1=gsg1[:, c, :],
                            op0=ALU.mult, op1=ALU.mult), reads=[B_psf[fo[c // 2]], B_st[st2], B_gsg1], writes=B_onb1)
                    ffree(fo[0]); ffree(fo[1])

                    def tail(h=h, g=g):
                        hi = halloc()
                        for ech in range(2):
                            for c in range(4):
                                tr(psh[hi][:, ech * 512 + c * 128:ech * 512 + (c + 1) * 128], onb1[:, c, ech * 128:(ech + 1) * 128],
                                   idb[:], B_onb1, B_psh[hi])
                        P.op('act', lambda e, hi=hi, h=h, g=g: e.copy(
                            out=ob[:, h * 2:h * 2 + 2, sl(g)], in_=psh[hi][:].rearrange("p (c t) -> p c t", c=2)),
                            reads=[B_psh[hi]], writes=[B_ob[h * 2][g], B_ob[h * 2 + 1][g]])
                        hfree(hi)
                    pending.append(tail)
                else:
                    fv = falloc()
                    for k in range(8):
                        mm(psf[fv][:, 0:256], hT[:, k, LP:NT], WB[:, k, 0:256], k == 0, k == 7, [B_hT, BWB], B_psf[fv])
                    P.op('act', lambda e, fv=fv: e.copy(out=vtmS, in_=psf[fv][:, 0:256]), reads=[B_psf[fv]], writes=BL_vtmS)
                    ffree(fv)
                    fs = falloc()
                    for dch in range(2):
                        mm(psf[fs][:, 0:128], th_[ki][:, dch * 512:dch * 512 + 128], th_[qi][:, dch * 512:dch * 512 + 128],
                           dch == 0, dch == 1, [B_th[ki], B_th[qi]], B_psf[fs])
                    P.op('dve', lambda e, fs=fs: e.tensor_tensor(out=stmS, in0=psf[fs][:, 0:128], in1=maskSf, op=ALU.mult),
                         reads=[B_psf[fs]] + BL_maskSf, writes=BL_stmS)
                    ffree(fs)
                    hi = halloc()
                    for dch in range(2):
                        tr(psh[hi][:, dch * 128:(dch + 1) * 128], th_[ki][:, dch * 512:dch * 512 + 128], idb[:], [B_th[ki]], B_psh[hi])
                    P.op('dve', lambda e, hi=hi, h=h: e.tensor_tensor(
                        out=kdm, in0=psh[hi][:, 0:256].unsqueeze(1).to_broadcast([128, NB, 256]),
                        in1=rmd[:, h * NB:(h + 1) * NB].unsqueeze(2).to_broadcast([128, NB, 256]), op=ALU.mult),
                        reads=[B_psh[hi]] + BL_rmd, writes=BL_kdm)
                    hfree(hi)
                    sgi = [tfa(), tfa()]
                    for ech in range(2):
                        fi = falloc()
                        for k in range(8):
                            mm(psf[fi][:, 0:128], WB[:, k, 256 + ech * 128:256 + (ech + 1) * 128], hT[:, k, LP:NT], k == 0, k == 7,
                               [B_hT, BWB], B_psf[fi])
                        ti = sgi[ech]
                        P.op('act', lambda e, fi=fi, ti=ti: e.activation(out=tf[ti][:, 0:128], in_=psf[fi][:, 0:128], func=AF.Tanh, scale=0.5),
                             reads=[B_psf[fi]], writes=[B_tf[ti]])
                        P.op('dve', lambda e, fi=fi, ti=ti: e.scalar_tensor_tensor(
                            out=tf[ti][:, 0:128], in0=tf[ti][:, 0:128], scalar=1.0, in1=psf[fi][:, 0:128], op0=ALU.add, op1=ALU.mult),
                            reads=[B_psf[fi], B_tf[ti]], writes=[B_tf[ti]])
                        ffree(fi)
                    foTs = [falloc(), falloc()]
                    for ech in range(2):
                        mm(psf[foTs[ech]][:, 0:128], vtmS[:, ech * 128:(ech + 1) * 128], stmS, True, False,
                           [BL_vtmS, BL_stmS], B_psf[foTs[ech]])
                    def s_cast(b):
                        if b < NB:
                            P.op('act', lambda e, b=b: e.copy(out=Ssbf[b % 3], in_=Ss32[b % 4]), reads=B_Ss32[b % 4], writes=B_Ssbf[b % 3])

                    s_cast(0)
                    s_cast(1)
                    s_cast(2)
                    for b in range(NB):
                        bi = b % 3
                        s4 = b % 4
                        sg_ap, sg_b = stgS[b % 4]
                        for ech in range(2):
                            ov = psf[foTs[ech]][:, b * 8:b * 8 + 8]
                            for dch in range(2):
                                mm(ov, Ssbf[bi][:, dch, ech * 128:(ech + 1) * 128], th_[qdi][:, dch * 512 + b * 8:dch * 512 + b * 8 + 8],
                                   False, (dch == 1 and b == NB - 1), [B_Ssbf[bi], B_th[qdi]], B_psf[foTs[ech]])
                        fkv = falloc()
                        for dch in range(2):
                            mm(psf[fkv][:, dch * 256:dch * 256 + 256], kdm[:, b, dch * 128:(dch + 1) * 128], vtmS, True, True,
                               [BL_kdm, BL_vtmS], B_psf[fkv])
                        s_cast(b + 3)
                        P.op('dve', lambda e, fkv=fkv, sg_ap=sg_ap, s4=s4, h=h: e.scalar_tensor_tensor(
                            out=sg_ap, in0=Ss32[s4].rearrange("p c e -> p (c e)"), scalar=CDS[h], in1=psf[fkv][:],
                            op0=ALU.mult, op1=ALU.add), reads=[B_psf[fkv]] + B_Ss32[s4], writes=sg_b)
                        ffree(fkv)
                        s_issue(b + 4)
                        P.op('sp', lambda e, b=b, sg_ap=sg_ap, h=h: e.dma_start(
                            out=nrs[b, h].rearrange("(c p) e -> p c e", p=128), in_=sg_ap.rearrange("p (c e) -> p c e", c=2)),
                            reads=sg_b, dma_sem='o_nrs%d' % (b % 4), is_out=True)
                    wdone()
                    wdone()
                    sq = tha()
                    for ech in range(2):
                        P.op('act', lambda e, f=foTs[ech], sq=sq, ech=ech: e.activation(
                            out=th_[sq][:, ech * 128:(ech + 1) * 128], in_=psf[f][:, 0:128], func=AF.Square),
                            reads=[B_psf[foTs[ech]]], writes=[B_th[sq]])
                    fss = falloc()
                    for ech in range(2):
                        mm(psf[fss][:, 0:128], ones[:], th_[sq][:, ech * 128:(ech + 1) * 128], ech == 0, ech == 1, [B_th[sq]] + RC, B_psf[fss])
                    trs = tfa()
                    P.op('act', lambda e, fss=fss, trs=trs: e.activation(
                        out=tf[trs][:, 0:128], in_=psf[fss][:, 0:128], func=AF.Sqrt, bias=epsb4[:, 0:1], scale=4.0 / 256),
                        reads=[B_psf[fss]] + RC, writes=[B_tf[trs]])
                    ffree(fss)
                    P.op('dve', lambda e, trs=trs: e.reciprocal(out=tf[trs][:, 0:128], in_=tf[trs][:, 0:128]),
                         reads=[B_tf[trs]], writes=[B_tf[trs]])
                    for ech in range(2):
                        ti = sgi[ech]
                        P.op('dve', lambda e, ech=ech, ti=ti, trs=trs, f=foTs[ech], h=h: e.scalar_tensor_tensor(
                            out=tf[ti][:, 128:256], in0=psf[f][:, 0:128], scalar=gretT[:, h * 2 + ech:h * 2 + ech + 1],
                            in1=tf[trs][:, 0:128], op0=ALU.mult, op1=ALU.mult), reads=[B_psf[foTs[ech]], B_tf[trs], B_tf[ti]] + RC, writes=[B_tf[ti]])
                        P.op('pool', lambda e, ech=ech, ti=ti, h=h: e.tensor_tensor(
                            out=ob[:, h * 2 + ech, LP:NT], in0=tf[ti][:, 128:256], in1=tf[ti][:, 0:128], op=ALU.mult),
                            reads=[B_tf[ti]], writes=[B_ob[h * 2 + ech][4]])
                    ffree(foTs[0]); ffree(foTs[1])
        if stop < 8:
            P.disabled = True
        project_branch(0, False)

        if stop < 9:
            P.disabled = True
        tf_n[0] = NTF
        P.op('sp', lambda e: e.dma_start(out=gaux[:], in_=final_norm_g.partition_broadcast(128)), writes=[B_gaux], dma_sem='gaux')
        W0, BW0 = wget()
        W1, BW1 = wget()
        allmg = [B_mg[fc][g] for fc in range(8) for g in range(5)]
        P.tag = 'final'
        def xload(t):
            if t < NTILE:
                P.op('sp', lambda e, t=t: e.dma_start(out=xbuf[t % NXR][:], in_=xtile_src(t)), writes=[B_x[t % NXR]], dma_sem='x%d' % (t % NXR))

        for t in range(4):
            xload(t)
        for t in range(NTILE):
            xi = t % NXR
            xload(t + 4)
            g = min(t // 4, 4)
            fr = [falloc(), falloc()]
            for j, (W, BW) in enumerate(((W0, BW0), (W1, BW1))):
                for k in range(8):
                    mm(psf[fr[j]][:], mgd[:, k, t * 128:(t + 1) * 128], W[:, k, :], k == 0, k == 7, [B_mg[k][g], BW], B_psf[fr[j]])
            for j in range(2):
                P.op('dve', lambda e, j=j, xi=xi, f=fr[j]: e.scalar_tensor_tensor(
                    out=xbuf[xi][:, j * 512:(j + 1) * 512], in0=psf[f][:], scalar=0.5, in1=xbuf[xi][:, j * 512:(j + 1) * 512],
                    op0=ALU.mult, op1=ALU.add), reads=[B_psf[fr[j]], B_x[xi]], writes=[B_x[xi]])
                ffree(fr[j])
            st = stat()
            sa = ssb[:, st * 4:st * 4 + 1]; sb_ = ssb[:, st * 4 + 1:st * 4 + 2]
            P.op('act', lambda e, xi=xi, sa=sa: e.activation(out=hb[xi][:], in_=xbuf[xi][:], func=AF.Square, accum_out=sa),
                 reads=[B_x[xi]], writes=[B_hb[xi], B_st[st]])
            rstd_of(sa, sb_, D, [B_st[st]])
            P.op('dve', lambda e, xi=xi, sb_=sb_: e.scalar_tensor_tensor(
                out=xbuf[xi][:], in0=xbuf[xi][:], scalar=sb_, in1=gaux[:], op0=ALU.mult, op1=ALU.mult),
                reads=[B_x[xi], B_st[st], B_gaux], writes=[B_x[xi]])
            dst = yp[t * 128:(t + 1) * 128, :] if t < 16 else ys
            P.op('sp', lambda e, dst=dst, xi=xi: e.dma_start(out=dst, in_=xbuf[xi][:]), reads=[B_x[xi]],
                 dma_sem='o_x%d' % xi, is_out=True)

        P.emit(nc)
    return nc, consts


_CACHE = {}


def kernel(x_prompt, x_sample, state_ret, state_conv, cache_mem_k, cache_mem_v, mem_prompt,
           norm_g, w_in, b_gate, ret_norm_g, conv_w, conv_b, w_ret_o, w_conv_o, w_mem_o,
           w_out, mem_norm_g, w_mem_kv, final_norm_g):
    f = lambda a: np.ascontiguousarray(np.asarray(a, dtype=np.float32))
    if 'nc' not in _CACHE:
        _CACHE['nc'] = build_nc()
    nc, consts = _CACHE['nc']
    shared = {
        "norm_g": f(norm_g).reshape(D), "b_gate": f(b_gate).reshape(3072),
        "ret_norm_g": f(ret_norm_g).reshape(1024), "conv_w": f(conv_w).reshape(3, D), "conv_b": f(conv_b).reshape(D),
        "mem_norm_g": f(mem_norm_g).reshape(D), "final_norm_g": f(final_norm_g).reshape(D),
        "wb": _pack_weights({"w_in": f(w_in).reshape(D, 13312), "w_ret_o": f(w_ret_o).reshape(D, D),
                             "w_conv_o": f(w_conv_o).reshape(D, D), "w_mem_o": f(w_mem_o).reshape(D, D),
                             "w_out": f(w_out).reshape(D, D), "w_mem_kv": f(w_mem_kv).reshape(D, 2048)}),
        "c_cos": consts['cos'], "c_sin": consts['sin'], "c_maskX": consts['maskX'], "c_qdecX": consts['qdecX'],
        "c_kdecX": consts['kdecX'], "c_maskSf": consts['maskSf'], "c_rmd": consts['rmd'], "c_qdecS": consts['qdecS'],
        "c_ident": consts['ident'],
    }
    xpr = f(x_prompt); xsa = f(x_sample); sr = f(state_ret); sc = f(state_conv)
    ck = f(cache_mem_k); cv = f(cache_mem_v); mp = f(mem_prompt)
    in_maps = []
    for c in range(NCORES):
        bs = slice(c * NB, (c + 1) * NB)
        m = dict(shared)
        m["xp"] = xpr[c]
        m["xs"] = xsa[bs].reshape(NS, D)
        m["sret"] = sr[0, bs]
        m["sconv"] = sc[0, bs].reshape(NB * 2, D)
        m["cmk"] = ck[0, bs].reshape(NB, 256, D)
        m["cmv"] = cv[0, bs].reshape(NB, 256, D)
        m["memp"] = mp[c]
        in_maps.append(m)
    res = run_bass_kernel_spmd(nc, in_maps, core_ids=list(range(NCORES)))
    R = res.results
    y_prompt = np.stack([R[c]["yp"] for c in range(NCORES)], 0).reshape(8, LP, D)
    y_sample = np.concatenate([R[c]["ys"].reshape(NB, LS, D) for c in range(NCORES)], 0)
    nrp_ = np.stack([R[c]["nrp"] for c in range(NCORES)], 0)[None]
    nrs_ = np.concatenate([R[c]["nrs"] for c in range(NCORES)], 0)[None]
    ncp_ = np.stack([R[c]["ncp"] for c in range(NCORES)], 0)[None]
    ncs_ = np.concatenate([R[c]["ncs"].reshape(NB, 2, D) for c in range(NCORES)], 0)[None]
    nmk_ = np.stack([R[c]["nmk"].reshape(256, 4, 256) for c in range(NCORES)], 0)[None]
    nmv_ = np.stack([R[c]["nmv"].reshape(256, 4, 256) for c in range(NCORES)], 0)[None]
    return (y_prompt.astype(np.float32), y_sample.astype(np.float32), nrp_.astype(np.float32), nrs_.astype(np.float32),
            ncp_.astype(np.float32), ncs_.astype(np.float32), nmk_.astype(np.float32), nmv_.astype(np.float32))
```
